# Optimizing a Trainium2 kernel written in Bass

```python
import math
import jax
import jax.numpy as jnp
from jax import lax
import numpy as np

D_MODEL = 1024
BATCH = 16
SEQ = 256
DEPTH = 4
DEC_BATCH = 2
DEC_SEQ = 1024
PAST_LEN = 512

GRID_W = 64
MIX_W = D_MODEL
HEAD_DIM = 64
A_WIDTH = D_MODEL // 4
A_HEADS = A_WIDTH // HEAD_DIM
B_WIDTH = D_MODEL // 2
C_WIDTH = MIX_W - A_WIDTH - B_WIDTH
C_HEADS = C_WIDTH // HEAD_DIM
IN_COLS = 5 * A_WIDTH + 3 * B_WIDTH + 4 * C_WIDTH
D_FF = ((8 * D_MODEL // 3 + 127) // 128) * 128
N_MOD = 9
SHORT_CONV = 3
N_BANDS = 16
POS_EMB = 1 + 2 * N_BANDS
FILTER_HID = 64
DECAY_TARGET = 1e-2
FAST_DECAY_PCT = 0.3
SLOW_DECAY_PCT = 1.5
MIN_DECAY = math.log(DECAY_TARGET) / SLOW_DECAY_PCT
MAX_DECAY = math.log(DECAY_TARGET) / FAST_DECAY_PCT
HGRN_CHUNK = 32
RET_CHUNK = 64
ROPE_BASE = 10000.0
EPS = 1e-6

kernel_name = 'hybrid_hgrn2_hyena_retention_diffusion_step'

F32 = jnp.float32


def rmsnorm(x, w):
    xf = x.astype(F32)
    y = xf * lax.rsqrt(jnp.mean(xf * xf, axis=-1, keepdims=True) + EPS) * w.astype(F32)
    return y.astype(x.dtype)


def head_rmsnorm(o):
    return o * lax.rsqrt(jnp.mean(o * o, axis=-1, keepdims=True) + EPS)


def to_heads(a, n_heads):
    b, t, _ = a.shape
    return a.reshape(b, t, n_heads, -1).transpose(0, 2, 1, 3)


def from_heads(a):
    b, h, t, d = a.shape
    return a.transpose(0, 2, 1, 3).reshape(b, t, h * d)


def flip_t(a):
    return a[:, :, ::-1]


def swiglu(h, w_in, w_out):
    g, u = jnp.split(h @ w_in, 2, axis=-1)
    return (jax.nn.silu(g) * u) @ w_out


def ada(cvec, w, b):
    return (jax.nn.silu(cvec) @ w + b).reshape(cvec.shape[0], N_MOD, D_MODEL)


def chunk_gla(q, k, v, logf, s0):
    bn, h, t, dk = q.shape
    dv = v.shape[-1]
    c = HGRN_CHUNK
    n = t // c
    q = q.reshape(bn, h, n, c, dk)
    k = k.reshape(bn, h, n, c, dk)
    logf = logf.reshape(bn, h, n, c, dk)
    v = v.reshape(bn, h, n, c, dv)
    b = jnp.cumsum(logf, axis=3)
    causal = jnp.tril(jnp.ones((c, c), bool))[:, :, None]
    diff = b[:, :, :, :, None, :] - b[:, :, :, None, :, :]
    decay = jnp.exp(jnp.where(causal, diff, -jnp.inf))
    attn = jnp.einsum('bhntd,bhnsd,bhntsd->bhnts', q, k, decay)
    o = jnp.einsum('bhnts,bhnsv->bhntv', attn, v)
    b_last = b[:, :, :, -1:, :]
    kv = jnp.einsum('bhnsd,bhnsv->nbhdv', k * jnp.exp(b_last - b), v)
    f_chunk = jnp.moveaxis(jnp.exp(b_last[:, :, :, 0]), 2, 0)

    def step(s, inp):
        fc, kvc = inp
        return fc[..., None] * s + kvc, s

    s_fin, s_prev = lax.scan(step, s0, (f_chunk, kv))
    o = o + jnp.einsum('bhntd,nbhdv->bhntv', q * jnp.exp(b), s_prev)
    return o.reshape(bn, h, t, dv), s_fin


def chunk_retention(q, k, v, log_gamma, s0):
    bn, h, t, dk = q.shape
    dv = v.shape[-1]
    c = RET_CHUNK
    n = t // c
    q = q.reshape(bn, h, n, c, dk)
    k = k.reshape(bn, h, n, c, dk)
    v = v.reshape(bn, h, n, c, dv)
    lg = log_gamma[:, None, None]
    j = jnp.arange(c, dtype=F32)
    diff = j[:, None] - j[None, :]
    dmat = jnp.where(diff >= 0, jnp.exp(jnp.where(diff >= 0, diff, 0.0) * lg), 0.0)
    attn = jnp.einsum('bhntd,bhnsd->bhnts', q, k) * dmat[None, :, None]
    o = jnp.einsum('bhnts,bhnsv->bhntv', attn, v)
    k_dec = k * jnp.exp((c - 1 - j)[:, None] * lg)[None, :, None]
    q_dec = q * jnp.exp((j + 1)[:, None] * lg)[None, :, None]
    kv = jnp.einsum('bhnsd,bhnsv->nbhdv', k_dec, v)
    g_c = jnp.exp(c * log_gamma)[None, :, None, None]

    def step(s, kvc):
        return g_c * s + kvc, s

    s_fin, s_prev = lax.scan(step, s0, kv)
    o = o + jnp.einsum('bhntd,nbhdv->bhntv', q_dec, s_prev)
    return o.reshape(bn, h, t, dv), s_fin


def rope_2d(x, row, col):
    half = x.shape[-1] // 2
    freqs = 1.0 / (ROPE_BASE ** (jnp.arange(0, half, 2, dtype=F32) / half))

    def rot(xp, pos):
        ang = pos[:, None] * freqs[None]
        cos, sin = jnp.cos(ang), jnp.sin(ang)
        x1, x2 = jnp.split(xp, 2, axis=-1)
        return jnp.concatenate([x1 * cos - x2 * sin, x1 * sin + x2 * cos], axis=-1)

    return jnp.concatenate([rot(x[..., :half], row), rot(x[..., half:], col)], axis=-1)


def hgrn_gates(z, lbd):
    lbd = lbd[None, :, None, :]
    logf = jnp.logaddexp(jnp.log(lbd), jnp.log1p(-lbd) + jax.nn.log_sigmoid(z))
    k = (1.0 - lbd) * jax.nn.sigmoid(-z)
    return k, logf


def hyena_filter(length, w1, b1, w2, b2, w3, freq):
    t = jnp.arange(length, dtype=F32) / length
    bands = jnp.arange(1, N_BANDS + 1, dtype=F32)
    ang = 2.0 * math.pi * t[:, None] * bands[None]
    z = jnp.concatenate([t[:, None], jnp.cos(ang), jnp.sin(ang)], axis=-1)
    hid = jnp.sin(freq * (z @ w1 + b1))
    hid = jnp.sin(freq * (hid @ w2 + b2))
    filt = (hid @ w3).reshape(length, 2, B_WIDTH)
    deltas = jnp.abs(jnp.linspace(MIN_DECAY, MAX_DECAY, B_WIDTH, dtype=F32))
    filt = filt * jnp.exp(-t[:, None, None] * deltas)
    fwd, bwd = filt[:, 0], filt[:, 1]
    return jnp.concatenate([fwd, jnp.zeros((1, B_WIDTH), F32), bwd[:0:-1]], axis=0)


def hyena(hy, p):
    t = hy.shape[1]
    w = p['hyena_conv'].astype(F32)
    pad = jnp.pad(hy, ((0, 0), (1, 1), (0, 0)))
    hc = pad[:, :-2] * w[0] + pad[:, 1:-1] * w[1] + pad[:, 2:] * w[2]
    x0, x1, v = jnp.split(hc, 3, axis=-1)
    u = x1 * v
    kf = hyena_filter(t, p['hyena_w1'].astype(F32), p['hyena_b1'].astype(F32), p['hyena_w2'].astype(F32),
                      p['hyena_b2'].astype(F32), p['hyena_w3'].astype(F32), p['hyena_freq'].astype(F32))
    y = jnp.fft.irfft(jnp.fft.rfft(u, n=2 * t, axis=1) * jnp.fft.rfft(kf, axis=0)[None], n=2 * t, axis=1)[:, :t]
    return x0 * (y + p['hyena_bias'].astype(F32) * u)


def mixer(h, p, lb, lg_f, lg_b, s_h0, s_r0, rope_pos):
    proj = (h @ p['w_in']).astype(F32)
    qa, ia, zf, zb, ga = jnp.split(proj[..., :5 * A_WIDTH], 5, axis=-1)
    hy = proj[..., 5 * A_WIDTH:5 * A_WIDTH + 3 * B_WIDTH]
    qc, kc, vc, gc = jnp.split(proj[..., 5 * A_WIDTH + 3 * B_WIDTH:], 4, axis=-1)

    q = to_heads(jax.nn.silu(qa), A_HEADS)
    v = to_heads(ia, A_HEADS)
    k_f, logf_f = hgrn_gates(to_heads(zf, A_HEADS), lb[0])
    k_b, logf_b = hgrn_gates(to_heads(zb, A_HEADS), lb[1])
    o_f, s_hf = chunk_gla(q, k_f, v, logf_f, s_h0[:, 0].astype(F32))
    o_b, s_hb = chunk_gla(flip_t(q), flip_t(k_b), flip_t(v), flip_t(logf_b), s_h0[:, 1].astype(F32))
    o_a = head_rmsnorm(o_f + flip_t(o_b)) * p['hgrn_norm_w'].astype(F32)
    a_out = from_heads(o_a) * jax.nn.silu(ga)

    b_out = hyena(hy, p)

    qr = to_heads(qc, C_HEADS)
    kr = to_heads(kc, C_HEADS) * (HEAD_DIM ** -0.5)
    vr = to_heads(vc, C_HEADS)
    if rope_pos is not None:
        qr = rope_2d(qr, rope_pos[0], rope_pos[1])
        kr = rope_2d(kr, rope_pos[0], rope_pos[1])
    r_f, s_rf = chunk_retention(qr, kr, vr, lg_f, s_r0[:, 0].astype(F32))
    r_b, s_rb = chunk_retention(flip_t(qr), flip_t(kr), flip_t(vr), lg_b, s_r0[:, 1].astype(F32))
    c_out = from_heads(head_rmsnorm(r_f + flip_t(r_b))) * jax.nn.silu(gc)

    out = jnp.concatenate([a_out, b_out, c_out], axis=-1).astype(h.dtype) @ p['w_out']
    return out, jnp.stack([s_hf, s_hb], axis=1), jnp.stack([s_rf, s_rb], axis=1)


def trunk_layer(x, mod, p, lb, lg_f, lg_b, s_h0, s_r0, rope_pos):
    m = mod.astype(x.dtype)[:, :, None, :]
    h = rmsnorm(x, p['norm_w'][0]) * (1.0 + m[:, 1]) + m[:, 0]
    x = x + 0.5 * m[:, 2] * swiglu(h, p['ffn_in'][0], p['ffn_out'][0])
    h = rmsnorm(x, p['norm_w'][1]) * (1.0 + m[:, 4]) + m[:, 3]
    mix, s_h, s_r = mixer(h, p, lb, lg_f, lg_b, s_h0, s_r0, rope_pos)
    x = x + m[:, 5] * mix
    h = rmsnorm(x, p['norm_w'][2]) * (1.0 + m[:, 7]) + m[:, 6]
    x = x + 0.5 * m[:, 8] * swiglu(h, p['ffn_in'][1], p['ffn_out'][1])
    return x, s_h, s_r


def setup_inputs(seed: int = 0) -> dict:
    key = jax.random.key(seed)
    ks = jax.random.split(key, 24)

    def nrm(k, shape, s):
        return jax.random.normal(k, shape, F32) * s

    return {
        'x_prompt': nrm(ks[0], (BATCH, SEQ, D_MODEL), 1.0),
        'x_sample': nrm(ks[1], (DEC_BATCH, DEC_SEQ, D_MODEL), 1.0),
        'state_hgrn': nrm(ks[2], (DEC_BATCH, DEPTH, 2, A_HEADS, HEAD_DIM, HEAD_DIM), 0.5),
        'state_ret': nrm(ks[3], (DEC_BATCH, DEPTH, 2, C_HEADS, HEAD_DIM, HEAD_DIM), 0.5),
        'c': nrm(ks[4], (DEC_BATCH, D_MODEL), 1.0),
        'c_ctx': nrm(ks[5], (D_MODEL,), 1.0),
        'w_ada': nrm(ks[6], (DEPTH, D_MODEL, N_MOD * D_MODEL), 0.5 * D_MODEL ** -0.5),
        'b_ada': nrm(ks[7], (DEPTH, N_MOD * D_MODEL), 0.02),
        'norm_w': 1.0 + nrm(ks[8], (DEPTH, 3, D_MODEL), 0.02),
        'ffn_in': nrm(ks[9], (DEPTH, 2, D_MODEL, 2 * D_FF), D_MODEL ** -0.5),
        'ffn_out': nrm(ks[10], (DEPTH, 2, D_FF, D_MODEL), D_FF ** -0.5),
        'w_in': nrm(ks[11], (DEPTH, D_MODEL, IN_COLS), D_MODEL ** -0.5),
        'w_out': nrm(ks[12], (DEPTH, MIX_W, D_MODEL), MIX_W ** -0.5),
        'hgrn_lb': nrm(ks[13], (2, DEPTH, A_WIDTH), 1.0),
        'hgrn_norm_w': 1.0 + nrm(ks[14], (DEPTH, HEAD_DIM), 0.02),
        'hyena_conv': nrm(ks[15], (DEPTH, SHORT_CONV, 3 * B_WIDTH), SHORT_CONV ** -0.5),
        'hyena_w1': nrm(ks[16], (DEPTH, POS_EMB, FILTER_HID), POS_EMB ** -0.5),
        'hyena_b1': nrm(ks[17], (DEPTH, FILTER_HID), 0.02),
        'hyena_w2': nrm(ks[18], (DEPTH, FILTER_HID, FILTER_HID), FILTER_HID ** -0.5),
        'hyena_b2': nrm(ks[19], (DEPTH, FILTER_HID), 0.02),
        'hyena_w3': nrm(ks[20], (DEPTH, FILTER_HID, 2 * B_WIDTH), 0.1 * FILTER_HID ** -0.5),
        'hyena_freq': 1.0 + nrm(ks[21], (DEPTH, FILTER_HID), 0.02),
        'hyena_bias': nrm(ks[22], (DEPTH, B_WIDTH), 0.1),
        'final_norm_w': 1.0 + nrm(ks[23], (D_MODEL,), 0.02),
    }


def reference(x_prompt, x_sample, state_hgrn, state_ret, c, c_ctx, w_ada, b_ada, norm_w, ffn_in, ffn_out,
              w_in, w_out, hgrn_lb, hgrn_norm_w, hyena_conv, hyena_w1, hyena_b1, hyena_w2, hyena_b2,
              hyena_w3, hyena_freq, hyena_bias, final_norm_w):
    lb_all = jnp.cumsum(jax.nn.softmax(hgrn_lb.astype(F32), axis=1), axis=1)
    lb_all = lb_all - lb_all[:, :1]
    lg_all = jnp.log1p(-jnp.exp2(-5.0 - 0.5 * jnp.arange(2 * C_HEADS, dtype=F32)))
    lg_f, lg_b = lg_all[0::2], lg_all[1::2]

    t_lat = x_sample.shape[1]
    rows = t_lat // GRID_W
    row_pos = jnp.repeat(jnp.arange(rows, dtype=F32), GRID_W)
    col_pos = jnp.tile(jnp.arange(GRID_W, dtype=F32), rows)

    b_ctx = x_prompt.shape[0]
    zero_h = jnp.zeros((b_ctx, 2, A_HEADS, HEAD_DIM, HEAD_DIM), F32)
    zero_r = jnp.zeros((b_ctx, 2, C_HEADS, HEAD_DIM, HEAD_DIM), F32)

    xp = x_prompt
    xs = x_sample
    new_h = []
    new_r = []
    for l in range(DEPTH):
        p = {
            'norm_w': norm_w[l], 'ffn_in': ffn_in[l], 'ffn_out': ffn_out[l],
            'w_in': w_in[l], 'w_out': w_out[l], 'hgrn_norm_w': hgrn_norm_w[l],
            'hyena_conv': hyena_conv[l], 'hyena_w1': hyena_w1[l], 'hyena_b1': hyena_b1[l],
            'hyena_w2': hyena_w2[l], 'hyena_b2': hyena_b2[l], 'hyena_w3': hyena_w3[l],
            'hyena_freq': hyena_freq[l], 'hyena_bias': hyena_bias[l],
        }
        lb = lb_all[:, l].reshape(2, A_HEADS, HEAD_DIM)
        mod_ctx = ada(c_ctx[None], w_ada[l], b_ada[l])
        xp, s_h, s_r = trunk_layer(xp, mod_ctx, p, lb, lg_f, lg_b, zero_h, zero_r, None)
        new_h.append(s_h)
        new_r.append(s_r)
        mod_lat = ada(c, w_ada[l], b_ada[l])
        xs, _, _ = trunk_layer(xs, mod_lat, p, lb, lg_f, lg_b, state_hgrn[:, l], state_ret[:, l],
                               (row_pos, col_pos))

    y_prompt = rmsnorm(xp, final_norm_w)
    y_sample = rmsnorm(xs, final_norm_w)
    new_state_hgrn = jnp.stack(new_h, axis=1).astype(x_prompt.dtype)
    new_state_ret = jnp.stack(new_r, axis=1).astype(x_prompt.dtype)
    return (y_prompt, y_sample, new_state_hgrn, new_state_ret)
```

```python
import concourse.bass as bass
import concourse.mybir as mybir

import os
ANNOTATE = bool(os.environ.get("KANNOT"))
ENGS = ("pe", "act", "dve", "pool", "sp")
DT_SIZE = {"dt.float32": 4, "dt.bfloat16": 2, "dt.int32": 4, "dt.uint32": 4, "dt.float16": 2, "dt.uint8": 1, "dt.int8": 1, "dt.uint16": 2, "dt.int16": 2}


class Op:
    __slots__ = ("id", "eng", "fn", "deps", "is_dma", "signal", "count", "dsem", "dval", "is_mm", "tag")

    def __init__(self, id, eng, fn, is_dma, is_mm):
        self.id = id
        self.eng = eng
        self.fn = fn
        self.deps = set()
        self.is_dma = is_dma
        self.is_mm = is_mm
        self.signal = False
        self.count = 0
        self.dsem = None
        self.dval = 0


def footprint(ap):
    sp = str(ap.space)
    if "DRAM" in sp.upper():
        return None
    t = ap.tensor
    shp = list(t.shape)
    F = 1
    for s in shp[1:]:
        F *= s
    off = int(ap.offset)
    pairs = ap.ap
    esz = DT_SIZE[str(ap.dtype)]
    p0 = off // F
    lo = off % F
    pstep, pcnt = pairs[0]
    if pstep == F or pcnt == 1:
        p1 = p0 + pcnt
        rest = pairs[1:]
    else:
        p1 = p0 + 1
        rest = pairs
    ext = 0
    for st, cn in rest:
        ext += abs(st) * (cn - 1)
    hi = lo + ext + 1
    if 'PSUM' in sp.upper():
        return (ap.tensor.name, (p0 // 32) * 32, ((p1 + 31) // 32) * 32, 0, 1 << 20)
    return (ap.tensor.name, p0, p1, lo * esz, hi * esz)


class Prog:
    def __init__(self, nc, same_engine_sync=True, ndma_sems=8):
        self.nc = nc
        self.ops = []
        self.recs = {}
        self.same_engine_sync = same_engine_sync
        self.ndma = ndma_sems

    def add(self, eng, fn, reads=(), writes=(), is_dma=False, is_mm=False):
        op = Op(len(self.ops), eng, fn, is_dma, is_mm)
        op.tag = getattr(self, 'tag', '')
        self.ops.append(op)
        for ap in reads:
            fp = footprint(ap)
            if fp is None:
                continue
            self._access(op, fp, False)
        for ap in writes:
            fp = footprint(ap)
            if fp is None:
                continue
            self._access(op, fp, True)
        return op

    def _access(self, op, fp, is_write):
        name, p0, p1, lo, hi = fp
        lst = self.recs.setdefault(name, [])
        keep = []
        for r in lst:
            ov = not (r[1] <= p0 or p1 <= r[0] or r[3] <= lo or hi <= r[2])
            if ov and r[4] != op.id:
                if is_write or r[5]:
                    op.deps.add(r[4])
                if is_write and r[0] >= p0 and r[1] <= p1 and r[2] >= lo and r[3] <= hi:
                    continue
            keep.append(r)
        keep.append([p0, p1, lo, hi, op.id, is_write])
        self.recs[name] = keep

    def finalize(self):
        nc = self.nc
        ops = self.ops
        for op in ops:
            best = {}
            for d in list(op.deps):
                o = ops[d]
                if o.is_dma:
                    continue
                if o.eng not in best or d > best[o.eng]:
                    best[o.eng] = d
            for d in list(op.deps):
                o = ops[d]
                if not o.is_dma and best[o.eng] != d:
                    op.deps.discard(d)
        for op in ops:
            for d in list(op.deps):
                o = ops[d]
                if o.is_dma:
                    continue
                if o.eng == op.eng and not op.is_dma:
                    if o.eng == "pe" or not self.same_engine_sync:
                        op.deps.discard(d)
                        continue
                o.signal = True
        cnt = {e: 0 for e in ENGS}
        dcnt = {e: 0 for e in ENGS}
        for op in ops:
            if op.is_dma:
                i = dcnt[op.eng]
                dcnt[op.eng] += 1
                op.dsem = (op.eng, i % self.ndma)
                op.dval = 16 * (i // self.ndma + 1)
            elif op.signal:
                cnt[op.eng] += 1
                op.count = cnt[op.eng]
        self.cnt = cnt
        import contextlib
        stack = contextlib.ExitStack()
        self.stack = stack
        esem = {e: stack.enter_context(nc.semaphore("c_" + e)) for e in ENGS}
        dsem = {}
        for e in ENGS:
            if dcnt[e]:
                for j in range(self.ndma):
                    dsem[(e, j)] = stack.enter_context(nc.semaphore("d_%s%d" % (e, j)))
        byeng = {e: [o for o in ops if o.eng == e] for e in ENGS}
        engobj = {"pe": "tensor", "act": "scalar", "dve": "vector", "pool": "gpsimd", "sp": "sync"}
        last_dma = {}

        def run_engine(e, eng):
            waited = {}
            lst = byeng[e]
            for op in lst:
                need = {}
                for d in op.deps:
                    o = ops[d]
                    if o.is_dma:
                        key = ("d",) + o.dsem
                        v = o.dval
                    else:
                        key = ("c", o.eng)
                        v = o.count
                    if v > need.get(key, 0):
                        need[key] = v
                if op.is_dma:
                    if op.dval > 16:
                        key = ("d",) + op.dsem
                        v = op.dval - 16
                        if v > need.get(key, 0):
                            need[key] = v
                for key, v in need.items():
                    if waited.get(key, 0) >= v:
                        continue
                    waited[key] = v
                    sem = dsem[key[1:]] if key[0] == "d" else esem[key[1]]
                    eng.wait_ge(sem, v)
                ins = op.fn(eng)
                if ANNOTATE and op.tag:
                    ins.annotate(op.tag)
                if op.is_dma:
                    ins.then_inc(dsem[op.dsem], 16)
                elif op.signal:
                    ins.then_inc(esem[op.eng], 1)
            for j in range(self.ndma):
                if (e, j) in dsem:
                    n = (dcnt[e] - 1 - j) // self.ndma + 1 if dcnt[e] > j else 0
                    if n > 0:
                        eng.wait_ge(dsem[(e, j)], 16 * n)

        with nc.Block() as block:
            @block.tensor
            def _(eng):
                run_engine("pe", eng)

            @block.scalar
            def _(eng):
                run_engine("act", eng)

            @block.vector
            def _(eng):
                run_engine("dve", eng)

            @block.gpsimd
            def _(eng):
                run_engine("pool", eng)

            @block.sync
            def _(eng):
                run_engine("sp", eng)
        stack.close()

import math
import numpy as np
import ml_dtypes
from concourse.bass_utils import run_bass_kernel_spmd

F32 = mybir.dt.float32
BF16 = mybir.dt.bfloat16
AF = mybir.ActivationFunctionType
ALU = mybir.AluOpType
AX = mybir.AxisListType

D = 1024
T = 1024
DFF = 2816
NF = 22
EPS = 1e-6


class KB:
    def __init__(self, L, do_mixer=True):
        self.L = L
        self.do_mixer = do_mixer
        nc = bass.Bass("TRN2", target_bir_lowering=False)
        self.nc = nc
        self.P = Prog(nc)
        self.din = {}
        self.dout = {}
        self.nbank = 0

    def inp(self, name, shape, dt=F32):
        self.din[name] = self.nc.dram_tensor(name, list(shape), dt, kind="ExternalInput").ap()
        return self.din[name]

    def outp(self, name, shape, dt=F32):
        self.dout[name] = self.nc.dram_tensor(name, list(shape), dt, kind="ExternalOutput").ap()
        return self.dout[name]

    def mm(self, out, lhsT, rhs, start, stop):
        self.P.add("pe", lambda e: e.matmul(out, lhsT=lhsT, rhs=rhs, start=start, stop=stop),
                   reads=[lhsT, rhs], writes=[out], is_mm=True)

    def tr(self, out, in_, ident):
        self.P.add("pe", lambda e: e.transpose(out, in_, ident), reads=[in_, ident], writes=[out], is_mm=True)

    def act(self, out, in_, func, bias=None, scale=None, eng="act"):
        kw = {}
        rd = [in_]
        if bias is not None:
            kw["bias"] = bias
            if not isinstance(bias, (int, float)):
                rd.append(bias)
        if scale is not None:
            kw["scale"] = scale
            if not isinstance(scale, (int, float)):
                rd.append(scale)
        self.P.add(eng, lambda e: e.activation(out=out, in_=in_, func=func, **kw), reads=rd, writes=[out])

    def tt(self, out, in0, in1, op, eng="dve"):
        self.P.add(eng, lambda e: e.tensor_tensor(out=out, in0=in0, in1=in1, op=op), reads=[in0, in1], writes=[out])

    def ts(self, out, in0, s1, s2, op0, op1=None, eng="dve"):
        rd = [in0]
        for s in (s1, s2):
            if s is not None and not isinstance(s, (int, float)):
                rd.append(s)
        if op1 is None:
            self.P.add(eng, lambda e: e.tensor_scalar(out=out, in0=in0, scalar1=s1, scalar2=None, op0=op0), reads=rd, writes=[out])
        else:
            self.P.add(eng, lambda e: e.tensor_scalar(out=out, in0=in0, scalar1=s1, scalar2=s2, op0=op0, op1=op1), reads=rd, writes=[out])

    def stt(self, out, in0, scalar, in1, op0, op1, eng="dve"):
        rd = [in0, in1]
        if not isinstance(scalar, (int, float)):
            rd.append(scalar)
        self.P.add(eng, lambda e: e.scalar_tensor_tensor(out=out, in0=in0, scalar=scalar, in1=in1, op0=op0, op1=op1),
                   reads=rd, writes=[out])

    def cp(self, out, in_, eng="dve"):
        if eng == "act":
            self.P.add(eng, lambda e: e.copy(out=out, in_=in_), reads=[in_], writes=[out])
        else:
            self.P.add(eng, lambda e: e.tensor_copy(out=out, in_=in_), reads=[in_], writes=[out])

    def memset(self, out, v, eng="dve"):
        self.P.add(eng, lambda e: e.memset(out, v), writes=[out])

    def scan(self, out, d0, d1, init=0.0):
        self.P.add("dve", lambda e: e.tensor_tensor_scan(out=out, data0=d0, data1=d1, initial=init, op0=ALU.mult, op1=ALU.add),
                   reads=[d0, d1], writes=[out])

    def dma(self, q, out, in_):
        self.P.add(q, lambda e: e.dma_start(out=out, in_=in_), reads=[in_], writes=[out], is_dma=True)

    def bank(self):
        b = self.banks[self.nbank % 8]
        self.nbank += 1
        return b

    def build(self):
        nc = self.nc
        L = self.L
        xT_in = self.inp("xT", [D, T])
        cond_in = self.inp("cond", [128, 8])
        w_ada = self.inp("w_ada", [L, D, 9 * D])
        b_adaT = self.inp("b_adaT", [128, L, 72])
        norm_wT = self.inp("norm_wT", [128, L, 3, 8])
        fin_wT = self.inp("fin_wT", [128, 8])
        ffn_in = self.inp("ffn_in", [L, 2, D, 2 * DFF])
        ffn_out = self.inp("ffn_out", [L, 2, DFF, D])
        yT_out = self.outp("yT", [D, T])
        if self.do_mixer:
            self.mixer_decl()

        self.xT = nc.alloc_sbuf_tensor("xTs", [128, 8, T], F32)
        self.hT = nc.alloc_sbuf_tensor("hTs", [128, 8, T], BF16)
        self.slots = [nc.alloc_sbuf_tensor("slot%d" % i, [128, 4096], BF16) for i in range(3)]
        self.nslot = 0
        self.AR = nc.alloc_sbuf_tensor("arena", [128, 22 * 1024], F32)
        self.small = nc.alloc_sbuf_tensor("small", [128, 548], F32)
        self.rstd = nc.alloc_sbuf_tensor("rstd", [128, T], F32)
        self.tmpf = [nc.alloc_sbuf_tensor("tmpf%d" % i, [128, T], F32) for i in range(2)]
        self.ones = nc.alloc_sbuf_tensor("ones", [128, 128], BF16)
        self.banks = [nc.alloc_psum_tensor("bank%d" % i, [128, 512], F32) for i in range(8)]
        self.ntmp = 0
        sm = self.small
        self.cond = sm[:, 0:8]
        self.scond = sm[:, 8:16]
        self.modT = sm[:, 16:88]
        self.badaT = sm[:, 88:88 + 72 * L].rearrange("p (l j) -> p l j", l=L)
        o = 88 + 72 * 4
        self.normw = sm[:, o:o + 24 * L].rearrange("p (l i c) -> p l i c", l=L, i=3)
        o += 24 * 4
        self.finw = sm[:, o:o + 8]
        o += 8
        self.Ascale = sm[:, o:o + 24].rearrange("p (i c) -> p i c", i=3)
        o += 24
        self.gates = sm[:, o:o + 24].rearrange("p (i c) -> p i c", i=3)
        o += 24
        self.scondb = nc.alloc_sbuf_tensor("scondb", [128, 8], BF16)
        self.modT2 = nc.alloc_sbuf_tensor("modT2", [128, 72], F32)
        self.modTs = [self.modT, self.modT2[:, :]]
        self.eps1k = sm[:, o:o + 1]
        o += 1
        self.small_o = o

        self.dma("sp", self.xT[:], xT_in.rearrange("(c p) t -> p c t", p=128))
        self.dma("sp", self.cond, cond_in)
        self.dma("sp", self.badaT, b_adaT)
        self.dma("sp", self.normw, norm_wT)
        self.dma("sp", self.finw, fin_wT)
        self.memset(self.ones[:], 1.0)
        self.memset(self.eps1k, D * EPS)
        if self.do_mixer:
            self.mixer_init()
        self.act(self.scond, self.cond, AF.Silu)
        self.cp(self.scondb[:], self.scond)

        for _ in self.ada_steps(0, w_ada):
            pass
        for l in range(L):
            self.ada_finish(l)
            self.norm(0, self.normw[:, l, 0, :])
            self.ffn(ffn_in[l, 0], ffn_out[l, 0], self.gates[:, 0, :])
            if self.do_mixer:
                self.norm(1, self.normw[:, l, 1, :])
                self.mixer(l)
            self.norm(2, self.normw[:, l, 2, :])
            nxt = self.ada_steps(l + 1, w_ada) if l + 1 < L else None
            self.ffn(ffn_in[l, 1], ffn_out[l, 1], self.gates[:, 2, :], extra=nxt)
        self.rms_stats()
        fa = sm[:, self.small_o:self.small_o + 8]
        self.ts(fa, self.finw, 32.0, None, ALU.mult)
        yv = self.AR[:, 0:8 * T].rearrange("p (c t) -> p c t", c=8)
        for c in range(8):
            self.stt(yv[:, c, :], self.xT[:, c, :], fa[:, c:c + 1], self.rstd[:], ALU.mult, ALU.mult)
        self.dma("sp", yT_out.rearrange("(c p) t -> p c t", p=128), yv)
        self.P.finalize()

    def next_slot(self):
        s = self.slots[self.nslot % len(self.slots)]
        self.nslot += 1
        return s

    def ada_steps(self, l, w_ada):
        modT = self.modTs[l % 2]
        for j4 in range(18):
            ptag = self.P.tag if hasattr(self.P, "tag") else ""
            self.P.tag = "ada"
            slot = self.next_slot()
            sv = slot[:, :].rearrange("p (k n) -> p k n", k=8)
            self.dma("pool", sv, w_ada[l].rearrange("(k p) n -> p k n", p=128)[:, :, j4 * 512:(j4 + 1) * 512])
            psm = self.bank()
            for jj in range(4):
                for k in range(8):
                    self.mm(psm[:, jj:jj + 1], sv[:, k, jj * 128:(jj + 1) * 128], self.scondb[:, k:k + 1], k == 0, k == 7)
            self.tt(modT[:, j4 * 4:(j4 + 1) * 4], psm[:, 0:4], self.badaT[:, l, j4 * 4:(j4 + 1) * 4], ALU.add)
            self.P.tag = ptag
            yield

    def ada_finish(self, l):
        mod = self.modTs[l % 2].rearrange("p (m c) -> p m c", m=9)
        for i in range(3):
            self.stt(self.Ascale[:, i, :], mod[:, 3 * i + 1, :], 1.0, self.normw[:, l, i, :], ALU.add, ALU.mult)
            self.ts(self.Ascale[:, i, :], self.Ascale[:, i, :], 32.0, None, ALU.mult)
        self.ts(self.gates[:, 0, :], mod[:, 2, :], 0.5, None, ALU.mult)
        self.cp(self.gates[:, 1, :], mod[:, 5, :])
        self.ts(self.gates[:, 2, :], mod[:, 8, :], 0.5, None, ALU.mult)
        self.mod = mod

    def rms_stats(self):
        self.P.tag = "norm"
        sq = self.hT
        for c in range(8):
            if c % 2 == 0:
                self.act(sq[:, c, :], self.xT[:, c, :], AF.Square)
            else:
                self.tt(sq[:, c, :], self.xT[:, c, :], self.xT[:, c, :], ALU.mult)
        for th in range(2):
            ps = self.bank()
            for c in range(8):
                self.mm(ps[:, :], self.ones[:], sq[:, c, th * 512:(th + 1) * 512], c == 0, c == 7)
            rs = self.rstd[:, th * 512:(th + 1) * 512]
            self.act(rs, ps[:, :], AF.Ln, bias=self.eps1k, scale=1.0)
            self.act(rs, rs, AF.Exp, scale=-0.5)

    def norm(self, i, nw):
        self.rms_stats()
        for c in range(8):
            tmp = self.tmpf[self.ntmp % 2]
            self.ntmp += 1
            self.stt(tmp[:], self.xT[:, c, :], self.Ascale[:, i, c:c + 1], self.rstd[:], ALU.mult, ALU.mult)
            self.act(self.hT[:, c, :], tmp[:], AF.Identity, bias=self.mod[:, 3 * i, c:c + 1], scale=1.0)

    def ffn(self, w_in, w_out, gate, extra=None):
        self.P.tag = "ffn_in"
        actT = self.AR[:, 0:NF * 512].bitcast(BF16).rearrange("p (f t) -> p f t", f=NF)
        wv = w_in.rearrange("(k p) (b j c) -> p k b j c", p=128, b=2, j=11)
        for j in range(11):
            slot = self.next_slot()
            sv = slot[:, :].rearrange("p (k b c) -> p k b c", k=8, b=2)
            for b in range(2):
                self.dma("pool", sv[:, :, b, :], wv[:, :, b, j, :])
            for fc in range(2):
                f = 2 * j + fc
                pg = [self.bank(), self.bank()]
                pu = [self.bank(), self.bank()]
                for k in range(8):
                    for tt_ in range(2):
                        self.mm(pg[tt_][:, :], sv[:, k, 0, fc * 128:(fc + 1) * 128], self.hT[:, k, tt_ * 512:(tt_ + 1) * 512], k == 0, k == 7)
                for k in range(8):
                    for tt_ in range(2):
                        self.mm(pu[tt_][:, :], sv[:, k, 1, fc * 128:(fc + 1) * 128], self.hT[:, k, tt_ * 512:(tt_ + 1) * 512], k == 0, k == 7)
                for tt_ in range(2):
                    tmp = self.tmpf[self.ntmp % 2]
                    self.ntmp += 1
                    self.act(tmp[:, 0:512], pg[tt_][:, :], AF.Silu)
                    self.tt(actT[:, f, tt_ * 512:(tt_ + 1) * 512], tmp[:, 0:512], pu[tt_][:, :], ALU.mult)
            if extra is not None:
                next(extra, None)
        self.P.tag = "ffn_out"
        wo = w_out.rearrange("(f p) d -> p f d", p=128)
        for dc in range(8):
            slot = self.next_slot()
            sv = slot[:, 0:NF * 128].rearrange("p (f d) -> p f d", f=NF)
            self.dma("pool", sv, wo[:, :, dc * 128:(dc + 1) * 128])
            po = [self.bank(), self.bank()]
            for f in range(NF):
                for tt_ in range(2):
                    self.mm(po[tt_][:, :], sv[:, f, :], actT[:, f, tt_ * 512:(tt_ + 1) * 512], f == 0, f == NF - 1)
            for tt_ in range(2):
                xs = self.xT[:, dc, tt_ * 512:(tt_ + 1) * 512]
                self.stt(xs, po[tt_][:, :], gate[:, dc:dc + 1], xs, ALU.mult, ALU.add)
            if extra is not None:
                next(extra, None)
        if extra is not None:
            for _ in extra:
                pass


def _prep_common(inputs, L):
    f = lambda a: np.ascontiguousarray(np.asarray(a, dtype=np.float32))
    com = {}
    com["w_ada"] = f(inputs["w_ada"])[:L]
    com["b_adaT"] = f(np.asarray(inputs["b_ada"])[:L].reshape(L, 72, 128).transpose(2, 0, 1))
    com["norm_wT"] = f(np.asarray(inputs["norm_w"])[:L].reshape(L, 3, 8, 128).transpose(3, 0, 1, 2))
    com["fin_wT"] = f(np.asarray(inputs["final_norm_w"]).reshape(8, 128).T)
    com["ffn_in"] = f(inputs["ffn_in"])[:L]
    com["ffn_out"] = f(inputs["ffn_out"])[:L]
    return com


def core_assign():
    return [("s", 0), ("s", 1), ("p", 0), ("p", 1), ("p", 2), ("p", 3), ("p", 3), ("p", 3)]

AW = 22 * 1024


def _mixer_decl(self):
    L = self.L
    i = self.inp
    self.w_in = i("w_in", [L, D, 3840])
    self.w_out = i("w_out", [L, D, D])
    self.w_inp = i("w_inp", [L, D, 512])
    self.d_cont = i("cont", [128, 1])
    self.d_ropeC = i("ropeC", [128, T])
    self.d_ropeS = i("ropeS", [128, T])
    self.d_smask = i("smask", [128, T])
    self.d_cm = i("cm", [128, 2, 512])
    self.d_chm = i("chm", [128, 4])
    self.d_gvec = i("gvec", [128, 32])
    self.d_ident = i("ident", [128, 128], BF16)
    self.d_bones = i("bones", [128, 128], BF16)
    self.d_retT = i("retT", [128, 2, 2, 3, 32])
    self.d_retF = i("retF", [128, 2, 2, 32])
    self.d_lb = i("hgrn_lbT", [128, 4, 4])
    self.d_hnw = i("hnwT", [128, L])
    self.d_conv = i("convT", [128, L, 3, 12])
    self.d_hbias = i("hbiasT", [128, L, 4])
    self.d_zT = i("zT", [33, T])
    self.d_w1 = i("hy_w1", [L, 33, 64])
    self.d_w2 = i("hy_w2", [L, 64, 64])
    self.d_w3 = i("hy_w3", [L, 64, 1024])
    self.d_fb = i("hy_fb", [64, L, 5])
    self.d_delta = i("deltab", [128, 512])
    self.d_negtl = i("negtl", [128, 8])
    self.d_m0 = i("m0", [128, 8])
    self.d_Cf = i("Cf", [T, T], BF16)
    self.d_Sf = i("Sf", [T, T], BF16)
    self.d_Ci = i("CiT", [T, T], BF16)
    self.d_Si = i("SiT", [T, T], BF16)
    self.d_s0 = i("s0", [128, 4, L, 2, 64])
    self.o_sth = self.outp("sth", [4, L, 2, 4, 64, 64])
    self.o_str = self.outp("str", [4, L, 2, 4, 64, 64])


def _mixer_init(self):
    nc = self.nc
    L = self.L
    a = lambda n, s, dt=F32: nc.alloc_sbuf_tensor("s_" + n, s, dt)
    self.ropeC = a("ropeC", [128, T]); self.ropeS = a("ropeS", [128, T]); self.smask = a("smask", [128, T])
    self.cm = a("cm", [128, 2, 512])
    self.cm2 = self.cm[:, :, :].rearrange("p d (k h n) -> p d k h n", k=2, h=2); self.chm = a("chm", [128, 4]); self.gvec = a("gvec", [128, 32])
    self.ident = a("ident", [128, 128], BF16); self.bones = a("bones", [128, 128], BF16)
    self.retT = a("retT", [128, 2, 2, 3, 32]); self.retF = a("retF", [128, 2, 2, 32])
    self.delta = a("delta", [128, 512]); self.zT = a("zTs", [33, T])
    self.ms = a("msmall", [128, 512])
    self.sfin = a("sfin", [128, 256])
    self.qx = [a("qx0", [128, 256], BF16), a("qx1", [128, 256], BF16)]
    self.memset(self.qx[0][:], 0.0); self.memset(self.qx[1][:], 0.0)
    self.s0 = a("s0s", [128, 4, 2, 64])
    self.hw1 = a("hw1", [33, L, 64]); self.hw2 = a("hw2", [64, L, 64])
    ms = self.ms
    o = 0

    def take(n):
        nonlocal o
        v = ms[:, o:o + n]
        o += n
        return v
    self.cont = take(1); self.cm1 = take(1)
    self.lbx = take(16).rearrange("p (a l) -> p a l", l=4)
    self.oml = take(16).rearrange("p (a l) -> p a l", l=4)
    self.lbs = take(8)
    self.lnoml = take(16).rearrange("p (a l) -> p a l", l=4)
    self.epsc = take(1)
    self.hnw = take(L); self.onesc = take(1)
    self.conv = take(L * 36).rearrange("p (l t c) -> p l t c", l=L, t=3)
    self.nconv = take(24).rearrange("p (t c) -> p t c", t=2)
    self.hbias = take(L * 4).rearrange("p (l c) -> p l c", l=L)
    self.fb = take(L * 5).rearrange("p (l c) -> p l c", l=L)
    self.fs = take(4)
    self.negtl = take(8); self.m0 = take(8)
    self.Fv = take(32); self.Fg = take(32)
    q = "sp"
    for dst, src in ((self.ropeC[:], self.d_ropeC), (self.ropeS[:], self.d_ropeS), (self.smask[:], self.d_smask),
                     (self.cm[:], self.d_cm), (self.chm[:], self.d_chm), (self.gvec[:], self.d_gvec),
                     (self.ident[:], self.d_ident), (self.bones[:], self.d_bones), (self.retT[:], self.d_retT),
                     (self.retF[:], self.d_retF), (self.delta[:], self.d_delta), (self.zT[:], self.d_zT),
                     (self.cont, self.d_cont), (self.lbx, self.d_lb), (self.hnw, self.d_hnw), (self.conv, self.d_conv),
                     (self.hbias, self.d_hbias), (self.fb[0:64], self.d_fb), (self.negtl, self.d_negtl), (self.m0, self.d_m0),
                     (self.hw1[:], self.d_w1.rearrange("l k n -> k l n")),
                     (self.hw2[:], self.d_w2.rearrange("l k n -> k l n"))):
        self.dma(q, dst, src)
    self.memset(self.onesc, 1.0)
    self.ts(self.cm1, self.cont, -1.0, None, ALU.add)
    self.act(self.lbx, self.lbx, AF.Exp)
    self.P.add("dve", lambda e: e.reduce_sum(out=self.lbs[:, 0:4], in_=self.lbx, axis=AX.X), reads=[self.lbx], writes=[self.lbs[:, 0:4]])
    self.P.add("dve", lambda e: e.reciprocal(out=self.lbs[:, 4:8], in_=self.lbs[:, 0:4]), reads=[self.lbs[:, 0:4]], writes=[self.lbs[:, 4:8]])
    self.tt(self.lbx, self.lbx, self.lbs[:, 4:8].unsqueeze(2).broadcast_to([128, 4, 4]), ALU.mult)
    self.memset(self.oml[:, :, 0:1], 1.0)
    for l in range(1, 4):
        self.tt(self.oml[:, :, l:l + 1], self.oml[:, :, l - 1:l], self.lbx[:, :, l:l + 1], ALU.subtract)
    self.act(self.lnoml, self.oml, AF.Ln)
    self.memset(self.epsc, EPS)


def _slotload(self, pieces):
    N = sum(p[1] for p in pieces)
    slot = self.next_slot()
    sv = slot[:, 0:8 * N].rearrange("p (k n) -> p k n", k=8)
    o = 0
    for src, n, q in pieces:
        self.dma(q, sv[:, :, o:o + n], src)
        o += n
    return sv


def _wcols(self, w, c0, n):
    return (w.rearrange("(k p) n -> p k n", p=128)[:, :, c0:c0 + n], n, "pool")


def _mixer(self, l):
    AR = self.AR
    L = self.L
    W = self.w_in[l]
    o = 0

    def fw(n):
        nonlocal o
        v = AR[:, o:o + n]
        o += n
        return v
    mixT = fw(4096).bitcast(BF16).rearrange("p (c t) -> p c t", c=8)
    v_tm = fw(2048).bitcast(BF16).rearrange("p (k n) -> p k n", k=8)
    base = o
    self.dma("sp", self.s0[:], self.d_s0[:, :, l, :, :])
    self.P.tag = "mix_v"
    sv = _slotload(self, [_wcols(self, W, 256, 256), _wcols(self, W, 3328, 256)])
    for tile in range(8):
        ps = self.bank()
        for k in range(8):
            self.mm(ps[:, :], self.hT[:, k, tile * 128:(tile + 1) * 128], sv[:, k, :], k == 0, k == 7)
        self.cp(v_tm[:, tile, :], ps[:, :], eng="act")

    import os
    PARTS = os.environ.get('MIX_PARTS', 'hy,gla')
    if 'hy' in PARTS:
        Gr = fw(2048).bitcast(BF16).rearrange("p (k n) -> p k n", k=8)
        Gi = fw(2048).bitcast(BF16).rearrange("p (k n) -> p k n", k=8)
        hb = o
        S_tm = fw(2048).bitcast(BF16).rearrange("p (k n) -> p k n", k=8)
        D_tm = fw(2048).bitcast(BF16).rearrange("p (k n) -> p k n", k=8)
        hid = [fw(1024), fw(1024)]
        hw3 = fw(1024)[0:64, :]
        self.P.tag = "hy_filt"
        self.dma("sp", hw3, self.d_w3[l])
        f3 = self.fs
        self.ts(f3[0:64, 0:1], self.fb[0:64, l, 2:3], 1.0 / 3.0, None, ALU.mult)
        self.tt(f3[0:64, 1:2], f3[0:64, 0:1], self.fb[0:64, l, 0:1], ALU.mult)
        self.tt(f3[0:64, 2:3], f3[0:64, 0:1], self.fb[0:64, l, 1:2], ALU.mult)

        def sin3(dst, ps, bcol):
            tmp = self.tmpf[self.ntmp % 2]
            self.ntmp += 1
            s = tmp[0:64, 0:512]
            s2 = tmp[0:64, 512:1024]
            self.act(s, ps, AF.Sin, bias=f3[0:64, bcol:bcol + 1], scale=f3[0:64, 0:1])
            self.tt(s2, s, s, ALU.mult)
            self.ts(s2, s2, -4.0, 3.0, ALU.mult, ALU.add)
            self.tt(dst, s2, s, ALU.mult)
        for th in range(2):
            ps = self.bank()
            self.mm(ps[0:64, :], self.hw1[:, l, :], self.zT[:, th * 512:(th + 1) * 512], True, True)
            sin3(hid[0][0:64, th * 512:(th + 1) * 512], ps[0:64, :], 1)
        for th in range(2):
            ps = self.bank()
            self.mm(ps[0:64, :], self.hw2[:, l, :], hid[0][0:64, th * 512:(th + 1) * 512], True, True)
            sin3(hid[1][0:64, th * 512:(th + 1) * 512], ps[0:64, :], 2)
        for tile in range(8):
            pf = self.bank()
            pb = self.bank()
            self.mm(pf[:, :], hid[1][0:64, tile * 128:(tile + 1) * 128], hw3[:, 0:512], True, True)
            self.mm(pb[:, :], hid[1][0:64, tile * 128:(tile + 1) * 128], hw3[:, 512:1024], True, True)
            t0 = self.tmpf[self.ntmp % 2]; self.ntmp += 1
            t1 = self.tmpf[self.ntmp % 2]; self.ntmp += 1
            dec = t0[:, 0:512]
            self.act(dec, self.delta[:], AF.Exp, scale=self.negtl[:, tile:tile + 1])
            self.tt(t0[:, 512:1024], pf[:, :], dec, ALU.mult)
            self.stt(t1[:, 0:512], pb[:, :], self.m0[:, tile:tile + 1], dec, ALU.mult, ALU.mult)
            self.tt(S_tm[:, tile, :], t0[:, 512:1024], t1[:, 0:512], ALU.add)
            self.tt(D_tm[:, tile, :], t0[:, 512:1024], t1[:, 0:512], ALU.subtract)
        for (tab, X, G) in ((self.d_Cf, S_tm, Gr), (self.d_Sf, D_tm, Gi)):
            tv = tab.rearrange("(k p) f -> p k f", p=128)
            for fh in range(2):
                sv = _slotload(self, [(tv[:, :, fh * 512:(fh + 1) * 512], 512, "sp")])
                for fc in range(4):
                    ps = self.bank()
                    for k in range(8):
                        self.mm(ps[:, :], sv[:, k, fc * 128:(fc + 1) * 128], X[:, k, :], k == 0, k == 7)
                    self.cp(G[:, fh * 4 + fc, :], ps[:, :], eng="act")
        self.P.tag = "hy_chunk"
        o = hb
        x0b = fw(2048).bitcast(BF16).rearrange("p (c t) -> p c t", c=4)
        ubf = fw(2048).bitcast(BF16).rearrange("p (c t) -> p c t", c=4)
        u_tm = fw(2048).bitcast(BF16).rearrange("p (k n) -> p k n", k=8)
        tb = o
        x0a = fw(1024); a1 = fw(1024); a2 = fw(1024); hraw = [fw(1024), fw(1024)]
        o = tb
        Ur = fw(2048).rearrange("p (k n) -> p k n", k=4)
        Pr = fw(2048).bitcast(BF16).rearrange("p (k n) -> p k n", k=8)
        Pi = fw(2048).bitcast(BF16).rearrange("p (k n) -> p k n", k=8)
        assert o <= AW, o
        self.ts(self.nconv[:, 0, :], self.conv[:, l, 0, :], self.cm1, None, ALU.mult)
        self.ts(self.nconv[:, 1, :], self.conv[:, l, 2, :], self.cm1, None, ALU.mult)
        nh = 0
        for j in range(4):
            sv = _slotload(self, [_wcols(self, W, 1280 + j * 128, 128), _wcols(self, W, 1792 + j * 128, 128),
                                  _wcols(self, W, 2304 + j * 128, 128)])
            accs = (x0a, a1, a2)
            for cc in range(3):
                hr = hraw[nh % 2]; nh += 1
                for th in range(2):
                    ps = self.bank()
                    for k in range(8):
                        self.mm(ps[:, :], sv[:, k, cc * 128:(cc + 1) * 128], self.hT[:, k, th * 512:(th + 1) * 512], k == 0, k == 7)
                    self.cp(hr[:, th * 512:(th + 1) * 512], ps[:, :], eng="act")
                ch = cc * 4 + j
                acc = accs[cc]
                self.act(acc, hr, AF.Copy, scale=self.conv[:, l, 1, ch:ch + 1])
                self.stt(acc[:, 1:T], hr[:, 0:T - 1], self.conv[:, l, 0, ch:ch + 1], acc[:, 1:T], ALU.mult, ALU.add)
                self.stt(acc[:, 0:T - 1], hr[:, 1:T], self.conv[:, l, 2, ch:ch + 1], acc[:, 0:T - 1], ALU.mult, ALU.add)
                self.stt(acc[:, 256:T:256], hr[:, 255:T - 1:256], self.nconv[:, 0, ch:ch + 1], acc[:, 256:T:256], ALU.mult, ALU.add)
                self.stt(acc[:, 255:T - 1:256], hr[:, 256:T:256], self.nconv[:, 1, ch:ch + 1], acc[:, 255:T - 1:256], ALU.mult, ALU.add)
            self.cp(x0b[:, j, :], x0a, eng="act")
            self.tt(ubf[:, j, :], a1, a2, ALU.mult)
        for j in range(4):
            pt = self.bank()
            ptb = pt[:, :].bitcast(BF16)
            for k in range(8):
                self.tr(ptb[:, k * 128:(k + 1) * 128], ubf[:, j, k * 128:(k + 1) * 128], self.ident[:])
            self.cp(u_tm[:, :, j * 128:(j + 1) * 128], ptb.rearrange("p (k n) -> p k n", k=8), eng=("act" if j % 2 == 0 else "dve"))
        tvC = self.d_Cf.rearrange("(k p) f -> p k f", p=128)
        tvS = self.d_Sf.rearrange("(k p) f -> p k f", p=128)
        for fh in range(2):
            sv = _slotload(self, [(tvC[:, :, fh * 512:(fh + 1) * 512], 512, "sp")])
            for fc in range(4):
                ps = self.bank()
                for k in range(8):
                    self.mm(ps[:, :], sv[:, k, fc * 128:(fc + 1) * 128], u_tm[:, k, :], k == 0, k == 7)
                self.cp(Ur[:, fc, :], ps[:, :], eng="act")
            sv = _slotload(self, [(tvS[:, :, fh * 512:(fh + 1) * 512], 512, "sp")])
            for fc in range(4):
                fi = fh * 4 + fc
                ps = self.bank()
                for k in range(8):
                    self.mm(ps[:, :], sv[:, k, fc * 128:(fc + 1) * 128], u_tm[:, k, :], k == 0, k == 7)
                Ui = ps[:, :]
                t0 = self.tmpf[0]; t1 = self.tmpf[1]
                self.tt(t0[:, 0:512], Gr[:, fi, :], Ur[:, fc, :], ALU.mult)
                self.tt(t0[:, 512:1024], Ui, Gi[:, fi, :], ALU.mult)
                self.tt(Pr[:, fi, :], t0[:, 0:512], t0[:, 512:1024], ALU.subtract)
                self.tt(t1[:, 0:512], Ui, Gr[:, fi, :], ALU.mult)
                self.tt(t1[:, 512:1024], Gi[:, fi, :], Ur[:, fc, :], ALU.mult)
                self.tt(Pi[:, fi, :], t1[:, 0:512], t1[:, 512:1024], ALU.add)
        tvCi = self.d_Ci.rearrange("(k p) t -> p k t", p=128)
        tvSi = self.d_Si.rearrange("(k p) t -> p k t", p=128)
        for th in range(2):
            sl = slice(th * 512, (th + 1) * 512)
            svc = _slotload(self, [(tvCi[:, :, sl], 512, "sp")])
            svs = _slotload(self, [(tvSi[:, :, sl], 512, "sp")])
            for j in range(4):
                py = self.bank()
                for k in range(8):
                    self.mm(py[:, :], Pr[:, k, j * 128:(j + 1) * 128], svc[:, k, :], k == 0, False)
                for k in range(8):
                    self.mm(py[:, :], Pi[:, k, j * 128:(j + 1) * 128], svs[:, k, :], False, k == 7)
                t0 = self.tmpf[j % 2]
                self.stt(t0[:, 0:512], ubf[:, j, sl], self.hbias[:, l, j:j + 1], py[:, :], ALU.mult, ALU.add)
                self.tt(mixT[:, 2 + j, sl], t0[:, 0:512], x0b[:, j, sl], ALU.mult)

    if 'gla' in PARTS:
        o = base
        qf = fw(1024); kf = fw(1024); lg = fw(1024); bb = fw(1024); E = fw(1024)
        qt = fw(512).bitcast(BF16); kt = fw(512).bitcast(BF16); kh = fw(512).bitcast(BF16)
        kh_tm = fw(512).bitcast(BF16).rearrange("p (k n) -> p k n", k=8)
        KV = fw(2112); Fr = fw(2112)
        SP = fw(2048).bitcast(BF16).rearrange("p (n m) -> p n m", n=32)
        osb = fw(1024)
        vpad = fw(1024).bitcast(BF16).rearrange("p (k h m) -> p k h m", k=8, h=2)
        vx = [fw(256).bitcast(BF16), fw(256).bitcast(BF16)]
        Ab = [fw(128).bitcast(BF16).rearrange("p (h n) -> p h n", h=2), fw(128).bitcast(BF16).rearrange("p (h n) -> p h n", h=2)]
        Sfin = self.sfin[:, :].rearrange("p (b v) -> p b v", b=4)
        assert o <= AW, o
        KV3 = KV.rearrange("p (d n) -> p d n", n=33)
        Fr3 = Fr.rearrange("p (d n) -> p d n", n=33)
        self.memset(SP[:, :, :], 0.0)
        self.memset(vpad[:, :, :, :], 0.0)
        self.memset(Fr3[:, :, 0:1], 0.0)
        nvx = 0
        for g in range(4):
            ret = g >= 2
            gp = g - 2
            vc0 = (256 + gp * 128) if ret else g * 128
            self.P.tag = "gla_proj"
            for h in range(2):
                self.cp(vpad[:, :, h, h * 64:(h + 1) * 64], v_tm[:, :, vc0 + h * 64:vc0 + (h + 1) * 64], eng="act")
            if ret:
                sv = _slotload(self, [_wcols(self, W, 2816 + gp * 128, 128), _wcols(self, W, 3072 + gp * 128, 128),
                                      _wcols(self, self.w_inp[l], gp * 128, 128), _wcols(self, self.w_inp[l], 256 + gp * 128, 128)])
            else:
                sv = _slotload(self, [_wcols(self, W, g * 128, 128), _wcols(self, W, 512 + g * 128, 128),
                                      _wcols(self, W, 768 + g * 128, 128)])

            def proj(cc, th):
                ps = self.bank()
                for k in range(8):
                    self.mm(ps[:, :], sv[:, k, cc * 128:(cc + 1) * 128], self.hT[:, k, th * 512:(th + 1) * 512], k == 0, k == 7)
                return ps
            if ret:
                for (dst, c_a, c_b, sc) in ((qf, 0, 2, 1.0), (kf, 1, 3, 0.125)):
                    for th in range(2):
                        sl = slice(th * 512, (th + 1) * 512)
                        pa = proj(c_a, th)
                        pb_ = proj(c_b, th)
                        t0 = self.tmpf[self.ntmp % 2]; self.ntmp += 1
                        self.tt(t0[:, 0:512], pa[:, :], self.ropeC[:, sl], ALU.mult)
                        self.tt(t0[:, 512:1024], pb_[:, :], self.ropeS[:, sl], ALU.mult)
                        self.tt(t0[:, 0:512], t0[:, 0:512], t0[:, 512:1024], ALU.add)
                        self.ts(dst[:, sl], t0[:, 0:512], sc, None, ALU.mult)
            else:
                for th in range(2):
                    sl = slice(th * 512, (th + 1) * 512)
                    pa = proj(0, th)
                    self.act(qf[:, sl], pa[:, :], AF.Silu)
            for dr in [int(x) for x in os.environ.get('GLA_DIRS', '0,1').split(',')]:
                self.P.tag = "gla_A"
                q3 = qf.rearrange("p (n j) -> p n j", j=32)
                k3 = kf.rearrange("p (n j) -> p n j", j=32)
                qt3 = qt.rearrange("p (n j) -> p n j", j=32)
                kt3 = kt.rearrange("p (n j) -> p n j", j=32)
                kh3 = kh.rearrange("p (n j) -> p n j", j=32)
                if ret:
                    tab = lambda kind: self.retT[:, gp, dr, kind:kind + 1, :].broadcast_to([128, 32, 32])
                    self.tt(qt3, q3, tab(0), ALU.mult)
                    self.tt(kt3, k3, tab(1), ALU.mult)
                    self.tt(kh3, k3, tab(2), ALU.mult)
                    Fv = self.retF[:, gp, dr, :]
                else:
                    a_idx = dr * 2 + g
                    b3 = bb.rearrange("p (n j) -> p n j", j=32)
                    tot = b3[:, :, 31:32]
                    HS = [slice(0, 512), slice(512, 1024)]
                    n3 = lambda v, th: v[:, HS[th]].rearrange("p (n j) -> p n j", j=32)
                    toth = lambda th: b3[:, th * 16:(th + 1) * 16, 31:32].broadcast_to([128, 16, 32])
                    t0 = self.tmpf[0]
                    lnoml = self.lnoml[:, a_idx, l:l + 1]
                    pzs = [proj(1 + dr, th) for th in range(2)]
                    for th in range(2):
                        self.act(t0[:, HS[th]], pzs[th][:, :], AF.Exp)
                    for th in range(2):
                        self.act(t0[:, HS[th]], t0[:, HS[th]], AF.Ln, bias=self.onesc, scale=1.0)
                    for th in range(2):
                        self.act(kf[:, HS[th]], t0[:, HS[th]], AF.Exp, bias=lnoml, scale=-1.0)
                    for th in range(2):
                        self.act(lg[:, HS[th]], kf[:, HS[th]], AF.Ln, bias=self.onesc, scale=-1.0)
                    for th in range(2):
                        self.scan(bb[:, HS[th]], self.smask[:, HS[th]], lg[:, HS[th]])
                    if dr == 0:
                        bc = bb
                    else:
                        for th in range(2):
                            self.tt(E[:, HS[th]], lg[:, HS[th]], bb[:, HS[th]], ALU.subtract)
                        for th in range(2):
                            self.tt(n3(lg, th), n3(E, th), toth(th), ALU.add)
                        bc = lg
                    for th in range(2):
                        self.act(E[:, HS[th]], bc[:, HS[th]], AF.Exp)
                    for th in range(2):
                        self.act(t0[:, HS[th]], bc[:, HS[th]], AF.Exp, scale=-1.0)
                    for th in range(2):
                        self.tt(qt[:, HS[th]], qf[:, HS[th]], E[:, HS[th]], ALU.mult)
                    for th in range(2):
                        self.tt(kt[:, HS[th]], kf[:, HS[th]], t0[:, HS[th]], ALU.mult)
                    for th in range(2):
                        self.tt(n3(E, th), toth(th), n3(bc, th), ALU.subtract)
                    for th in range(2):
                        self.act(E[:, HS[th]], E[:, HS[th]], AF.Exp)
                    for th in range(2):
                        self.tt(kh[:, HS[th]], kf[:, HS[th]], E[:, HS[th]], ALU.mult)
                    self.act(self.Fv.unsqueeze(2), tot, AF.Exp)
                    Fv = self.Fv
                if int(os.environ.get('GLA_STOP', '99')) <= 1:
                    continue
                self.P.tag = "gla_B"
                Fproc = Fv if dr == 0 else Fv[:, 31::-1]
                self.tt(self.Fg, Fproc, self.gvec[:], ALU.mult)
                self.cp(Fr3[:, :, 1:33], self.Fg.unsqueeze(1).broadcast_to([128, 64, 32]))
                self.cp(KV3[:, :, 0:1], self.s0[:, g, dr, :].unsqueeze(2))
                pt = self.bank()
                ptb = pt[:, :].bitcast(BF16)
                for k in range(8):
                    self.tr(ptb[:, k * 128:(k + 1) * 128], kh[:, k * 128:(k + 1) * 128], self.ident[:])
                self.cp(kh_tm[:, :, :], ptb.rearrange("p (k n) -> p k n", k=8), eng="act")
                if int(os.environ.get('GLA_STOP', '99')) <= 2:
                    continue
                for tile in range(8):
                    vxt = vx[nvx % 2]; nvx += 1
                    v2 = v_tm[:, tile, vc0:vc0 + 128].rearrange("p (h d) -> p h d", h=2)
                    self.tt(vxt.rearrange("p (h c d) -> p h c d", h=2, c=4),
                            v2.unsqueeze(2).broadcast_to([128, 2, 4, 64]),
                            self.chm[:].unsqueeze(1).unsqueeze(3).broadcast_to([128, 2, 4, 64]), ALU.mult)
                    ps = self.bank()
                    self.mm(ps[:, :], kh_tm[:, tile, :], vxt, True, True)
                    for h in range(2):
                        src = ps[h * 64:(h + 1) * 64, h * 256:(h + 1) * 256].rearrange("p (c d) -> p c d", c=4)
                        if dr == 0:
                            dst = KV3[h * 64:(h + 1) * 64, :, 1 + 4 * tile:5 + 4 * tile]
                        else:
                            hi = 32 - 4 * tile
                            dst = KV3[h * 64:(h + 1) * 64, :, hi:hi - 4:-1] if hi - 4 > 0 else KV3[h * 64:(h + 1) * 64, :, hi:0:-1]
                        self.cp(dst.transpose([0, 2, 1]), src, eng="act")
                if int(os.environ.get('GLA_STOP', '99')) <= 3:
                    continue
                self.P.tag = "gla_C"
                self.scan(KV, Fr, KV)
                if int(os.environ.get('GLA_STOP', '99')) <= 4:
                    continue
                srcf = KV3[:, :, 8:33:8] if dr == 0 else KV3[:, :, 32:0:-8]
                self.cp(Sfin, srcf.transpose([0, 2, 1]))
                od = (self.o_str if ret else self.o_sth)
                hh = (gp if ret else g) * 2
                self.dma("sp", od[:, l, dr, hh:hh + 2, :, :].rearrange("b h k v -> (h k) b v"), Sfin)
                if int(os.environ.get('GLA_STOP', '99')) <= 5:
                    continue
                for h in range(2):
                    s_ = KV3[h * 64:(h + 1) * 64, :, 0:32] if dr == 0 else KV3[h * 64:(h + 1) * 64, :, 31::-1]
                    self.cp(SP[h * 64:(h + 1) * 64, :, h * 64:(h + 1) * 64], s_.transpose([0, 2, 1]), eng=("act" if h == 0 else "dve"))
                if int(os.environ.get('GLA_STOP', '99')) <= 6:
                    continue
                for h in range(2):
                    hp = slice(h * 64, (h + 1) * 64)
                    if dr == 0:
                        gsrc = KV3[hp, :, 8:32:8]; gdst = SP[hp, 8:32:8, hp]
                    else:
                        gsrc = KV3[hp, :, 24:0:-8]; gdst = SP[hp, 7:24:8, hp]
                    self.ts(gdst, gsrc.transpose([0, 2, 1]), self.cont[hp, :], None, ALU.mult)
                self.P.tag = "gla_D"
                qxa = self.rstd[:, :].bitcast(BF16).rearrange("p (k h n) -> p k h n", k=8, h=2)
                Aall = self.tmpf[1][:, :].bitcast(BF16).rearrange("p (k h n) -> p k h n", k=8, h=2)
                for h in range(2):
                    hp = slice(h * 64, (h + 1) * 64)
                    oth = slice((1 - h) * 64, (2 - h) * 64)
                    self.cp(qxa[hp, :, h, :], qt[hp, :].rearrange("p (k n) -> p k n", k=8), eng="act")
                    self.memset(qxa[oth, :, h, :], 0.0)
                for bp in range(4):
                    pa = self.bank()
                    for t2 in range(2):
                        tile = 2 * bp + t2
                        cs = slice(tile * 128, (tile + 1) * 128)
                        self.mm(pa[:, t2 * 256:(t2 + 1) * 256], kt[:, cs], qxa[:, tile, :, :], True, True)
                    self.tt(Aall[:, 2 * bp:2 * bp + 2, :, :], pa[:, :].rearrange("p (k h n) -> p k h n", k=2, h=2),
                            self.cm2[:, dr, :, :, :], ALU.mult)
                for tile in range(8):
                    cs = slice(tile * 128, (tile + 1) * 128)
                    po = self.bank()
                    self.mm(po[:, 0:128], vpad[:, tile, 0, :], Aall[:, tile, 0, :], True, False)
                    self.mm(po[:, 0:128], vpad[:, tile, 1, :], Aall[:, tile, 1, :], False, False)
                    for c in range(4):
                        n = 4 * tile + c
                        self.mm(po[:, 32 * c:32 * c + 32], SP[:, n, :], qt[:, tile * 128 + 32 * c:tile * 128 + 32 * c + 32], False, c == 3)
                    if dr == 0:
                        self.cp(osb[:, cs], po[:, 0:128], eng="act")
                    else:
                        self.tt(osb[:, cs], osb[:, cs], po[:, 0:128], ALU.add)
            self.P.tag = "gla_epi"
            gcol = (3584 + gp * 128) if ret else (1024 + g * 128)
            sv = _slotload(self, [_wcols(self, W, gcol, 128)])
            sq = qt
            self.act(sq, osb, AF.Square)
            nw = self.onesc if ret else self.hnw[:, l:l + 1]
            mc = (6 + gp) if ret else g
            for th in range(2):
                sl = slice(th * 512, (th + 1) * 512)
                ps = self.bank()
                self.mm(ps[:, :], self.bones[:], sq[:, sl], True, True)
                rs = E[:, sl]
                self.act(rs, ps[:, :], AF.Ln, bias=self.epsc, scale=1.0 / 64.0)
                self.act(rs, rs, AF.Exp, scale=-0.5)
                pg = proj(0, th)
                t0 = self.tmpf[self.ntmp % 2]; self.ntmp += 1
                self.act(t0[:, 0:512], pg[:, :], AF.Silu)
                self.tt(t0[:, 512:1024], osb[:, sl], rs, ALU.mult)
                self.stt(mixT[:, mc, sl], t0[:, 512:1024], nw, t0[:, 0:512], ALU.mult, ALU.mult)

    self.P.tag = "wout"
    wo = self.w_out[l].rearrange("(k p) n -> p k n", p=128)
    for dh in range(2):
        sv = _slotload(self, [(wo[:, :, dh * 512:(dh + 1) * 512], 512, "pool")])
        for dcc in range(4):
            dc = dh * 4 + dcc
            for th in range(2):
                ps = self.bank()
                for k in range(8):
                    self.mm(ps[:, :], sv[:, k, dcc * 128:(dcc + 1) * 128], mixT[:, k, th * 512:(th + 1) * 512], k == 0, k == 7)
                xs = self.xT[:, dc, th * 512:(th + 1) * 512]
                self.stt(xs, ps[:, :], self.gates[:, 1, dc:dc + 1], xs, ALU.mult, ALU.add)


KB.mixer_decl = _mixer_decl
KB.mixer_init = _mixer_init
KB.mixer = _mixer

def _consts(kind):
    bf = ml_dtypes.bfloat16
    Ls = 1024 if kind == "s" else 256
    c = {}
    t = np.arange(T)
    tl = t % Ls
    cont = 1.0 if kind == "s" else 0.0
    c["cont"] = np.full((128, 1), cont, np.float32)
    d = np.arange(64)
    fr = 1.0 / (10000.0 ** (np.arange(0, 32, 2, dtype=np.float64) / 32.0))
    fidx = d % 16
    pos = np.where(d[:, None] < 32, (t // 64)[None, :], (t % 64)[None, :]).astype(np.float64)
    ang = pos * fr[fidx][:, None]
    sgn = np.where((d % 32) < 16, -1.0, 1.0)[:, None]
    if kind == "s":
        C = np.cos(ang); S = sgn * np.sin(ang)
    else:
        C = np.ones((64, T)); S = np.zeros((64, T))
    c["ropeC"] = np.tile(C, (2, 1)).astype(np.float32)
    c["ropeS"] = np.tile(S, (2, 1)).astype(np.float32)
    c["smask"] = np.tile((t % 32 != 0).astype(np.float32)[None], (128, 1))
    s_ = np.arange(128)[:, None]; t_ = np.arange(128)[None, :]
    same = (s_ // 32) == (t_ // 32)
    cm = np.stack([(same & (s_ <= t_)), (same & (s_ >= t_))], axis=1).astype(np.float32)
    c["cm"] = np.ascontiguousarray(np.concatenate([cm, cm, cm, cm], axis=2))
    c["chm"] = ((np.arange(128)[:, None] // 32) == np.arange(4)[None, :]).astype(np.float32)
    gv = np.ones((128, 32), np.float32); gv[:, [8, 16, 24]] = cont
    c["gvec"] = gv
    c["ident"] = np.eye(128, dtype=np.float32).astype(bf)
    bo = np.zeros((128, 128), np.float32); bo[:64, :64] = 1; bo[64:, 64:] = 1
    c["bones"] = bo.astype(bf)
    lg_all = np.log1p(-np.exp2(-5.0 - 0.5 * np.arange(8, dtype=np.float64)))
    retT = np.zeros((128, 2, 2, 3, 32)); retF = np.zeros((128, 2, 2, 32))
    j = np.arange(32)
    for p in range(128):
        for gp in range(2):
            hh = gp * 2 + p // 64
            for dr in range(2):
                lg = lg_all[2 * hh + dr]
                if dr == 0:
                    retT[p, gp, dr, 0] = np.exp(lg * (j + 1)); retT[p, gp, dr, 1] = np.exp(-lg * (j + 1)); retT[p, gp, dr, 2] = np.exp(lg * (31 - j))
                else:
                    retT[p, gp, dr, 0] = np.exp(lg * (32 - j)); retT[p, gp, dr, 1] = np.exp(-lg * (32 - j)); retT[p, gp, dr, 2] = np.exp(lg * j)
                retF[p, gp, dr, :] = np.exp(32 * lg)
    c["retT"] = retT.astype(np.float32); c["retF"] = retF.astype(np.float32)
    tn = tl.astype(np.float64) / Ls
    bands = np.arange(1, 17, dtype=np.float64)
    a2 = 2.0 * np.pi * tn[:, None] * bands[None]
    z = np.concatenate([tn[:, None], np.cos(a2), np.sin(a2)], axis=-1)
    c["zT"] = np.ascontiguousarray(z.T).astype(np.float32)
    MIN_DECAY = math.log(1e-2) / 1.5; MAX_DECAY = math.log(1e-2) / 0.3
    deltas = np.abs(np.linspace(MIN_DECAY, MAX_DECAY, 512, dtype=np.float32)).astype(np.float32)
    c["deltab"] = np.tile(deltas[None], (128, 1)).astype(np.float32)
    tt_ = (np.arange(8)[None, :] * 128 + np.arange(128)[:, None])
    c["negtl"] = (-((tt_ % Ls).astype(np.float64) / Ls)).astype(np.float32)
    c["m0"] = ((tt_ % Ls) != 0).astype(np.float32)
    th = np.zeros((T, T)); blk = (t[:, None] // Ls) == (t[None, :] // Ls)
    th = np.pi * (2 * tl[None, :] + 1) * tl[:, None] / (2.0 * Ls)
    Cm = np.where(blk, np.cos(th), 0.0); Sm = np.where(blk, -np.sin(th), 0.0)
    c["Cf"] = Cm.astype(np.float32).astype(bf); c["Sf"] = Sm.astype(np.float32).astype(bf)
    c["CiT"] = np.ascontiguousarray((Cm / Ls).T).astype(np.float32).astype(bf)
    c["SiT"] = np.ascontiguousarray((Sm / Ls).T).astype(np.float32).astype(bf)
    return c


def _prep_mixer_common(inputs, L):
    f = lambda a: np.ascontiguousarray(np.asarray(a, dtype=np.float32))
    com = {}
    w_in = np.asarray(inputs["w_in"], dtype=np.float32)[:L]
    com["w_in"] = f(w_in)
    com["w_out"] = f(inputs["w_out"])[:L]
    d = np.arange(64)
    perm = np.where((d % 32) < 16, d + 16, d - 16)
    colperm = (np.arange(4)[:, None] * 64 + perm[None, :]).reshape(-1)
    com["w_inp"] = f(np.concatenate([w_in[:, :, 2816 + colperm], w_in[:, :, 3072 + colperm]], axis=-1))
    lb = np.asarray(inputs["hgrn_lb"], dtype=np.float32)
    com["hgrn_lbT"] = f(lb.reshape(2, 4, 2, 128).transpose(3, 0, 2, 1).reshape(128, 4, 4))
    hn = np.asarray(inputs["hgrn_norm_w"], dtype=np.float32)[:L]
    com["hnwT"] = f(np.tile(hn.T, (2, 1)))
    hc = np.asarray(inputs["hyena_conv"], dtype=np.float32)[:L]
    com["convT"] = f(hc.reshape(L, 3, 12, 128).transpose(3, 0, 1, 2))
    hb = np.asarray(inputs["hyena_bias"], dtype=np.float32)[:L]
    com["hbiasT"] = f(hb.reshape(L, 4, 128).transpose(2, 0, 1))
    com["hy_w1"] = f(inputs["hyena_w1"])[:L]; com["hy_w2"] = f(inputs["hyena_w2"])[:L]; com["hy_w3"] = f(inputs["hyena_w3"])[:L]
    fb = np.zeros((64, L, 5), np.float32)
    fb[:, :, 0] = np.asarray(inputs["hyena_b1"])[:L].T; fb[:, :, 1] = np.asarray(inputs["hyena_b2"])[:L].T
    fb[:, :, 2] = np.asarray(inputs["hyena_freq"])[:L].T
    com["hy_fb"] = fb
    return com


def _s0_for(inputs, kind, i, L):
    s0 = np.zeros((128, 4, L, 2, 64), np.float32)
    if kind == "s":
        sh = np.asarray(inputs["state_hgrn"], dtype=np.float32)[i][:L]
        sr = np.asarray(inputs["state_ret"], dtype=np.float32)[i][:L]
        for g in range(2):
            s0[:, g] = sh[:, :, 2 * g:2 * g + 2].transpose(2, 3, 0, 1, 4).reshape(128, L, 2, 64)
            s0[:, 2 + g] = sr[:, :, 2 * g:2 * g + 2].transpose(2, 3, 0, 1, 4).reshape(128, L, 2, 64)
    return s0


_KB_CACHE = {}


def run_all(inputs, L=4):
    if L not in _KB_CACHE:
        kb = KB(L, do_mixer=True)
        kb.build()
        _KB_CACHE[L] = kb
    kb = _KB_CACHE[L]
    com = _prep_common(inputs, L)
    com.update(_prep_mixer_common(inputs, L))
    cst = {"s": _consts("s"), "p": _consts("p")}
    maps = []
    for kind, i in core_assign():
        if kind == "s":
            x = np.asarray(inputs["x_sample"], dtype=np.float32)[i]
            cond = np.asarray(inputs["c"], dtype=np.float32)[i]
        else:
            x = np.asarray(inputs["x_prompt"], dtype=np.float32)[4 * i:4 * i + 4].reshape(1024, 1024)
            cond = np.asarray(inputs["c_ctx"], dtype=np.float32)
        m = dict(com)
        m.update(cst[kind])
        m["xT"] = np.ascontiguousarray(x.T)
        m["cond"] = np.ascontiguousarray(cond.reshape(8, 128).T)
        m["s0"] = _s0_for(inputs, kind, i, L)
        maps.append(m)
    res = run_bass_kernel_spmd(kb.nc, maps, core_ids=list(range(8)))
    R_ = res.results
    y_s = np.stack([np.ascontiguousarray(R_[i]["yT"].T) for i in range(2)], axis=0)
    y_p = np.concatenate([np.ascontiguousarray(R_[2 + g]["yT"].T).reshape(4, 256, 1024) for g in range(4)], axis=0)
    sth = np.concatenate([R_[2 + g]["sth"] for g in range(4)], axis=0)
    str_ = np.concatenate([R_[2 + g]["str"] for g in range(4)], axis=0)
    return (y_p.astype(np.float32), y_s.astype(np.float32), sth.astype(np.float32), str_.astype(np.float32))


def kernel(**inputs):
    return run_all(inputs, 4)
```

```python
import concourse.bass as bass
import concourse.mybir as mybir

import os
ANNOTATE = bool(os.environ.get("KANNOT"))
ENGS = ("pe", "act", "dve", "pool", "sp")
DT_SIZE = {"dt.float32": 4, "dt.bfloat16": 2, "dt.int32": 4, "dt.uint32": 4, "dt.float16": 2, "dt.uint8": 1, "dt.int8": 1, "dt.uint16": 2, "dt.int16": 2}


class Op:
    __slots__ = ("id", "eng", "fn", "deps", "is_dma", "signal", "count", "dsem", "dval", "is_mm", "tag")

    def __init__(self, id, eng, fn, is_dma, is_mm):
        self.id = id
        self.eng = eng
        self.fn = fn
        self.deps = set()
        self.is_dma = is_dma
        self.is_mm = is_mm
        self.signal = False
        self.count = 0
        self.dsem = None
        self.dval = 0


def footprint(ap):
    sp = str(ap.space)
    if "DRAM" in sp.upper():
        return None
    t = ap.tensor
    shp = list(t.shape)
    F = 1
    for s in shp[1:]:
        F *= s
    off = int(ap.offset)
    pairs = ap.ap
    esz = DT_SIZE[str(ap.dtype)]
    p0 = off // F
    lo = off % F
    pstep, pcnt = pairs[0]
    if pstep == F or pcnt == 1:
        p1 = p0 + pcnt
        rest = pairs[1:]
    else:
        p1 = p0 + 1
        rest = pairs
    ext = 0
    for st, cn in rest:
        ext += abs(st) * (cn - 1)
    hi = lo + ext + 1
    if 'PSUM' in sp.upper():
        return (ap.tensor.name, (p0 // 32) * 32, ((p1 + 31) // 32) * 32, 0, 1 << 20)
    return (ap.tensor.name, p0, p1, lo * esz, hi * esz)


class Prog:
    def __init__(self, nc, same_engine_sync=True, ndma_sems=8):
        self.nc = nc
        self.ops = []
        self.recs = {}
        self.same_engine_sync = same_engine_sync
        self.ndma = ndma_sems

    def add(self, eng, fn, reads=(), writes=(), is_dma=False, is_mm=False):
        op = Op(len(self.ops), eng, fn, is_dma, is_mm)
        op.tag = getattr(self, 'tag', '')
        self.ops.append(op)
        for ap in reads:
            fp = footprint(ap)
            if fp is None:
                continue
            self._access(op, fp, False)
        for ap in writes:
            fp = footprint(ap)
            if fp is None:
                continue
            self._access(op, fp, True)
        return op

    def _access(self, op, fp, is_write):
        name, p0, p1, lo, hi = fp
        lst = self.recs.setdefault(name, [])
        keep = []
        for r in lst:
            ov = not (r[1] <= p0 or p1 <= r[0] or r[3] <= lo or hi <= r[2])
            if ov and r[4] != op.id:
                if is_write or r[5]:
                    op.deps.add(r[4])
                if is_write and r[0] >= p0 and r[1] <= p1 and r[2] >= lo and r[3] <= hi:
                    continue
            keep.append(r)
        keep.append([p0, p1, lo, hi, op.id, is_write])
        self.recs[name] = keep

    def finalize(self):
        nc = self.nc
        ops = self.ops
        for op in ops:
            best = {}
            for d in list(op.deps):
                o = ops[d]
                if o.is_dma:
                    continue
                if o.eng not in best or d > best[o.eng]:
                    best[o.eng] = d
            for d in list(op.deps):
                o = ops[d]
                if not o.is_dma and best[o.eng] != d:
                    op.deps.discard(d)
        for op in ops:
            for d in list(op.deps):
                o = ops[d]
                if o.is_dma:
                    continue
                if o.eng == op.eng and not op.is_dma:
                    if o.eng == "pe" or not self.same_engine_sync:
                        op.deps.discard(d)
                        continue
                o.signal = True
        cnt = {e: 0 for e in ENGS}
        dcnt = {e: 0 for e in ENGS}
        for op in ops:
            if op.is_dma:
                i = dcnt[op.eng]
                dcnt[op.eng] += 1
                op.dsem = (op.eng, i % self.ndma)
                op.dval = 16 * (i // self.ndma + 1)
            elif op.signal:
                cnt[op.eng] += 1
                op.count = cnt[op.eng]
        self.cnt = cnt
        import contextlib
        stack = contextlib.ExitStack()
        self.stack = stack
        esem = {e: stack.enter_context(nc.semaphore("c_" + e)) for e in ENGS}
        dsem = {}
        for e in ENGS:
            if dcnt[e]:
                for j in range(self.ndma):
                    dsem[(e, j)] = stack.enter_context(nc.semaphore("d_%s%d" % (e, j)))
        byeng = {e: [o for o in ops if o.eng == e] for e in ENGS}
        engobj = {"pe": "tensor", "act": "scalar", "dve": "vector", "pool": "gpsimd", "sp": "sync"}
        last_dma = {}

        def run_engine(e, eng):
            waited = {}
            lst = byeng[e]
            for op in lst:
                need = {}
                for d in op.deps:
                    o = ops[d]
                    if o.is_dma:
                        key = ("d",) + o.dsem
                        v = o.dval
                    else:
                        key = ("c", o.eng)
                        v = o.count
                    if v > need.get(key, 0):
                        need[key] = v
                if op.is_dma:
                    if op.dval > 16:
                        key = ("d",) + op.dsem
                        v = op.dval - 16
                        if v > need.get(key, 0):
                            need[key] = v
                for key, v in need.items():
                    if waited.get(key, 0) >= v:
                        continue
                    waited[key] = v
                    sem = dsem[key[1:]] if key[0] == "d" else esem[key[1]]
                    eng.wait_ge(sem, v)
                ins = op.fn(eng)
                if ANNOTATE and op.tag:
                    ins.annotate(op.tag)
                if op.is_dma:
                    ins.then_inc(dsem[op.dsem], 16)
                elif op.signal:
                    ins.then_inc(esem[op.eng], 1)
            for j in range(self.ndma):
                if (e, j) in dsem:
                    n = (dcnt[e] - 1 - j) // self.ndma + 1 if dcnt[e] > j else 0
                    if n > 0:
                        eng.wait_ge(dsem[(e, j)], 16 * n)

        with nc.Block() as block:
            @block.tensor
            def _(eng):
                run_engine("pe", eng)

            @block.scalar
            def _(eng):
                run_engine("act", eng)

            @block.vector
            def _(eng):
                run_engine("dve", eng)

            @block.gpsimd
            def _(eng):
                run_engine("pool", eng)

            @block.sync
            def _(eng):
                run_engine("sp", eng)
        stack.close()

import math
import numpy as np
import ml_dtypes
from concourse.bass_utils import run_bass_kernel_spmd

F32 = mybir.dt.float32
BF16 = mybir.dt.bfloat16
AF = mybir.ActivationFunctionType
ALU = mybir.AluOpType
AX = mybir.AxisListType

D = 1024
T = 1024
DFF = 2816
NF = 22
EPS = 1e-6


class KB:
    def __init__(self, L, do_mixer=True):
        self.L = L
        self.do_mixer = do_mixer
        nc = bass.Bass("TRN2", target_bir_lowering=False)
        self.nc = nc
        self.P = Prog(nc)
        self.din = {}
        self.dout = {}
        self.nbank = 0

    def inp(self, name, shape, dt=F32):
        self.din[name] = self.nc.dram_tensor(name, list(shape), dt, kind="ExternalInput").ap()
        return self.din[name]

    def outp(self, name, shape, dt=F32):
        self.dout[name] = self.nc.dram_tensor(name, list(shape), dt, kind="ExternalOutput").ap()
        return self.dout[name]

    def mm(self, out, lhsT, rhs, start, stop):
        self.P.add("pe", lambda e: e.matmul(out, lhsT=lhsT, rhs=rhs, start=start, stop=stop),
                   reads=[lhsT, rhs], writes=[out], is_mm=True)

    def tr(self, out, in_, ident):
        self.P.add("pe", lambda e: e.transpose(out, in_, ident), reads=[in_, ident], writes=[out], is_mm=True)

    def act(self, out, in_, func, bias=None, scale=None, eng="act"):
        kw = {}
        rd = [in_]
        if bias is not None:
            kw["bias"] = bias
            if not isinstance(bias, (int, float)):
                rd.append(bias)
        if scale is not None:
            kw["scale"] = scale
            if not isinstance(scale, (int, float)):
                rd.append(scale)
        self.P.add(eng, lambda e: e.activation(out=out, in_=in_, func=func, **kw), reads=rd, writes=[out])

    def tt(self, out, in0, in1, op, eng="dve"):
        self.P.add(eng, lambda e: e.tensor_tensor(out=out, in0=in0, in1=in1, op=op), reads=[in0, in1], writes=[out])

    def ts(self, out, in0, s1, s2, op0, op1=None, eng="dve"):
        rd = [in0]
        for s in (s1, s2):
            if s is not None and not isinstance(s, (int, float)):
                rd.append(s)
        if op1 is None:
            self.P.add(eng, lambda e: e.tensor_scalar(out=out, in0=in0, scalar1=s1, scalar2=None, op0=op0), reads=rd, writes=[out])
        else:
            self.P.add(eng, lambda e: e.tensor_scalar(out=out, in0=in0, scalar1=s1, scalar2=s2, op0=op0, op1=op1), reads=rd, writes=[out])

    def stt(self, out, in0, scalar, in1, op0, op1, eng="dve"):
        rd = [in0, in1]
        if not isinstance(scalar, (int, float)):
            rd.append(scalar)
        self.P.add(eng, lambda e: e.scalar_tensor_tensor(out=out, in0=in0, scalar=scalar, in1=in1, op0=op0, op1=op1),
                   reads=rd, writes=[out])

    def cp(self, out, in_, eng="dve"):
        if eng == "act":
            self.P.add(eng, lambda e: e.copy(out=out, in_=in_), reads=[in_], writes=[out])
        else:
            self.P.add(eng, lambda e: e.tensor_copy(out=out, in_=in_), reads=[in_], writes=[out])

    def memset(self, out, v, eng="dve"):
        self.P.add(eng, lambda e: e.memset(out, v), writes=[out])

    def scan(self, out, d0, d1, init=0.0):
        self.P.add("dve", lambda e: e.tensor_tensor_scan(out=out, data0=d0, data1=d1, initial=init, op0=ALU.mult, op1=ALU.add),
                   reads=[d0, d1], writes=[out])

    def dma(self, q, out, in_):
        self.P.add(q, lambda e: e.dma_start(out=out, in_=in_), reads=[in_], writes=[out], is_dma=True)

    def bank(self):
        b = self.banks[self.nbank % 8]
        self.nbank += 1
        return b

    def build(self):
        nc = self.nc
        L = self.L
        xT_in = self.inp("xT", [D, T])
        cond_in = self.inp("cond", [128, 8])
        w_ada = self.inp("w_ada", [L, D, 9 * D])
        b_adaT = self.inp("b_adaT", [128, L, 72])
        norm_wT = self.inp("norm_wT", [128, L, 3, 8])
        fin_wT = self.inp("fin_wT", [128, 8])
        ffn_in = self.inp("ffn_in", [L, 2, D, 2 * DFF])
        ffn_out = self.inp("ffn_out", [L, 2, DFF, D])
        yT_out = self.outp("yT", [D, T])
        if self.do_mixer:
            self.mixer_decl()

        self.xT = nc.alloc_sbuf_tensor("xTs", [128, 8, T], F32)
        self.hT = nc.alloc_sbuf_tensor("hTs", [128, 8, T], BF16)
        self.slots = [nc.alloc_sbuf_tensor("slot%d" % i, [128, 4096], BF16) for i in range(3)]
        self.nslot = 0
        self.AR = nc.alloc_sbuf_tensor("arena", [128, 22 * 1024], F32)
        self.small = nc.alloc_sbuf_tensor("small", [128, 548], F32)
        self.rstd = nc.alloc_sbuf_tensor("rstd", [128, T], F32)
        self.tmpf = [nc.alloc_sbuf_tensor("tmpf%d" % i, [128, T], F32) for i in range(2)]
        self.ones = nc.alloc_sbuf_tensor("ones", [128, 128], BF16)
        self.banks = [nc.alloc_psum_tensor("bank%d" % i, [128, 512], F32) for i in range(8)]
        self.ntmp = 0
        sm = self.small
        self.cond = sm[:, 0:8]
        self.scond = sm[:, 8:16]
        self.modT = sm[:, 16:88]
        self.badaT = sm[:, 88:88 + 72 * L].rearrange("p (l j) -> p l j", l=L)
        o = 88 + 72 * 4
        self.normw = sm[:, o:o + 24 * L].rearrange("p (l i c) -> p l i c", l=L, i=3)
        o += 24 * 4
        self.finw = sm[:, o:o + 8]
        o += 8
        self.Ascale = sm[:, o:o + 24].rearrange("p (i c) -> p i c", i=3)
        o += 24
        self.gates = sm[:, o:o + 24].rearrange("p (i c) -> p i c", i=3)
        o += 24
        self.scondb = nc.alloc_sbuf_tensor("scondb", [128, 8], BF16)
        self.modT2 = nc.alloc_sbuf_tensor("modT2", [128, 72], F32)
        self.modTs = [self.modT, self.modT2[:, :]]
        self.eps1k = sm[:, o:o + 1]
        o += 1
        self.small_o = o

        self.dma("sp", self.xT[:], xT_in.rearrange("(c p) t -> p c t", p=128))
        self.dma("sp", self.cond, cond_in)
        self.dma("sp", self.badaT, b_adaT)
        self.dma("sp", self.normw, norm_wT)
        self.dma("sp", self.finw, fin_wT)
        self.memset(self.ones[:], 1.0)
        self.memset(self.eps1k, D * EPS)
        if self.do_mixer:
            self.mixer_init()
        self.act(self.scond, self.cond, AF.Silu)
        self.cp(self.scondb[:], self.scond)

        for _ in self.ada_steps(0, w_ada):
            pass
        for l in range(L):
            self.ada_finish(l)
            self.norm(0, self.normw[:, l, 0, :])
            self.ffn(ffn_in[l, 0], ffn_out[l, 0], self.gates[:, 0, :])
            if self.do_mixer:
                self.norm(1, self.normw[:, l, 1, :])
                self.mixer(l)
            self.norm(2, self.normw[:, l, 2, :])
            nxt = self.ada_steps(l + 1, w_ada) if l + 1 < L else None
            self.ffn(ffn_in[l, 1], ffn_out[l, 1], self.gates[:, 2, :], extra=nxt)
        self.rms_stats()
        fa = sm[:, self.small_o:self.small_o + 8]
        self.ts(fa, self.finw, 32.0, None, ALU.mult)
        yv = self.AR[:, 0:8 * T].rearrange("p (c t) -> p c t", c=8)
        for c in range(8):
            self.stt(yv[:, c, :], self.xT[:, c, :], fa[:, c:c + 1], self.rstd[:], ALU.mult, ALU.mult)
        self.dma("sp", yT_out.rearrange("(c p) t -> p c t", p=128), yv)
        self.P.finalize()

    def next_slot(self):
        s = self.slots[self.nslot % len(self.slots)]
        self.nslot += 1
        return s

    def ada_steps(self, l, w_ada):
        modT = self.modTs[l % 2]
        for j4 in range(18):
            ptag = self.P.tag if hasattr(self.P, "tag") else ""
            self.P.tag = "ada"
            slot = self.next_slot()
            sv = slot[:, :].rearrange("p (k n) -> p k n", k=8)
            self.dma("pool", sv, w_ada[l].rearrange("(k p) n -> p k n", p=128)[:, :, j4 * 512:(j4 + 1) * 512])
            psm = self.bank()
            for jj in range(4):
                for k in range(8):
                    self.mm(psm[:, jj:jj + 1], sv[:, k, jj * 128:(jj + 1) * 128], self.scondb[:, k:k + 1], k == 0, k == 7)
            self.tt(modT[:, j4 * 4:(j4 + 1) * 4], psm[:, 0:4], self.badaT[:, l, j4 * 4:(j4 + 1) * 4], ALU.add)
            self.P.tag = ptag
            yield

    def ada_finish(self, l):
        mod = self.modTs[l % 2].rearrange("p (m c) -> p m c", m=9)
        for i in range(3):
            self.stt(self.Ascale[:, i, :], mod[:, 3 * i + 1, :], 1.0, self.normw[:, l, i, :], ALU.add, ALU.mult)
            self.ts(self.Ascale[:, i, :], self.Ascale[:, i, :], 32.0, None, ALU.mult)
        self.ts(self.gates[:, 0, :], mod[:, 2, :], 0.5, None, ALU.mult)
        self.cp(self.gates[:, 1, :], mod[:, 5, :])
        self.ts(self.gates[:, 2, :], mod[:, 8, :], 0.5, None, ALU.mult)
        self.mod = mod

    def rms_stats(self):
        self.P.tag = "norm"
        sq = self.hT
        for c in range(8):
            if c % 2 == 0:
                self.act(sq[:, c, :], self.xT[:, c, :], AF.Square)
            else:
                self.tt(sq[:, c, :], self.xT[:, c, :], self.xT[:, c, :], ALU.mult)
        for th in range(2):
            ps = self.bank()
            for c in range(8):
                self.mm(ps[:, :], self.ones[:], sq[:, c, th * 512:(th + 1) * 512], c == 0, c == 7)
            rs = self.rstd[:, th * 512:(th + 1) * 512]
            self.act(rs, ps[:, :], AF.Ln, bias=self.eps1k, scale=1.0)
            self.act(rs, rs, AF.Exp, scale=-0.5)

    def norm(self, i, nw):
        self.rms_stats()
        for c in range(8):
            tmp = self.tmpf[self.ntmp % 2]
            self.ntmp += 1
            self.stt(tmp[:], self.xT[:, c, :], self.Ascale[:, i, c:c + 1], self.rstd[:], ALU.mult, ALU.mult)
            self.act(self.hT[:, c, :], tmp[:], AF.Identity, bias=self.mod[:, 3 * i, c:c + 1], scale=1.0)

    def ffn(self, w_in, w_out, gate, extra=None):
        self.P.tag = "ffn_in"
        actT = self.AR[:, 0:NF * 512].bitcast(BF16).rearrange("p (f t) -> p f t", f=NF)
        wv = w_in.rearrange("(k p) (b j c) -> p k b j c", p=128, b=2, j=11)
        for j in range(11):
            slot = self.next_slot()
            sv = slot[:, :].rearrange("p (k b c) -> p k b c", k=8, b=2)
            for b in range(2):
                self.dma("pool", sv[:, :, b, :], wv[:, :, b, j, :])
            for fc in range(2):
                f = 2 * j + fc
                pg = [self.bank(), self.bank()]
                pu = [self.bank(), self.bank()]
                for k in range(8):
                    for tt_ in range(2):
                        self.mm(pg[tt_][:, :], sv[:, k, 0, fc * 128:(fc + 1) * 128], self.hT[:, k, tt_ * 512:(tt_ + 1) * 512], k == 0, k == 7)
                for k in range(8):
                    for tt_ in range(2):
                        self.mm(pu[tt_][:, :], sv[:, k, 1, fc * 128:(fc + 1) * 128], self.hT[:, k, tt_ * 512:(tt_ + 1) * 512], k == 0, k == 7)
                for tt_ in range(2):
                    tmp = self.tmpf[self.ntmp % 2]
                    self.ntmp += 1
                    self.act(tmp[:, 0:512], pg[tt_][:, :], AF.Silu)
                    self.tt(actT[:, f, tt_ * 512:(tt_ + 1) * 512], tmp[:, 0:512], pu[tt_][:, :], ALU.mult)
            if extra is not None:
                next(extra, None)
        self.P.tag = "ffn_out"
        wo = w_out.rearrange("(f p) d -> p f d", p=128)
        for dc in range(8):
            slot = self.next_slot()
            sv = slot[:, 0:NF * 128].rearrange("p (f d) -> p f d", f=NF)
            self.dma("pool", sv, wo[:, :, dc * 128:(dc + 1) * 128])
            po = [self.bank(), self.bank()]
            for f in range(NF):
                for tt_ in range(2):
                    self.mm(po[tt_][:, :], sv[:, f, :], actT[:, f, tt_ * 512:(tt_ + 1) * 512], f == 0, f == NF - 1)
            for tt_ in range(2):
                xs = self.xT[:, dc, tt_ * 512:(tt_ + 1) * 512]
                self.stt(xs, po[tt_][:, :], gate[:, dc:dc + 1], xs, ALU.mult, ALU.add)
            if extra is not None:
                next(extra, None)
        if extra is not None:
            for _ in extra:
                pass


def _prep_common(inputs, L):
    f = lambda a: np.ascontiguousarray(np.asarray(a, dtype=np.float32))
    com = {}
    com["w_ada"] = f(inputs["w_ada"])[:L]
    com["b_adaT"] = f(np.asarray(inputs["b_ada"])[:L].reshape(L, 72, 128).transpose(2, 0, 1))
    com["norm_wT"] = f(np.asarray(inputs["norm_w"])[:L].reshape(L, 3, 8, 128).transpose(3, 0, 1, 2))
    com["fin_wT"] = f(np.asarray(inputs["final_norm_w"]).reshape(8, 128).T)
    com["ffn_in"] = f(inputs["ffn_in"])[:L]
    com["ffn_out"] = f(inputs["ffn_out"])[:L]
    return com


def core_assign():
    return [("s", 0), ("s", 1), ("p", 0), ("p", 1), ("p", 2), ("p", 3), ("p", 3), ("p", 3)]

AW = 22 * 1024


def _mixer_decl(self):
    L = self.L
    i = self.inp
    self.w_in = i("w_in", [L, D, 3840])
    self.w_out = i("w_out", [L, D, D])
    self.w_inp = i("w_inp", [L, D, 512])
    self.d_cont = i("cont", [128, 1])
    self.d_ropeC = i("ropeC", [128, T])
    self.d_ropeS = i("ropeS", [128, T])
    self.d_smask = i("smask", [128, T])
    self.d_cm = i("cm", [128, 2, 512], BF16)
    self.d_cmr = i("cmr", [128, 2, 512], BF16)
    self.d_rfa = i("rfa", [128, 2, 2, 3, 4])
    self.d_rF8 = i("rF8", [128, 2, 2, 8])
    self.d_gvec8 = i("gvec8", [128, 8])
    self.d_chm = i("chm", [128, 4])
    self.d_gvec = i("gvec", [128, 32])
    self.d_ident = i("ident", [128, 128], BF16)
    self.d_bones = i("bones", [128, 128], BF16)
    self.d_retT = i("retT", [128, 2, 2, 3, 32])
    self.d_retF = i("retF", [128, 2, 2, 32])
    self.d_lb = i("hgrn_lbT", [128, 4, 4])
    self.d_hnw = i("hnwT", [128, L])
    self.d_conv = i("convT", [128, L, 3, 12])
    self.d_hbias = i("hbiasT", [128, L, 4])
    self.d_zT = i("zT", [33, T])
    self.d_w1 = i("hy_w1", [L, 33, 64])
    self.d_w2 = i("hy_w2", [L, 64, 64])
    self.d_w3 = i("hy_w3", [L, 64, 1024])
    self.d_fb = i("hy_fb", [64, L, 5])
    self.d_delta = i("deltab", [128, 512])
    self.d_negtl = i("negtl", [128, 8])
    self.d_m0 = i("m0", [128, 8])
    self.d_Cf = i("Cf", [T, T], BF16)
    self.d_Sf = i("Sf", [T, T], BF16)
    self.d_Ci = i("CiT", [T, T], BF16)
    self.d_Si = i("SiT", [T, T], BF16)
    self.d_s0 = i("s0", [128, 4, L, 2, 64])
    self.o_sth = self.outp("sth", [4, L, 2, 4, 64, 64])
    self.o_str = self.outp("str", [4, L, 2, 4, 64, 64])


def _mixer_init(self):
    nc = self.nc
    L = self.L
    a = lambda n, s, dt=F32: nc.alloc_sbuf_tensor("s_" + n, s, dt)
    self.ropeC = a("ropeC", [128, T]); self.ropeS = a("ropeS", [128, T]); self.smask = a("smask", [128, T])
    self.cm = a("cm", [128, 2, 512], BF16)
    self.cm2 = self.cm[:, :, :].rearrange("p d (k h n) -> p d k h n", k=2, h=2)
    self.cmr = a("cmr", [128, 2, 512], BF16)
    self.cmr2 = self.cmr[:, :, :].rearrange("p d (k h n) -> p d k h n", k=2, h=2); self.chm = a("chm", [128, 4]); self.gvec = a("gvec", [128, 32])
    self.ident = a("ident", [128, 128], BF16); self.bones = a("bones", [128, 128], BF16)
    self.retT = a("retT", [128, 2, 2, 3, 32]); self.retF = a("retF", [128, 2, 2, 32])
    self.delta = a("delta", [128, 512]); self.zT = a("zTs", [33, T])
    self.ms = a("msmall", [128, 512])
    self.sfin = a("sfin", [128, 256])
    self.qx = [a("qx0", [128, 256], BF16), a("qx1", [128, 256], BF16)]
    self.memset(self.qx[0][:], 0.0); self.memset(self.qx[1][:], 0.0)
    self.s0 = a("s0s", [128, 4, 2, 64])
    self.hw1 = a("hw1", [33, L, 64]); self.hw2 = a("hw2", [64, L, 64])
    ms = self.ms
    o = 0

    def take(n):
        nonlocal o
        v = ms[:, o:o + n]
        o += n
        return v
    self.cont = take(1); self.cm1 = take(1)
    self.lbx = take(16).rearrange("p (a l) -> p a l", l=4)
    self.oml = take(16).rearrange("p (a l) -> p a l", l=4)
    self.lbs = take(8)
    self.lnoml = take(16).rearrange("p (a l) -> p a l", l=4)
    self.epsc = take(1)
    self.hnw = take(L); self.onesc = take(1)
    self.conv = take(L * 36).rearrange("p (l t c) -> p l t c", l=L, t=3)
    self.nconv = take(24).rearrange("p (t c) -> p t c", t=2)
    self.hbias = take(L * 4).rearrange("p (l c) -> p l c", l=L)
    self.fb = take(L * 5).rearrange("p (l c) -> p l c", l=L)
    self.fs = take(4)
    self.negtl = take(8); self.m0 = take(8)
    self.Fv = take(32); self.Fg = take(32)
    self.rfa = take(48).rearrange("p (g d k a) -> p g d k a", g=2, d=2, k=3)
    self.rF8 = take(32).rearrange("p (g d n) -> p g d n", g=2, d=2)
    self.gvec8 = take(8)
    q = "sp"
    for dst, src in ((self.ropeC[:], self.d_ropeC), (self.ropeS[:], self.d_ropeS), (self.smask[:], self.d_smask),
                     (self.cm[:], self.d_cm), (self.cmr[:], self.d_cmr), (self.rfa, self.d_rfa), (self.rF8, self.d_rF8), (self.gvec8, self.d_gvec8), (self.chm[:], self.d_chm), (self.gvec[:], self.d_gvec),
                     (self.ident[:], self.d_ident), (self.bones[:], self.d_bones), (self.retT[:], self.d_retT),
                     (self.retF[:], self.d_retF), (self.delta[:], self.d_delta), (self.zT[:], self.d_zT),
                     (self.cont, self.d_cont), (self.lbx, self.d_lb), (self.hnw, self.d_hnw), (self.conv, self.d_conv),
                     (self.hbias, self.d_hbias), (self.fb[0:64], self.d_fb), (self.negtl, self.d_negtl), (self.m0, self.d_m0),
                     (self.hw1[:], self.d_w1.rearrange("l k n -> k l n")),
                     (self.hw2[:], self.d_w2.rearrange("l k n -> k l n"))):
        self.dma(q, dst, src)
    self.memset(self.onesc, 1.0)
    self.ts(self.cm1, self.cont, -1.0, None, ALU.add)
    self.act(self.lbx, self.lbx, AF.Exp)
    self.P.add("dve", lambda e: e.reduce_sum(out=self.lbs[:, 0:4], in_=self.lbx, axis=AX.X), reads=[self.lbx], writes=[self.lbs[:, 0:4]])
    self.P.add("dve", lambda e: e.reciprocal(out=self.lbs[:, 4:8], in_=self.lbs[:, 0:4]), reads=[self.lbs[:, 0:4]], writes=[self.lbs[:, 4:8]])
    self.tt(self.lbx, self.lbx, self.lbs[:, 4:8].unsqueeze(2).broadcast_to([128, 4, 4]), ALU.mult)
    self.memset(self.oml[:, :, 0:1], 1.0)
    for l in range(1, 4):
        self.tt(self.oml[:, :, l:l + 1], self.oml[:, :, l - 1:l], self.lbx[:, :, l:l + 1], ALU.subtract)
    self.act(self.lnoml, self.oml, AF.Ln)
    self.memset(self.epsc, EPS)


def _slotload(self, pieces):
    N = sum(p[1] for p in pieces)
    slot = self.next_slot()
    sv = slot[:, 0:8 * N].rearrange("p (k n) -> p k n", k=8)
    o = 0
    for src, n, q in pieces:
        self.dma(q, sv[:, :, o:o + n], src)
        o += n
    return sv


def _wcols(self, w, c0, n):
    return (w.rearrange("(k p) n -> p k n", p=128)[:, :, c0:c0 + n], n, "pool")


def _mixer(self, l):
    AR = self.AR
    L = self.L
    W = self.w_in[l]
    o = 0

    def fw(n):
        nonlocal o
        v = AR[:, o:o + n]
        o += n
        return v
    mixT = fw(4096).bitcast(BF16).rearrange("p (c t) -> p c t", c=8)
    v_tm = fw(2048).bitcast(BF16).rearrange("p (k n) -> p k n", k=8)
    base = o
    self.dma("sp", self.s0[:], self.d_s0[:, :, l, :, :])
    self.P.tag = "mix_v"
    sv = _slotload(self, [_wcols(self, W, 256, 256), _wcols(self, W, 3328, 256)])
    for tile in range(8):
        ps = self.bank()
        for k in range(8):
            self.mm(ps[:, :], self.hT[:, k, tile * 128:(tile + 1) * 128], sv[:, k, :], k == 0, k == 7)
        self.cp(v_tm[:, tile, :], ps[:, :], eng="act")

    import os
    PARTS = os.environ.get('MIX_PARTS', 'hy,gla')
    if 'hy' in PARTS:
        Gr = fw(2048).bitcast(BF16).rearrange("p (k n) -> p k n", k=8)
        Gi = fw(2048).bitcast(BF16).rearrange("p (k n) -> p k n", k=8)
        hb = o
        S_tm = fw(2048).bitcast(BF16).rearrange("p (k n) -> p k n", k=8)
        D_tm = fw(2048).bitcast(BF16).rearrange("p (k n) -> p k n", k=8)
        hid = [fw(1024), fw(1024)]
        hw3 = fw(1024)[0:64, :]
        self.P.tag = "hy_filt"
        self.dma("sp", hw3, self.d_w3[l])
        f3 = self.fs
        self.ts(f3[0:64, 0:1], self.fb[0:64, l, 2:3], 1.0 / 3.0, None, ALU.mult)
        self.tt(f3[0:64, 1:2], f3[0:64, 0:1], self.fb[0:64, l, 0:1], ALU.mult)
        self.tt(f3[0:64, 2:3], f3[0:64, 0:1], self.fb[0:64, l, 1:2], ALU.mult)

        def sin3(dst, ps, bcol):
            tmp = self.tmpf[self.ntmp % 2]
            self.ntmp += 1
            s = tmp[0:64, 0:512]
            s2 = tmp[0:64, 512:1024]
            self.act(s, ps, AF.Sin, bias=f3[0:64, bcol:bcol + 1], scale=f3[0:64, 0:1])
            self.tt(s2, s, s, ALU.mult)
            self.ts(s2, s2, -4.0, 3.0, ALU.mult, ALU.add)
            self.tt(dst, s2, s, ALU.mult)
        for th in range(2):
            ps = self.bank()
            self.mm(ps[0:64, :], self.hw1[:, l, :], self.zT[:, th * 512:(th + 1) * 512], True, True)
            sin3(hid[0][0:64, th * 512:(th + 1) * 512], ps[0:64, :], 1)
        for th in range(2):
            ps = self.bank()
            self.mm(ps[0:64, :], self.hw2[:, l, :], hid[0][0:64, th * 512:(th + 1) * 512], True, True)
            sin3(hid[1][0:64, th * 512:(th + 1) * 512], ps[0:64, :], 2)
        for tile in range(8):
            pf = self.bank()
            pb = self.bank()
            self.mm(pf[:, :], hid[1][0:64, tile * 128:(tile + 1) * 128], hw3[:, 0:512], True, True)
            self.mm(pb[:, :], hid[1][0:64, tile * 128:(tile + 1) * 128], hw3[:, 512:1024], True, True)
            t0 = self.tmpf[self.ntmp % 2]; self.ntmp += 1
            t1 = self.tmpf[self.ntmp % 2]; self.ntmp += 1
            dec = t0[:, 0:512]
            self.act(dec, self.delta[:], AF.Exp, scale=self.negtl[:, tile:tile + 1])
            self.tt(t0[:, 512:1024], pf[:, :], dec, ALU.mult)
            self.stt(t1[:, 0:512], pb[:, :], self.m0[:, tile:tile + 1], dec, ALU.mult, ALU.mult)
            self.tt(S_tm[:, tile, :], t0[:, 512:1024], t1[:, 0:512], ALU.add)
            self.tt(D_tm[:, tile, :], t0[:, 512:1024], t1[:, 0:512], ALU.subtract)
        for (tab, X, G) in ((self.d_Cf, S_tm, Gr), (self.d_Sf, D_tm, Gi)):
            tv = tab.rearrange("(k p) f -> p k f", p=128)
            for fh in range(2):
                sv = _slotload(self, [(tv[:, :, fh * 512:(fh + 1) * 512], 512, "sp")])
                for fc in range(4):
                    ps = self.bank()
                    for k in range(8):
                        self.mm(ps[:, :], sv[:, k, fc * 128:(fc + 1) * 128], X[:, k, :], k == 0, k == 7)
                    self.cp(G[:, fh * 4 + fc, :], ps[:, :], eng="act")
        self.P.tag = "hy_chunk"
        o = hb
        x0b = fw(2048).bitcast(BF16).rearrange("p (c t) -> p c t", c=4)
        ubf = fw(2048).bitcast(BF16).rearrange("p (c t) -> p c t", c=4)
        u_tm = fw(2048).bitcast(BF16).rearrange("p (k n) -> p k n", k=8)
        tb = o
        x0a = fw(1024); a1 = fw(1024); a2 = fw(1024); hraw = [fw(1024), fw(1024)]
        o = tb
        Ur = fw(2048).rearrange("p (k n) -> p k n", k=4)
        Pr = fw(2048).bitcast(BF16).rearrange("p (k n) -> p k n", k=8)
        Pi = fw(2048).bitcast(BF16).rearrange("p (k n) -> p k n", k=8)
        assert o <= AW, o
        self.ts(self.nconv[:, 0, :], self.conv[:, l, 0, :], self.cm1, None, ALU.mult)
        self.ts(self.nconv[:, 1, :], self.conv[:, l, 2, :], self.cm1, None, ALU.mult)
        nh = 0
        for j in range(4):
            sv = _slotload(self, [_wcols(self, W, 1280 + j * 128, 128), _wcols(self, W, 1792 + j * 128, 128),
                                  _wcols(self, W, 2304 + j * 128, 128)])
            accs = (x0a, a1, a2)
            for cc in range(3):
                hr = hraw[nh % 2]; nh += 1
                for th in range(2):
                    ps = self.bank()
                    for k in range(8):
                        self.mm(ps[:, :], sv[:, k, cc * 128:(cc + 1) * 128], self.hT[:, k, th * 512:(th + 1) * 512], k == 0, k == 7)
                    self.cp(hr[:, th * 512:(th + 1) * 512], ps[:, :], eng="act")
                ch = cc * 4 + j
                acc = accs[cc]
                self.act(acc, hr, AF.Copy, scale=self.conv[:, l, 1, ch:ch + 1])
                self.stt(acc[:, 1:T], hr[:, 0:T - 1], self.conv[:, l, 0, ch:ch + 1], acc[:, 1:T], ALU.mult, ALU.add)
                self.stt(acc[:, 0:T - 1], hr[:, 1:T], self.conv[:, l, 2, ch:ch + 1], acc[:, 0:T - 1], ALU.mult, ALU.add)
                self.stt(acc[:, 256:T:256], hr[:, 255:T - 1:256], self.nconv[:, 0, ch:ch + 1], acc[:, 256:T:256], ALU.mult, ALU.add)
                self.stt(acc[:, 255:T - 1:256], hr[:, 256:T:256], self.nconv[:, 1, ch:ch + 1], acc[:, 255:T - 1:256], ALU.mult, ALU.add)
            self.cp(x0b[:, j, :], x0a, eng="act")
            self.tt(ubf[:, j, :], a1, a2, ALU.mult)
        for j in range(4):
            pt = self.bank()
            ptb = pt[:, :].bitcast(BF16)
            for k in range(8):
                self.tr(ptb[:, k * 128:(k + 1) * 128], ubf[:, j, k * 128:(k + 1) * 128], self.ident[:])
            self.cp(u_tm[:, :, j * 128:(j + 1) * 128], ptb.rearrange("p (k n) -> p k n", k=8), eng=("act" if j % 2 == 0 else "dve"))
        tvC = self.d_Cf.rearrange("(k p) f -> p k f", p=128)
        tvS = self.d_Sf.rearrange("(k p) f -> p k f", p=128)
        for fh in range(2):
            sv = _slotload(self, [(tvC[:, :, fh * 512:(fh + 1) * 512], 512, "sp")])
            for fc in range(4):
                ps = self.bank()
                for k in range(8):
                    self.mm(ps[:, :], sv[:, k, fc * 128:(fc + 1) * 128], u_tm[:, k, :], k == 0, k == 7)
                self.cp(Ur[:, fc, :], ps[:, :], eng="act")
            sv = _slotload(self, [(tvS[:, :, fh * 512:(fh + 1) * 512], 512, "sp")])
            for fc in range(4):
                fi = fh * 4 + fc
                ps = self.bank()
                for k in range(8):
                    self.mm(ps[:, :], sv[:, k, fc * 128:(fc + 1) * 128], u_tm[:, k, :], k == 0, k == 7)
                Ui = ps[:, :]
                t0 = self.tmpf[0]; t1 = self.tmpf[1]
                self.tt(t0[:, 0:512], Gr[:, fi, :], Ur[:, fc, :], ALU.mult)
                self.tt(t0[:, 512:1024], Ui, Gi[:, fi, :], ALU.mult)
                self.tt(Pr[:, fi, :], t0[:, 0:512], t0[:, 512:1024], ALU.subtract)
                self.tt(t1[:, 0:512], Ui, Gr[:, fi, :], ALU.mult)
                self.tt(t1[:, 512:1024], Gi[:, fi, :], Ur[:, fc, :], ALU.mult)
                self.tt(Pi[:, fi, :], t1[:, 0:512], t1[:, 512:1024], ALU.add)
        tvCi = self.d_Ci.rearrange("(k p) t -> p k t", p=128)
        tvSi = self.d_Si.rearrange("(k p) t -> p k t", p=128)
        for th in range(2):
            sl = slice(th * 512, (th + 1) * 512)
            svc = _slotload(self, [(tvCi[:, :, sl], 512, "sp")])
            svs = _slotload(self, [(tvSi[:, :, sl], 512, "sp")])
            for j in range(4):
                py = self.bank()
                for k in range(8):
                    self.mm(py[:, :], Pr[:, k, j * 128:(j + 1) * 128], svc[:, k, :], k == 0, False)
                for k in range(8):
                    self.mm(py[:, :], Pi[:, k, j * 128:(j + 1) * 128], svs[:, k, :], False, k == 7)
                t0 = self.tmpf[j % 2]
                self.stt(t0[:, 0:512], ubf[:, j, sl], self.hbias[:, l, j:j + 1], py[:, :], ALU.mult, ALU.add)
                self.tt(mixT[:, 2 + j, sl], t0[:, 0:512], x0b[:, j, sl], ALU.mult)

    if 'gla' in PARTS:
        o = base
        qf = fw(1024); kf = fw(1024); lg = fw(1024); bb = fw(1024); E = fw(1024)
        qt = fw(512).bitcast(BF16); kt = fw(512).bitcast(BF16); kh = fw(512).bitcast(BF16)
        kh_tm = fw(512).bitcast(BF16).rearrange("p (k n) -> p k n", k=8)
        KV = fw(2112); Fr = fw(2112)
        SP = fw(2048).bitcast(BF16).rearrange("p (n m) -> p n m", n=32)
        osb = fw(1024)
        vpad = fw(1024).bitcast(BF16).rearrange("p (k h m) -> p k h m", k=8, h=2)
        vx = [fw(256).bitcast(BF16), fw(256).bitcast(BF16)]
        Ab = [fw(128).bitcast(BF16).rearrange("p (h n) -> p h n", h=2), fw(128).bitcast(BF16).rearrange("p (h n) -> p h n", h=2)]
        Sfin = self.sfin[:, :].rearrange("p (b v) -> p b v", b=4)
        assert o <= AW, o
        KV3 = KV.rearrange("p (d n) -> p d n", n=33)
        Fr3 = Fr.rearrange("p (d n) -> p d n", n=33)
        self.memset(SP[:, :, :], 0.0)
        self.memset(vpad[:, :, :, :], 0.0)
        self.memset(Fr3[:, :, 0:1], 0.0)
        nvx = 0
        for g in range(4):
            ret = g >= 2
            gp = g - 2
            vc0 = (256 + gp * 128) if ret else g * 128
            self.P.tag = "gla_proj"
            for h in range(2):
                self.cp(vpad[:, :, h, h * 64:(h + 1) * 64], v_tm[:, :, vc0 + h * 64:vc0 + (h + 1) * 64], eng="act")
            if ret:
                sv = _slotload(self, [_wcols(self, W, 2816 + gp * 128, 128), _wcols(self, W, 3072 + gp * 128, 128),
                                      _wcols(self, self.w_inp[l], gp * 128, 128), _wcols(self, self.w_inp[l], 256 + gp * 128, 128)])
            else:
                sv = _slotload(self, [_wcols(self, W, g * 128, 128), _wcols(self, W, 512 + g * 128, 128),
                                      _wcols(self, W, 768 + g * 128, 128), _wcols(self, W, 1024 + g * 128, 128)])
                svg, gcc = sv, 3
            if ret:
                svg, gcc = _slotload(self, [_wcols(self, W, 3584 + gp * 128, 128)]), 0
            mc = (6 + gp) if ret else g

            def proj(cc, th):
                ps = self.bank()
                for k in range(8):
                    self.mm(ps[:, :], sv[:, k, cc * 128:(cc + 1) * 128], self.hT[:, k, th * 512:(th + 1) * 512], k == 0, k == 7)
                return ps
            if ret:
                for (dst, c_a, c_b, sc) in ((qf, 0, 2, 1.0), (kf, 1, 3, 0.125)):
                    for th in range(2):
                        sl = slice(th * 512, (th + 1) * 512)
                        pa = proj(c_a, th)
                        pb_ = proj(c_b, th)
                        t0 = self.tmpf[self.ntmp % 2]; self.ntmp += 1
                        self.tt(t0[:, 0:512], pa[:, :], self.ropeC[:, sl], ALU.mult)
                        self.tt(t0[:, 512:1024], pb_[:, :], self.ropeS[:, sl], ALU.mult)
                        self.tt(t0[:, 0:512], t0[:, 0:512], t0[:, 512:1024], ALU.add)
                        self.ts(dst[:, sl], t0[:, 0:512], sc, None, ALU.mult)
            else:
                for th in range(2):
                    sl = slice(th * 512, (th + 1) * 512)
                    pa = proj(0, th)
                    self.act(qf[:, sl], pa[:, :], AF.Silu)
            for th in range(2):
                sl = slice(th * 512, (th + 1) * 512)
                pg = self.bank()
                for k in range(8):
                    self.mm(pg[:, :], svg[:, k, gcc * 128:(gcc + 1) * 128], self.hT[:, k, sl], k == 0, k == 7)
                self.act(mixT[:, mc, sl], pg[:, :], AF.Silu)
            for dr in [int(x) for x in os.environ.get('GLA_DIRS', '0,1').split(',')]:
                self.P.tag = "gla_A"
                q3 = qf.rearrange("p (n j) -> p n j", j=32)
                k3 = kf.rearrange("p (n j) -> p n j", j=32)
                qt3 = qt.rearrange("p (n j) -> p n j", j=32)
                kt3 = kt.rearrange("p (n j) -> p n j", j=32)
                kh3 = kh.rearrange("p (n j) -> p n j", j=32)
                if ret:
                    tab = lambda kind: self.retT[:, gp, dr, kind:kind + 1, :].broadcast_to([128, 32, 32])
                    fa4 = lambda kind: self.rfa[:, gp, dr, kind, :].unsqueeze(1).unsqueeze(3).broadcast_to([128, 8, 4, 32])
                    v4 = lambda v: v.rearrange("p (t a b) -> p t a b", a=4, b=32)
                    E3_ = E.rearrange("p (n j) -> p n j", j=32)
                    for (dst_, src3, kind) in ((qt, q3, 0), (kt, k3, 1), (kh, k3, 2)):
                        self.tt(E3_, src3, tab(kind), ALU.mult)
                        self.tt(v4(dst_), v4(E), fa4(kind), ALU.mult)
                    Fv = self.rF8[:, gp, dr, :]
                else:
                    a_idx = dr * 2 + g
                    b3 = bb.rearrange("p (n j) -> p n j", j=32)
                    tot = b3[:, :, 31:32]
                    HS = [slice(0, 512), slice(512, 1024)]
                    n3 = lambda v, th: v[:, HS[th]].rearrange("p (n j) -> p n j", j=32)
                    toth = lambda th: b3[:, th * 16:(th + 1) * 16, 31:32].broadcast_to([128, 16, 32])
                    t0 = self.tmpf[0]
                    lnoml = self.lnoml[:, a_idx, l:l + 1]
                    pzs = [proj(1 + dr, th) for th in range(2)]
                    for th in range(2):
                        self.act(t0[:, HS[th]], pzs[th][:, :], AF.Exp)
                    for th in range(2):
                        self.act(t0[:, HS[th]], t0[:, HS[th]], AF.Ln, bias=self.onesc, scale=1.0)
                    for th in range(2):
                        self.act(kf[:, HS[th]], t0[:, HS[th]], AF.Exp, bias=lnoml, scale=-1.0)
                    for th in range(2):
                        self.act(lg[:, HS[th]], kf[:, HS[th]], AF.Ln, bias=self.onesc, scale=-1.0)
                    for th in range(2):
                        self.scan(bb[:, HS[th]], self.smask[:, HS[th]], lg[:, HS[th]])
                    if dr == 0:
                        bc = bb
                    else:
                        for th in range(2):
                            self.tt(E[:, HS[th]], lg[:, HS[th]], bb[:, HS[th]], ALU.subtract)
                        for th in range(2):
                            self.tt(n3(lg, th), n3(E, th), toth(th), ALU.add)
                        bc = lg
                    for th in range(2):
                        self.act(E[:, HS[th]], bc[:, HS[th]], AF.Exp)
                    for th in range(2):
                        self.act(t0[:, HS[th]], bc[:, HS[th]], AF.Exp, scale=-1.0)
                    for th in range(2):
                        self.tt(qt[:, HS[th]], qf[:, HS[th]], E[:, HS[th]], ALU.mult)
                    for th in range(2):
                        self.tt(kt[:, HS[th]], kf[:, HS[th]], t0[:, HS[th]], ALU.mult)
                    for th in range(2):
                        self.tt(n3(E, th), toth(th), n3(bc, th), ALU.subtract)
                    for th in range(2):
                        self.act(E[:, HS[th]], E[:, HS[th]], AF.Exp)
                    for th in range(2):
                        self.tt(kh[:, HS[th]], kf[:, HS[th]], E[:, HS[th]], ALU.mult)
                    self.act(self.Fv.unsqueeze(2), tot, AF.Exp)
                    Fv = self.Fv
                if int(os.environ.get('GLA_STOP', '99')) <= 1:
                    continue
                self.P.tag = "gla_B"
                NC = 8 if ret else 32
                CPT = NC // 8
                BQ = NC // 4
                KVf = KV[:, 0:64 * (NC + 1)]
                Frf = Fr[:, 0:64 * (NC + 1)]
                KV3 = KVf.rearrange("p (d n) -> p d n", n=NC + 1)
                Fr3 = Frf.rearrange("p (d n) -> p d n", n=NC + 1)
                Fproc = Fv if dr == 0 else Fv[:, NC - 1::-1]
                Fg = self.Fg[:, 0:NC]
                self.tt(Fg, Fproc, (self.gvec8 if ret else self.gvec[:]), ALU.mult)
                self.memset(Fr3[:, :, 0:1], 0.0)
                self.cp(Fr3[:, :, 1:NC + 1], Fg.unsqueeze(1).broadcast_to([128, 64, NC]))
                self.cp(KV3[:, :, 0:1], self.s0[:, g, dr, :].unsqueeze(2))
                pt = self.bank()
                ptb = pt[:, :].bitcast(BF16)
                for k in range(8):
                    self.tr(ptb[:, k * 128:(k + 1) * 128], kh[:, k * 128:(k + 1) * 128], self.ident[:])
                self.cp(kh_tm[:, :, :], ptb.rearrange("p (k n) -> p k n", k=8), eng="act")
                if int(os.environ.get('GLA_STOP', '99')) <= 2:
                    continue
                for tile in (range(8) if ret else []):
                    ps = self.bank()
                    self.mm(ps[:, 0:128], kh_tm[:, tile, :], v_tm[:, tile, vc0:vc0 + 128], True, True)
                    for h in range(2):
                        hp = slice(h * 64, (h + 1) * 64)
                        slot_ = (1 + tile) if dr == 0 else (8 - tile)
                        self.cp(KV3[hp, :, slot_], ps[hp, h * 64:(h + 1) * 64], eng=("act" if h == 0 else "dve"))
                for tile in ([] if ret else range(8)):
                    vxt = vx[nvx % 2]; nvx += 1
                    v2 = v_tm[:, tile, vc0:vc0 + 128].rearrange("p (h d) -> p h d", h=2)
                    self.tt(vxt.rearrange("p (h c d) -> p h c d", h=2, c=4),
                            v2.unsqueeze(2).broadcast_to([128, 2, 4, 64]),
                            self.chm[:].unsqueeze(1).unsqueeze(3).broadcast_to([128, 2, 4, 64]), ALU.mult)
                    ps = self.bank()
                    self.mm(ps[:, :], kh_tm[:, tile, :], vxt, True, True)
                    for h in range(2):
                        src = ps[h * 64:(h + 1) * 64, h * 256:(h + 1) * 256].rearrange("p (c d) -> p c d", c=4)
                        if dr == 0:
                            dst = KV3[h * 64:(h + 1) * 64, :, 1 + 4 * tile:5 + 4 * tile]
                        else:
                            hi = 32 - 4 * tile
                            dst = KV3[h * 64:(h + 1) * 64, :, hi:hi - 4:-1] if hi - 4 > 0 else KV3[h * 64:(h + 1) * 64, :, hi:0:-1]
                        self.cp(dst.transpose([0, 2, 1]), src, eng="act")
                if int(os.environ.get('GLA_STOP', '99')) <= 3:
                    continue
                self.P.tag = "gla_C"
                self.scan(KVf, Frf, KVf)
                if int(os.environ.get('GLA_STOP', '99')) <= 4:
                    continue
                srcf = KV3[:, :, BQ:NC + 1:BQ] if dr == 0 else KV3[:, :, NC:0:-BQ]
                self.cp(Sfin, srcf.transpose([0, 2, 1]))
                od = (self.o_str if ret else self.o_sth)
                hh = (gp if ret else g) * 2
                self.dma("sp", od[:, l, dr, hh:hh + 2, :, :].rearrange("b h k v -> (h k) b v"), Sfin)
                if int(os.environ.get('GLA_STOP', '99')) <= 5:
                    continue
                for h in range(2):
                    s_ = KV3[h * 64:(h + 1) * 64, :, 0:NC] if dr == 0 else KV3[h * 64:(h + 1) * 64, :, NC - 1::-1]
                    self.cp(SP[h * 64:(h + 1) * 64, 0:NC, h * 64:(h + 1) * 64], s_.transpose([0, 2, 1]), eng=("act" if h == 0 else "dve"))
                if int(os.environ.get('GLA_STOP', '99')) <= 6:
                    continue
                for h in range(2):
                    hp = slice(h * 64, (h + 1) * 64)
                    if dr == 0:
                        gsrc = KV3[hp, :, BQ:NC:BQ]; gdst = SP[hp, BQ:NC:BQ, hp]
                    else:
                        gsrc = KV3[hp, :, NC - BQ:0:-BQ]; gdst = SP[hp, BQ - 1:NC - BQ:BQ, hp]
                    self.ts(gdst, gsrc.transpose([0, 2, 1]), self.cont[hp, :], None, ALU.mult)
                self.P.tag = "gla_D"
                qxa = self.rstd[:, :].bitcast(BF16).rearrange("p (k h n) -> p k h n", k=8, h=2)
                Aall = self.tmpf[1][:, :].bitcast(BF16).rearrange("p (k h n) -> p k h n", k=8, h=2)
                for h in range(2):
                    hp = slice(h * 64, (h + 1) * 64)
                    oth = slice((1 - h) * 64, (2 - h) * 64)
                    self.cp(qxa[hp, :, h, :], qt[hp, :].rearrange("p (k n) -> p k n", k=8), eng="act")
                    self.memset(qxa[oth, :, h, :], 0.0)
                for bp in range(4):
                    pa = self.bank()
                    for t2 in range(2):
                        tile = 2 * bp + t2
                        cs = slice(tile * 128, (tile + 1) * 128)
                        self.mm(pa[:, t2 * 256:(t2 + 1) * 256], kt[:, cs], qxa[:, tile, :, :], True, True)
                    self.tt(Aall[:, 2 * bp:2 * bp + 2, :, :], pa[:, :].rearrange("p (k h n) -> p k h n", k=2, h=2),
                            (self.cmr2 if ret else self.cm2)[:, dr, :, :, :], ALU.mult)
                for tile in range(8):
                    cs = slice(tile * 128, (tile + 1) * 128)
                    po = self.bank()
                    self.mm(po[:, 0:128], vpad[:, tile, 0, :], Aall[:, tile, 0, :], True, False)
                    self.mm(po[:, 0:128], vpad[:, tile, 1, :], Aall[:, tile, 1, :], False, False)
                    WC = 128 // CPT
                    for c in range(CPT):
                        n = CPT * tile + c
                        self.mm(po[:, WC * c:WC * (c + 1)], SP[:, n, :], qt[:, tile * 128 + WC * c:tile * 128 + WC * (c + 1)], False, c == CPT - 1)
                    if dr == 0:
                        self.cp(osb[:, cs], po[:, 0:128], eng="act")
                    else:
                        self.tt(osb[:, cs], osb[:, cs], po[:, 0:128], ALU.add)
            self.P.tag = "gla_epi"
            sq = qt
            self.act(sq, osb, AF.Square)
            nw = self.onesc if ret else self.hnw[:, l:l + 1]
            for th in range(2):
                sl = slice(th * 512, (th + 1) * 512)
                ps = self.bank()
                self.mm(ps[:, :], self.bones[:], sq[:, sl], True, True)
                rs = E[:, sl]
                self.act(rs, ps[:, :], AF.Ln, bias=self.epsc, scale=1.0 / 64.0)
                self.act(rs, rs, AF.Exp, scale=-0.5)
                t0 = self.tmpf[0]
                self.tt(t0[:, sl], osb[:, sl], rs, ALU.mult)
                self.stt(mixT[:, mc, sl], t0[:, sl], nw, mixT[:, mc, sl], ALU.mult, ALU.mult)

    self.P.tag = "wout"
    wo = self.w_out[l].rearrange("(k p) n -> p k n", p=128)
    for dh in range(2):
        sv = _slotload(self, [(wo[:, :, dh * 512:(dh + 1) * 512], 512, "pool")])
        for dcc in range(4):
            dc = dh * 4 + dcc
            for th in range(2):
                ps = self.bank()
                for k in range(8):
                    self.mm(ps[:, :], sv[:, k, dcc * 128:(dcc + 1) * 128], mixT[:, k, th * 512:(th + 1) * 512], k == 0, k == 7)
                xs = self.xT[:, dc, th * 512:(th + 1) * 512]
                self.stt(xs, ps[:, :], self.gates[:, 1, dc:dc + 1], xs, ALU.mult, ALU.add)


KB.mixer_decl = _mixer_decl
KB.mixer_init = _mixer_init
KB.mixer = _mixer

def _consts(kind):
    bf = ml_dtypes.bfloat16
    Ls = 1024 if kind == "s" else 256
    c = {}
    t = np.arange(T)
    tl = t % Ls
    cont = 1.0 if kind == "s" else 0.0
    c["cont"] = np.full((128, 1), cont, np.float32)
    d = np.arange(64)
    fr = 1.0 / (10000.0 ** (np.arange(0, 32, 2, dtype=np.float64) / 32.0))
    fidx = d % 16
    pos = np.where(d[:, None] < 32, (t // 64)[None, :], (t % 64)[None, :]).astype(np.float64)
    ang = pos * fr[fidx][:, None]
    sgn = np.where((d % 32) < 16, -1.0, 1.0)[:, None]
    if kind == "s":
        C = np.cos(ang); S = sgn * np.sin(ang)
    else:
        C = np.ones((64, T)); S = np.zeros((64, T))
    c["ropeC"] = np.tile(C, (2, 1)).astype(np.float32)
    c["ropeS"] = np.tile(S, (2, 1)).astype(np.float32)
    c["smask"] = np.tile((t % 32 != 0).astype(np.float32)[None], (128, 1))
    s_ = np.arange(128)[:, None]; t_ = np.arange(128)[None, :]
    same = (s_ // 32) == (t_ // 32)
    cm = np.stack([(same & (s_ <= t_)), (same & (s_ >= t_))], axis=1).astype(np.float32)
    c["cm"] = np.ascontiguousarray(np.concatenate([cm, cm, cm, cm], axis=2)).astype(bf)
    cmr = np.stack([(s_ <= t_), (s_ >= t_)], axis=1).astype(np.float32)
    c["cmr"] = np.ascontiguousarray(np.concatenate([cmr, cmr, cmr, cmr], axis=2)).astype(bf)
    gv8 = np.ones((128, 8), np.float32); gv8[:, [2, 4, 6]] = cont
    c["gvec8"] = gv8
    c["chm"] = ((np.arange(128)[:, None] // 32) == np.arange(4)[None, :]).astype(np.float32)
    gv = np.ones((128, 32), np.float32); gv[:, [8, 16, 24]] = cont
    c["gvec"] = gv
    c["ident"] = np.eye(128, dtype=np.float32).astype(bf)
    bo = np.zeros((128, 128), np.float32); bo[:64, :64] = 1; bo[64:, 64:] = 1
    c["bones"] = bo.astype(bf)
    lg_all = np.log1p(-np.exp2(-5.0 - 0.5 * np.arange(8, dtype=np.float64)))
    retT = np.zeros((128, 2, 2, 3, 32)); retF = np.zeros((128, 2, 2, 32))
    j = np.arange(32)
    for p in range(128):
        for gp in range(2):
            hh = gp * 2 + p // 64
            for dr in range(2):
                lg = lg_all[2 * hh + dr]
                if dr == 0:
                    retT[p, gp, dr, 0] = np.exp(lg * (j + 1)); retT[p, gp, dr, 1] = np.exp(-lg * (j + 1)); retT[p, gp, dr, 2] = np.exp(lg * (31 - j))
                else:
                    retT[p, gp, dr, 0] = np.exp(lg * (32 - j)); retT[p, gp, dr, 1] = np.exp(-lg * (32 - j)); retT[p, gp, dr, 2] = np.exp(lg * j)
                retF[p, gp, dr, :] = np.exp(32 * lg)
    c["retT"] = retT.astype(np.float32); c["retF"] = retF.astype(np.float32)
    rfa = np.zeros((128, 2, 2, 3, 4)); rF8 = np.zeros((128, 2, 2, 8))
    a_ = np.arange(4)
    for p in range(128):
        for gp in range(2):
            hh = gp * 2 + p // 64
            for dr in range(2):
                lg = lg_all[2 * hh + dr]
                if dr == 0:
                    rfa[p, gp, dr, 0] = np.exp(lg * 32 * a_); rfa[p, gp, dr, 1] = np.exp(-lg * 32 * a_); rfa[p, gp, dr, 2] = np.exp(lg * 32 * (3 - a_))
                else:
                    rfa[p, gp, dr, 0] = np.exp(lg * 32 * (3 - a_)); rfa[p, gp, dr, 1] = np.exp(-lg * 32 * (3 - a_)); rfa[p, gp, dr, 2] = np.exp(lg * 32 * a_)
                rF8[p, gp, dr, :] = np.exp(128 * lg)
    c["rfa"] = rfa.astype(np.float32); c["rF8"] = rF8.astype(np.float32)
    tn = tl.astype(np.float64) / Ls
    bands = np.arange(1, 17, dtype=np.float64)
    a2 = 2.0 * np.pi * tn[:, None] * bands[None]
    z = np.concatenate([tn[:, None], np.cos(a2), np.sin(a2)], axis=-1)
    c["zT"] = np.ascontiguousarray(z.T).astype(np.float32)
    MIN_DECAY = math.log(1e-2) / 1.5; MAX_DECAY = math.log(1e-2) / 0.3
    deltas = np.abs(np.linspace(MIN_DECAY, MAX_DECAY, 512, dtype=np.float32)).astype(np.float32)
    c["deltab"] = np.tile(deltas[None], (128, 1)).astype(np.float32)
    tt_ = (np.arange(8)[None, :] * 128 + np.arange(128)[:, None])
    c["negtl"] = (-((tt_ % Ls).astype(np.float64) / Ls)).astype(np.float32)
    c["m0"] = ((tt_ % Ls) != 0).astype(np.float32)
    th = np.zeros((T, T)); blk = (t[:, None] // Ls) == (t[None, :] // Ls)
    th = np.pi * (2 * tl[None, :] + 1) * tl[:, None] / (2.0 * Ls)
    Cm = np.where(blk, np.cos(th), 0.0); Sm = np.where(blk, -np.sin(th), 0.0)
    c["Cf"] = Cm.astype(np.float32).astype(bf); c["Sf"] = Sm.astype(np.float32).astype(bf)
    c["CiT"] = np.ascontiguousarray((Cm / Ls).T).astype(np.float32).astype(bf)
    c["SiT"] = np.ascontiguousarray((Sm / Ls).T).astype(np.float32).astype(bf)
    return c


def _prep_mixer_common(inputs, L):
    f = lambda a: np.ascontiguousarray(np.asarray(a, dtype=np.float32))
    com = {}
    w_in = np.asarray(inputs["w_in"], dtype=np.float32)[:L]
    com["w_in"] = f(w_in)
    com["w_out"] = f(inputs["w_out"])[:L]
    d = np.arange(64)
    perm = np.where((d % 32) < 16, d + 16, d - 16)
    colperm = (np.arange(4)[:, None] * 64 + perm[None, :]).reshape(-1)
    com["w_inp"] = f(np.concatenate([w_in[:, :, 2816 + colperm], w_in[:, :, 3072 + colperm]], axis=-1))
    lb = np.asarray(inputs["hgrn_lb"], dtype=np.float32)
    com["hgrn_lbT"] = f(lb.reshape(2, 4, 2, 128).transpose(3, 0, 2, 1).reshape(128, 4, 4))
    hn = np.asarray(inputs["hgrn_norm_w"], dtype=np.float32)[:L]
    com["hnwT"] = f(np.tile(hn.T, (2, 1)))
    hc = np.asarray(inputs["hyena_conv"], dtype=np.float32)[:L]
    com["convT"] = f(hc.reshape(L, 3, 12, 128).transpose(3, 0, 1, 2))
    hb = np.asarray(inputs["hyena_bias"], dtype=np.float32)[:L]
    com["hbiasT"] = f(hb.reshape(L, 4, 128).transpose(2, 0, 1))
    com["hy_w1"] = f(inputs["hyena_w1"])[:L]; com["hy_w2"] = f(inputs["hyena_w2"])[:L]; com["hy_w3"] = f(inputs["hyena_w3"])[:L]
    fb = np.zeros((64, L, 5), np.float32)
    fb[:, :, 0] = np.asarray(inputs["hyena_b1"])[:L].T; fb[:, :, 1] = np.asarray(inputs["hyena_b2"])[:L].T
    fb[:, :, 2] = np.asarray(inputs["hyena_freq"])[:L].T
    com["hy_fb"] = fb
    return com


def _s0_for(inputs, kind, i, L):
    s0 = np.zeros((128, 4, L, 2, 64), np.float32)
    if kind == "s":
        sh = np.asarray(inputs["state_hgrn"], dtype=np.float32)[i][:L]
        sr = np.asarray(inputs["state_ret"], dtype=np.float32)[i][:L]
        for g in range(2):
            s0[:, g] = sh[:, :, 2 * g:2 * g + 2].transpose(2, 3, 0, 1, 4).reshape(128, L, 2, 64)
            s0[:, 2 + g] = sr[:, :, 2 * g:2 * g + 2].transpose(2, 3, 0, 1, 4).reshape(128, L, 2, 64)
    return s0


_KB_CACHE = {}


def run_all(inputs, L=4):
    if L not in _KB_CACHE:
        kb = KB(L, do_mixer=True)
        kb.build()
        _KB_CACHE[L] = kb
    kb = _KB_CACHE[L]
    com = _prep_common(inputs, L)
    com.update(_prep_mixer_common(inputs, L))
    cst = {"s": _consts("s"), "p": _consts("p")}
    maps = []
    for kind, i in core_assign():
        if kind == "s":
            x = np.asarray(inputs["x_sample"], dtype=np.float32)[i]
            cond = np.asarray(inputs["c"], dtype=np.float32)[i]
        else:
            x = np.asarray(inputs["x_prompt"], dtype=np.float32)[4 * i:4 * i + 4].reshape(1024, 1024)
            cond = np.asarray(inputs["c_ctx"], dtype=np.float32)
        m = dict(com)
        m.update(cst[kind])
        m["xT"] = np.ascontiguousarray(x.T)
        m["cond"] = np.ascontiguousarray(cond.reshape(8, 128).T)
        m["s0"] = _s0_for(inputs, kind, i, L)
        maps.append(m)
    res = run_bass_kernel_spmd(kb.nc, maps, core_ids=list(range(8)))
    R_ = res.results
    y_s = np.stack([np.ascontiguousarray(R_[i]["yT"].T) for i in range(2)], axis=0)
    y_p = np.concatenate([np.ascontiguousarray(R_[2 + g]["yT"].T).reshape(4, 256, 1024) for g in range(4)], axis=0)
    sth = np.concatenate([R_[2 + g]["sth"] for g in range(4)], axis=0)
    str_ = np.concatenate([R_[2 + g]["str"] for g in range(4)], axis=0)
    return (y_p.astype(np.float32), y_s.astype(np.float32), sth.astype(np.float32), str_.astype(np.float32))


def kernel(**inputs):
    return run_all(inputs, 4)
```

```python
import concourse.bass as bass
import concourse.mybir as mybir

import os
ANNOTATE = bool(os.environ.get("KANNOT"))
ENGS = ("pe", "act", "dve", "pool", "sp")
DT_SIZE = {"dt.float32": 4, "dt.bfloat16": 2, "dt.int32": 4, "dt.uint32": 4, "dt.float16": 2, "dt.uint8": 1, "dt.int8": 1, "dt.uint16": 2, "dt.int16": 2}


class Op:
    __slots__ = ("id", "eng", "fn", "deps", "is_dma", "signal", "count", "dsem", "dval", "is_mm", "tag")

    def __init__(self, id, eng, fn, is_dma, is_mm):
        self.id = id
        self.eng = eng
        self.fn = fn
        self.deps = set()
        self.is_dma = is_dma
        self.is_mm = is_mm
        self.signal = False
        self.count = 0
        self.dsem = None
        self.dval = 0


def footprint(ap):
    sp = str(ap.space)
    if "DRAM" in sp.upper():
        return None
    t = ap.tensor
    shp = list(t.shape)
    F = 1
    for s in shp[1:]:
        F *= s
    off = int(ap.offset)
    pairs = ap.ap
    esz = DT_SIZE[str(ap.dtype)]
    p0 = off // F
    lo = off % F
    pstep, pcnt = pairs[0]
    if pstep == F or pcnt == 1:
        p1 = p0 + pcnt
        rest = pairs[1:]
    else:
        p1 = p0 + 1
        rest = pairs
    ext = 0
    for st, cn in rest:
        ext += abs(st) * (cn - 1)
    hi = lo + ext + 1
    if 'PSUM' in sp.upper():
        return (ap.tensor.name, (p0 // 32) * 32, ((p1 + 31) // 32) * 32, 0, 1 << 20)
    return (ap.tensor.name, p0, p1, lo * esz, hi * esz)


class Prog:
    def __init__(self, nc, same_engine_sync=True, ndma_sems=8):
        self.nc = nc
        self.ops = []
        self.recs = {}
        self.same_engine_sync = same_engine_sync
        self.ndma = ndma_sems

    def add(self, eng, fn, reads=(), writes=(), is_dma=False, is_mm=False):
        op = Op(len(self.ops), eng, fn, is_dma, is_mm)
        op.tag = getattr(self, 'tag', '')
        self.ops.append(op)
        for ap in reads:
            fp = footprint(ap)
            if fp is None:
                continue
            self._access(op, fp, False)
        for ap in writes:
            fp = footprint(ap)
            if fp is None:
                continue
            self._access(op, fp, True)
        return op

    def _access(self, op, fp, is_write):
        name, p0, p1, lo, hi = fp
        lst = self.recs.setdefault(name, [])
        keep = []
        for r in lst:
            ov = not (r[1] <= p0 or p1 <= r[0] or r[3] <= lo or hi <= r[2])
            if ov and r[4] != op.id:
                if is_write or r[5]:
                    op.deps.add(r[4])
                if is_write and r[0] >= p0 and r[1] <= p1 and r[2] >= lo and r[3] <= hi:
                    continue
            keep.append(r)
        keep.append([p0, p1, lo, hi, op.id, is_write])
        self.recs[name] = keep

    def finalize(self):
        nc = self.nc
        ops = self.ops
        for op in ops:
            best = {}
            for d in list(op.deps):
                o = ops[d]
                if o.is_dma:
                    continue
                if o.eng not in best or d > best[o.eng]:
                    best[o.eng] = d
            for d in list(op.deps):
                o = ops[d]
                if not o.is_dma and best[o.eng] != d:
                    op.deps.discard(d)
        for op in ops:
            for d in list(op.deps):
                o = ops[d]
                if o.is_dma:
                    continue
                if o.eng == op.eng and not op.is_dma:
                    if o.eng == "pe" or not self.same_engine_sync:
                        op.deps.discard(d)
                        continue
                o.signal = True
        cnt = {e: 0 for e in ENGS}
        dcnt = {e: 0 for e in ENGS}
        for op in ops:
            if op.is_dma:
                i = dcnt[op.eng]
                dcnt[op.eng] += 1
                op.dsem = (op.eng, i % self.ndma)
                op.dval = 16 * (i // self.ndma + 1)
            elif op.signal:
                cnt[op.eng] += 1
                op.count = cnt[op.eng]
        self.cnt = cnt
        import contextlib
        stack = contextlib.ExitStack()
        self.stack = stack
        esem = {e: stack.enter_context(nc.semaphore("c_" + e)) for e in ENGS}
        dsem = {}
        for e in ENGS:
            if dcnt[e]:
                for j in range(self.ndma):
                    dsem[(e, j)] = stack.enter_context(nc.semaphore("d_%s%d" % (e, j)))
        byeng = {e: [o for o in ops if o.eng == e] for e in ENGS}
        engobj = {"pe": "tensor", "act": "scalar", "dve": "vector", "pool": "gpsimd", "sp": "sync"}
        last_dma = {}

        def run_engine(e, eng):
            waited = {}
            lst = byeng[e]
            for op in lst:
                need = {}
                for d in op.deps:
                    o = ops[d]
                    if o.is_dma:
                        key = ("d",) + o.dsem
                        v = o.dval
                    else:
                        key = ("c", o.eng)
                        v = o.count
                    if v > need.get(key, 0):
                        need[key] = v
                if op.is_dma:
                    if op.dval > 16:
                        key = ("d",) + op.dsem
                        v = op.dval - 16
                        if v > need.get(key, 0):
                            need[key] = v
                for key, v in need.items():
                    if waited.get(key, 0) >= v:
                        continue
                    waited[key] = v
                    sem = dsem[key[1:]] if key[0] == "d" else esem[key[1]]
                    eng.wait_ge(sem, v)
                ins = op.fn(eng)
                if ANNOTATE and op.tag:
                    ins.annotate(op.tag)
                if op.is_dma:
                    ins.then_inc(dsem[op.dsem], 16)
                elif op.signal:
                    ins.then_inc(esem[op.eng], 1)
            for j in range(self.ndma):
                if (e, j) in dsem:
                    n = (dcnt[e] - 1 - j) // self.ndma + 1 if dcnt[e] > j else 0
                    if n > 0:
                        eng.wait_ge(dsem[(e, j)], 16 * n)

        with nc.Block() as block:
            @block.tensor
            def _(eng):
                run_engine("pe", eng)

            @block.scalar
            def _(eng):
                run_engine("act", eng)

            @block.vector
            def _(eng):
                run_engine("dve", eng)

            @block.gpsimd
            def _(eng):
                run_engine("pool", eng)

            @block.sync
            def _(eng):
                run_engine("sp", eng)
        stack.close()

import math
import numpy as np
import ml_dtypes
from concourse.bass_utils import run_bass_kernel_spmd

F32 = mybir.dt.float32
BF16 = mybir.dt.bfloat16
AF = mybir.ActivationFunctionType
ALU = mybir.AluOpType
AX = mybir.AxisListType

D = 1024
T = 1024
DFF = 2816
NF = 22
EPS = 1e-6


class KB:
    def __init__(self, L, do_mixer=True):
        self.L = L
        self.do_mixer = do_mixer
        nc = bass.Bass("TRN2", target_bir_lowering=False)
        self.nc = nc
        self.P = Prog(nc)
        self.din = {}
        self.dout = {}
        self.nbank = 0

    def inp(self, name, shape, dt=F32):
        self.din[name] = self.nc.dram_tensor(name, list(shape), dt, kind="ExternalInput").ap()
        return self.din[name]

    def outp(self, name, shape, dt=F32):
        self.dout[name] = self.nc.dram_tensor(name, list(shape), dt, kind="ExternalOutput").ap()
        return self.dout[name]

    def mm(self, out, lhsT, rhs, start, stop):
        self.P.add("pe", lambda e: e.matmul(out, lhsT=lhsT, rhs=rhs, start=start, stop=stop),
                   reads=[lhsT, rhs], writes=[out], is_mm=True)

    def tr(self, out, in_, ident):
        self.P.add("pe", lambda e: e.transpose(out, in_, ident), reads=[in_, ident], writes=[out], is_mm=True)

    def act(self, out, in_, func, bias=None, scale=None, eng="act"):
        kw = {}
        rd = [in_]
        if bias is not None:
            kw["bias"] = bias
            if not isinstance(bias, (int, float)):
                rd.append(bias)
        if scale is not None:
            kw["scale"] = scale
            if not isinstance(scale, (int, float)):
                rd.append(scale)
        self.P.add(eng, lambda e: e.activation(out=out, in_=in_, func=func, **kw), reads=rd, writes=[out])

    def tt(self, out, in0, in1, op, eng="dve"):
        self.P.add(eng, lambda e: e.tensor_tensor(out=out, in0=in0, in1=in1, op=op), reads=[in0, in1], writes=[out])

    def ts(self, out, in0, s1, s2, op0, op1=None, eng="dve"):
        rd = [in0]
        for s in (s1, s2):
            if s is not None and not isinstance(s, (int, float)):
                rd.append(s)
        if op1 is None:
            self.P.add(eng, lambda e: e.tensor_scalar(out=out, in0=in0, scalar1=s1, scalar2=None, op0=op0), reads=rd, writes=[out])
        else:
            self.P.add(eng, lambda e: e.tensor_scalar(out=out, in0=in0, scalar1=s1, scalar2=s2, op0=op0, op1=op1), reads=rd, writes=[out])

    def stt(self, out, in0, scalar, in1, op0, op1, eng="dve"):
        rd = [in0, in1]
        if not isinstance(scalar, (int, float)):
            rd.append(scalar)
        self.P.add(eng, lambda e: e.scalar_tensor_tensor(out=out, in0=in0, scalar=scalar, in1=in1, op0=op0, op1=op1),
                   reads=rd, writes=[out])

    def cp(self, out, in_, eng="dve"):
        if eng == "act":
            self.P.add(eng, lambda e: e.copy(out=out, in_=in_), reads=[in_], writes=[out])
        else:
            self.P.add(eng, lambda e: e.tensor_copy(out=out, in_=in_), reads=[in_], writes=[out])

    def memset(self, out, v, eng="dve"):
        self.P.add(eng, lambda e: e.memset(out, v), writes=[out])

    def scan(self, out, d0, d1, init=0.0):
        self.P.add("dve", lambda e: e.tensor_tensor_scan(out=out, data0=d0, data1=d1, initial=init, op0=ALU.mult, op1=ALU.add),
                   reads=[d0, d1], writes=[out])

    def dma(self, q, out, in_):
        self.P.add(q, lambda e: e.dma_start(out=out, in_=in_), reads=[in_], writes=[out], is_dma=True)

    def bank(self):
        b = self.banks[self.nbank % 8]
        self.nbank += 1
        return b

    def build(self):
        nc = self.nc
        L = self.L
        xT_in = self.inp("xT", [D, T])
        cond_in = self.inp("cond", [128, 8])
        w_ada = self.inp("w_ada", [L, D, 9 * D])
        b_adaT = self.inp("b_adaT", [128, L, 72])
        norm_wT = self.inp("norm_wT", [128, L, 3, 8])
        fin_wT = self.inp("fin_wT", [128, 8])
        ffn_in = self.inp("ffn_in", [L, 2, D, 2 * DFF])
        ffn_out = self.inp("ffn_out", [L, 2, DFF, D])
        yT_out = self.outp("yT", [D, T])
        if self.do_mixer:
            self.mixer_decl()

        self.xT = nc.alloc_sbuf_tensor("xTs", [128, 8, T], F32)
        self.hT = nc.alloc_sbuf_tensor("hTs", [128, 8, T], BF16)
        self.slots = [nc.alloc_sbuf_tensor("slot%d" % i, [128, 4096], BF16) for i in range(3)]
        self.nslot = 0
        self.AR = nc.alloc_sbuf_tensor("arena", [128, 22 * 1024], F32)
        self.small = nc.alloc_sbuf_tensor("small", [128, 548], F32)
        self.rstd = nc.alloc_sbuf_tensor("rstd", [128, T], F32)
        self.tmpf = [nc.alloc_sbuf_tensor("tmpf%d" % i, [128, T], F32) for i in range(2)]
        self.ones = nc.alloc_sbuf_tensor("ones", [128, 128], BF16)
        self.banks = [nc.alloc_psum_tensor("bank%d" % i, [128, 512], F32) for i in range(8)]
        self.ntmp = 0
        sm = self.small
        self.cond = sm[:, 0:8]
        self.scond = sm[:, 8:16]
        self.modT = sm[:, 16:88]
        self.badaT = sm[:, 88:88 + 72 * L].rearrange("p (l j) -> p l j", l=L)
        o = 88 + 72 * 4
        self.normw = sm[:, o:o + 24 * L].rearrange("p (l i c) -> p l i c", l=L, i=3)
        o += 24 * 4
        self.finw = sm[:, o:o + 8]
        o += 8
        self.Ascale = sm[:, o:o + 24].rearrange("p (i c) -> p i c", i=3)
        o += 24
        self.gates = sm[:, o:o + 24].rearrange("p (i c) -> p i c", i=3)
        o += 24
        self.scondb = nc.alloc_sbuf_tensor("scondb", [128, 8], BF16)
        self.modT2 = nc.alloc_sbuf_tensor("modT2", [128, 72], F32)
        self.modTs = [self.modT, self.modT2[:, :]]
        self.eps1k = sm[:, o:o + 1]
        o += 1
        self.small_o = o

        self.dma("sp", self.xT[:], xT_in.rearrange("(c p) t -> p c t", p=128))
        self.dma("sp", self.cond, cond_in)
        self.dma("sp", self.badaT, b_adaT)
        self.dma("sp", self.normw, norm_wT)
        self.dma("sp", self.finw, fin_wT)
        self.memset(self.ones[:], 1.0)
        self.memset(self.eps1k, D * EPS)
        if self.do_mixer:
            self.mixer_init()
        self.act(self.scond, self.cond, AF.Silu)
        self.cp(self.scondb[:], self.scond)

        for _ in self.ada_steps(0, w_ada):
            pass
        for l in range(L):
            self.ada_finish(l)
            self.norm(0, self.normw[:, l, 0, :])
            self.ffn(ffn_in[l, 0], ffn_out[l, 0], self.gates[:, 0, :])
            if self.do_mixer:
                self.norm(1, self.normw[:, l, 1, :])
                self.mixer(l)
            self.norm(2, self.normw[:, l, 2, :])
            nxt = self.ada_steps(l + 1, w_ada) if l + 1 < L else None
            self.ffn(ffn_in[l, 1], ffn_out[l, 1], self.gates[:, 2, :], extra=nxt)
        self.rms_stats()
        fa = sm[:, self.small_o:self.small_o + 8]
        self.ts(fa, self.finw, 32.0, None, ALU.mult)
        yv = self.AR[:, 0:8 * T].rearrange("p (c t) -> p c t", c=8)
        for c in range(8):
            self.stt(yv[:, c, :], self.xT[:, c, :], fa[:, c:c + 1], self.rstd[:], ALU.mult, ALU.mult)
        self.dma("sp", yT_out.rearrange("(c p) t -> p c t", p=128), yv)
        self.P.finalize()

    def next_slot(self):
        s = self.slots[self.nslot % len(self.slots)]
        self.nslot += 1
        return s

    def ada_steps(self, l, w_ada):
        modT = self.modTs[l % 2]
        for j4 in range(18):
            ptag = self.P.tag if hasattr(self.P, "tag") else ""
            self.P.tag = "ada"
            slot = self.next_slot()
            sv = slot[:, :].rearrange("p (k n) -> p k n", k=8)
            self.dma("pool", sv, w_ada[l].rearrange("(k p) n -> p k n", p=128)[:, :, j4 * 512:(j4 + 1) * 512])
            psm = self.bank()
            for jj in range(4):
                for k in range(8):
                    self.mm(psm[:, jj:jj + 1], sv[:, k, jj * 128:(jj + 1) * 128], self.scondb[:, k:k + 1], k == 0, k == 7)
            self.tt(modT[:, j4 * 4:(j4 + 1) * 4], psm[:, 0:4], self.badaT[:, l, j4 * 4:(j4 + 1) * 4], ALU.add)
            self.P.tag = ptag
            yield

    def ada_finish(self, l):
        mod = self.modTs[l % 2].rearrange("p (m c) -> p m c", m=9)
        for i in range(3):
            self.stt(self.Ascale[:, i, :], mod[:, 3 * i + 1, :], 1.0, self.normw[:, l, i, :], ALU.add, ALU.mult)
            self.ts(self.Ascale[:, i, :], self.Ascale[:, i, :], 32.0, None, ALU.mult)
        self.ts(self.gates[:, 0, :], mod[:, 2, :], 0.5, None, ALU.mult)
        self.cp(self.gates[:, 1, :], mod[:, 5, :])
        self.ts(self.gates[:, 2, :], mod[:, 8, :], 0.5, None, ALU.mult)
        self.mod = mod

    def rms_stats(self):
        self.P.tag = "norm"
        sq = self.hT
        for c in range(8):
            if c % 2 == 0:
                self.act(sq[:, c, :], self.xT[:, c, :], AF.Square)
            else:
                self.tt(sq[:, c, :], self.xT[:, c, :], self.xT[:, c, :], ALU.mult)
        for th in range(2):
            ps = self.bank()
            for c in range(8):
                self.mm(ps[:, :], self.ones[:], sq[:, c, th * 512:(th + 1) * 512], c == 0, c == 7)
            rs = self.rstd[:, th * 512:(th + 1) * 512]
            self.act(rs, ps[:, :], AF.Ln, bias=self.eps1k, scale=1.0)
            self.act(rs, rs, AF.Exp, scale=-0.5)

    def norm(self, i, nw):
        self.rms_stats()
        for c in range(8):
            tmp = self.tmpf[self.ntmp % 2]
            self.ntmp += 1
            self.stt(tmp[:], self.xT[:, c, :], self.Ascale[:, i, c:c + 1], self.rstd[:], ALU.mult, ALU.mult)
            self.act(self.hT[:, c, :], tmp[:], AF.Identity, bias=self.mod[:, 3 * i, c:c + 1], scale=1.0)

    def ffn(self, w_in, w_out, gate, extra=None):
        self.P.tag = "ffn_in"
        actT = self.AR[:, 0:NF * 512].bitcast(BF16).rearrange("p (f t) -> p f t", f=NF)
        wv = w_in.rearrange("(k p) (b j c) -> p k b j c", p=128, b=2, j=11)
        for j in range(11):
            slot = self.next_slot()
            sv = slot[:, :].rearrange("p (k b c) -> p k b c", k=8, b=2)
            for b in range(2):
                self.dma("pool", sv[:, :, b, :], wv[:, :, b, j, :])
            for fc in range(2):
                f = 2 * j + fc
                pg = [self.bank(), self.bank()]
                pu = [self.bank(), self.bank()]
                for k in range(8):
                    for tt_ in range(2):
                        self.mm(pg[tt_][:, :], sv[:, k, 0, fc * 128:(fc + 1) * 128], self.hT[:, k, tt_ * 512:(tt_ + 1) * 512], k == 0, k == 7)
                for k in range(8):
                    for tt_ in range(2):
                        self.mm(pu[tt_][:, :], sv[:, k, 1, fc * 128:(fc + 1) * 128], self.hT[:, k, tt_ * 512:(tt_ + 1) * 512], k == 0, k == 7)
                for tt_ in range(2):
                    tmp = self.tmpf[self.ntmp % 2]
                    self.ntmp += 1
                    self.act(tmp[:, 0:512], pg[tt_][:, :], AF.Silu)
                    self.tt(actT[:, f, tt_ * 512:(tt_ + 1) * 512], tmp[:, 0:512], pu[tt_][:, :], ALU.mult)
            if extra is not None:
                next(extra, None)
        self.P.tag = "ffn_out"
        wo = w_out.rearrange("(f p) d -> p f d", p=128)
        for dc in range(8):
            slot = self.next_slot()
            sv = slot[:, 0:NF * 128].rearrange("p (f d) -> p f d", f=NF)
            self.dma("pool", sv, wo[:, :, dc * 128:(dc + 1) * 128])
            po = [self.bank(), self.bank()]
            for f in range(NF):
                for tt_ in range(2):
                    self.mm(po[tt_][:, :], sv[:, f, :], actT[:, f, tt_ * 512:(tt_ + 1) * 512], f == 0, f == NF - 1)
            for tt_ in range(2):
                xs = self.xT[:, dc, tt_ * 512:(tt_ + 1) * 512]
                self.stt(xs, po[tt_][:, :], gate[:, dc:dc + 1], xs, ALU.mult, ALU.add)
            if extra is not None:
                next(extra, None)
        if extra is not None:
            for _ in extra:
                pass


def _prep_common(inputs, L):
    f = lambda a: np.ascontiguousarray(np.asarray(a, dtype=np.float32))
    com = {}
    com["w_ada"] = f(inputs["w_ada"])[:L]
    com["b_adaT"] = f(np.asarray(inputs["b_ada"])[:L].reshape(L, 72, 128).transpose(2, 0, 1))
    com["norm_wT"] = f(np.asarray(inputs["norm_w"])[:L].reshape(L, 3, 8, 128).transpose(3, 0, 1, 2))
    com["fin_wT"] = f(np.asarray(inputs["final_norm_w"]).reshape(8, 128).T)
    com["ffn_in"] = f(inputs["ffn_in"])[:L]
    com["ffn_out"] = f(inputs["ffn_out"])[:L]
    return com


def core_assign():
    return [("s", 0), ("s", 1), ("p", 0), ("p", 1), ("p", 2), ("p", 3), ("p", 3), ("p", 3)]

AW = 22 * 1024


def _mixer_decl(self):
    L = self.L
    i = self.inp
    self.w_in = i("w_in", [L, D, 3840])
    self.w_out = i("w_out", [L, D, D])
    self.w_inp = i("w_inp", [L, D, 512])
    self.d_cont = i("cont", [128, 1])
    self.d_ropeC = i("ropeC", [128, T])
    self.d_ropeS = i("ropeS", [128, T])
    self.d_smask = i("smask", [128, T])
    self.d_cm = i("cm", [128, 2, 512], BF16)
    self.d_cmr = i("cmr", [128, 2, 512], BF16)
    self.d_rfa = i("rfa", [128, 2, 2, 3, 4])
    self.d_rF8 = i("rF8", [128, 2, 2, 8])
    self.d_gvec8 = i("gvec8", [128, 8])
    self.d_chm = i("chm", [128, 4])
    self.d_gvec = i("gvec", [128, 32])
    self.d_ident = i("ident", [128, 128], BF16)
    self.d_bones = i("bones", [128, 128], BF16)
    self.d_retT = i("retT", [128, 2, 2, 3, 32])
    self.d_retF = i("retF", [128, 2, 2, 32])
    self.d_lb = i("hgrn_lbT", [128, 4, 4])
    self.d_hnw = i("hnwT", [128, L])
    self.d_conv = i("convT", [128, L, 3, 12])
    self.d_hbias = i("hbiasT", [128, L, 4])
    self.d_zT = i("zT", [33, T])
    self.d_w1 = i("hy_w1", [L, 33, 64])
    self.d_w2 = i("hy_w2", [L, 64, 64])
    self.d_w3 = i("hy_w3", [L, 64, 1024])
    self.d_fb = i("hy_fb", [64, L, 5])
    self.d_delta = i("deltab", [128, 512])
    self.d_negtl = i("negtl", [128, 8])
    self.d_m0 = i("m0", [128, 8])
    self.d_Cf = i("Cf", [T, T], BF16)
    self.d_Sf = i("Sf", [T, T], BF16)
    self.d_Ci = i("CiT", [T, T], BF16)
    self.d_Si = i("SiT", [T, T], BF16)
    self.d_s0 = i("s0", [128, 4, L, 2, 64])
    self.o_sth = self.outp("sth", [4, L, 2, 4, 64, 64])
    self.o_str = self.outp("str", [4, L, 2, 4, 64, 64])


def _mixer_init(self):
    nc = self.nc
    L = self.L
    a = lambda n, s, dt=F32: nc.alloc_sbuf_tensor("s_" + n, s, dt)
    self.ropeC = a("ropeC", [128, T]); self.ropeS = a("ropeS", [128, T]); self.smask = a("smask", [128, T])
    self.cm = a("cm", [128, 2, 512], BF16)
    self.cm2 = self.cm[:, :, :].rearrange("p d (k h n) -> p d k h n", k=2, h=2)
    self.cmr = a("cmr", [128, 2, 512], BF16)
    self.cmr2 = self.cmr[:, :, :].rearrange("p d (k h n) -> p d k h n", k=2, h=2); self.chm = a("chm", [128, 4]); self.gvec = a("gvec", [128, 32])
    self.ident = a("ident", [128, 128], BF16); self.bones = a("bones", [128, 128], BF16)
    self.retT = a("retT", [128, 2, 2, 3, 32]); self.retF = a("retF", [128, 2, 2, 32])
    self.delta = a("delta", [128, 512]); self.zT = a("zTs", [33, T])
    self.ms = a("msmall", [128, 512])
    self.sfin = a("sfin", [128, 256])
    self.qx = [a("qx0", [128, 256], BF16), a("qx1", [128, 256], BF16)]
    self.memset(self.qx[0][:], 0.0); self.memset(self.qx[1][:], 0.0)
    self.s0 = a("s0s", [128, 4, 2, 64])
    self.hw1 = a("hw1", [33, L, 64]); self.hw2 = a("hw2", [64, L, 64])
    ms = self.ms
    o = 0

    def take(n):
        nonlocal o
        v = ms[:, o:o + n]
        o += n
        return v
    self.cont = take(1); self.cm1 = take(1)
    self.lbx = take(16).rearrange("p (a l) -> p a l", l=4)
    self.oml = take(16).rearrange("p (a l) -> p a l", l=4)
    self.lbs = take(8)
    self.lnoml = take(16).rearrange("p (a l) -> p a l", l=4)
    self.epsc = take(1)
    self.hnw = take(L); self.onesc = take(1)
    self.conv = take(L * 36).rearrange("p (l t c) -> p l t c", l=L, t=3)
    self.nconv = take(24).rearrange("p (t c) -> p t c", t=2)
    self.hbias = take(L * 4).rearrange("p (l c) -> p l c", l=L)
    self.fb = take(L * 5).rearrange("p (l c) -> p l c", l=L)
    self.fs = take(4)
    self.negtl = take(8); self.m0 = take(8)
    self.Fv = take(32); self.Fg = take(32)
    self.rfa = take(48).rearrange("p (g d k a) -> p g d k a", g=2, d=2, k=3)
    self.rF8 = take(32).rearrange("p (g d n) -> p g d n", g=2, d=2)
    self.gvec8 = take(8)
    q = "sp"
    for dst, src in ((self.ropeC[:], self.d_ropeC), (self.ropeS[:], self.d_ropeS), (self.smask[:], self.d_smask),
                     (self.cm[:], self.d_cm), (self.cmr[:], self.d_cmr), (self.rfa, self.d_rfa), (self.rF8, self.d_rF8), (self.gvec8, self.d_gvec8), (self.chm[:], self.d_chm), (self.gvec[:], self.d_gvec),
                     (self.ident[:], self.d_ident), (self.bones[:], self.d_bones), (self.retT[:], self.d_retT),
                     (self.retF[:], self.d_retF), (self.delta[:], self.d_delta), (self.zT[:], self.d_zT),
                     (self.cont, self.d_cont), (self.lbx, self.d_lb), (self.hnw, self.d_hnw), (self.conv, self.d_conv),
                     (self.hbias, self.d_hbias), (self.fb[0:64], self.d_fb), (self.negtl, self.d_negtl), (self.m0, self.d_m0),
                     (self.hw1[:], self.d_w1.rearrange("l k n -> k l n")),
                     (self.hw2[:], self.d_w2.rearrange("l k n -> k l n"))):
        self.dma(q, dst, src)
    self.memset(self.onesc, 1.0)
    self.ts(self.cm1, self.cont, -1.0, None, ALU.add)
    self.act(self.lbx, self.lbx, AF.Exp)
    self.P.add("dve", lambda e: e.reduce_sum(out=self.lbs[:, 0:4], in_=self.lbx, axis=AX.X), reads=[self.lbx], writes=[self.lbs[:, 0:4]])
    self.P.add("dve", lambda e: e.reciprocal(out=self.lbs[:, 4:8], in_=self.lbs[:, 0:4]), reads=[self.lbs[:, 0:4]], writes=[self.lbs[:, 4:8]])
    self.tt(self.lbx, self.lbx, self.lbs[:, 4:8].unsqueeze(2).broadcast_to([128, 4, 4]), ALU.mult)
    self.memset(self.oml[:, :, 0:1], 1.0)
    for l in range(1, 4):
        self.tt(self.oml[:, :, l:l + 1], self.oml[:, :, l - 1:l], self.lbx[:, :, l:l + 1], ALU.subtract)
    self.act(self.lnoml, self.oml, AF.Ln)
    self.memset(self.epsc, EPS)


def _slotload(self, pieces):
    N = sum(p[1] for p in pieces)
    slot = self.next_slot()
    sv = slot[:, 0:8 * N].rearrange("p (k n) -> p k n", k=8)
    o = 0
    for src, n, q in pieces:
        self.dma(q, sv[:, :, o:o + n], src)
        o += n
    return sv


def _wcols(self, w, c0, n):
    return (w.rearrange("(k p) n -> p k n", p=128)[:, :, c0:c0 + n], n, "pool")


def _mixer(self, l):
    AR = self.AR
    L = self.L
    W = self.w_in[l]
    o = 0

    def fw(n):
        nonlocal o
        v = AR[:, o:o + n]
        o += n
        return v
    mixT = fw(4096).bitcast(BF16).rearrange("p (c t) -> p c t", c=8)
    v_tm = fw(2048).bitcast(BF16).rearrange("p (k n) -> p k n", k=8)
    base = o
    self.dma("sp", self.s0[:], self.d_s0[:, :, l, :, :])
    self.P.tag = "mix_v"
    sv = _slotload(self, [_wcols(self, W, 256, 256), _wcols(self, W, 3328, 256)])
    for tile in range(8):
        ps = self.bank()
        for k in range(8):
            self.mm(ps[:, :], self.hT[:, k, tile * 128:(tile + 1) * 128], sv[:, k, :], k == 0, k == 7)
        self.cp(v_tm[:, tile, :], ps[:, :], eng="act")

    import os
    PARTS = os.environ.get('MIX_PARTS', 'hy,gla')
    if 'hy' in PARTS:
        Gr = fw(2048).bitcast(BF16).rearrange("p (k n) -> p k n", k=8)
        Gi = fw(2048).bitcast(BF16).rearrange("p (k n) -> p k n", k=8)
        hb = o
        S_tm = fw(2048).bitcast(BF16).rearrange("p (k n) -> p k n", k=8)
        D_tm = fw(2048).bitcast(BF16).rearrange("p (k n) -> p k n", k=8)
        hid = [fw(1024), fw(1024)]
        hw3 = fw(1024)[0:64, :]
        self.P.tag = "hy_filt"
        self.dma("sp", hw3, self.d_w3[l])
        f3 = self.fs
        self.ts(f3[0:64, 0:1], self.fb[0:64, l, 2:3], 1.0 / 3.0, None, ALU.mult)
        self.tt(f3[0:64, 1:2], f3[0:64, 0:1], self.fb[0:64, l, 0:1], ALU.mult)
        self.tt(f3[0:64, 2:3], f3[0:64, 0:1], self.fb[0:64, l, 1:2], ALU.mult)

        def sin3(dst, ps, bcol):
            tmp = self.tmpf[self.ntmp % 2]
            self.ntmp += 1
            s = tmp[0:64, 0:512]
            s2 = tmp[0:64, 512:1024]
            self.act(s, ps, AF.Sin, bias=f3[0:64, bcol:bcol + 1], scale=f3[0:64, 0:1])
            self.tt(s2, s, s, ALU.mult)
            self.ts(s2, s2, -4.0, 3.0, ALU.mult, ALU.add)
            self.tt(dst, s2, s, ALU.mult)
        for th in range(2):
            ps = self.bank()
            self.mm(ps[0:64, :], self.hw1[:, l, :], self.zT[:, th * 512:(th + 1) * 512], True, True)
            sin3(hid[0][0:64, th * 512:(th + 1) * 512], ps[0:64, :], 1)
        for th in range(2):
            ps = self.bank()
            self.mm(ps[0:64, :], self.hw2[:, l, :], hid[0][0:64, th * 512:(th + 1) * 512], True, True)
            sin3(hid[1][0:64, th * 512:(th + 1) * 512], ps[0:64, :], 2)
        for tile in range(8):
            pf = self.bank()
            pb = self.bank()
            self.mm(pf[:, :], hid[1][0:64, tile * 128:(tile + 1) * 128], hw3[:, 0:512], True, True)
            self.mm(pb[:, :], hid[1][0:64, tile * 128:(tile + 1) * 128], hw3[:, 512:1024], True, True)
            t0 = self.tmpf[self.ntmp % 2]; self.ntmp += 1
            t1 = self.tmpf[self.ntmp % 2]; self.ntmp += 1
            dec = t0[:, 0:512]
            self.act(dec, self.delta[:], AF.Exp, scale=self.negtl[:, tile:tile + 1])
            self.tt(t0[:, 512:1024], pf[:, :], dec, ALU.mult)
            self.stt(t1[:, 0:512], pb[:, :], self.m0[:, tile:tile + 1], dec, ALU.mult, ALU.mult)
            self.tt(S_tm[:, tile, :], t0[:, 512:1024], t1[:, 0:512], ALU.add)
            self.tt(D_tm[:, tile, :], t0[:, 512:1024], t1[:, 0:512], ALU.subtract)
        for (tab, X, G) in ((self.d_Cf, S_tm, Gr), (self.d_Sf, D_tm, Gi)):
            tv = tab.rearrange("(k p) f -> p k f", p=128)
            for fh in range(2):
                sv = _slotload(self, [(tv[:, :, fh * 512:(fh + 1) * 512], 512, "sp")])
                for fc in range(4):
                    ps = self.bank()
                    for k in range(8):
                        self.mm(ps[:, :], sv[:, k, fc * 128:(fc + 1) * 128], X[:, k, :], k == 0, k == 7)
                    self.cp(G[:, fh * 4 + fc, :], ps[:, :], eng="act")
        self.P.tag = "hy_chunk"
        o = hb
        x0b = fw(2048).bitcast(BF16).rearrange("p (c t) -> p c t", c=4)
        ubf = fw(2048).bitcast(BF16).rearrange("p (c t) -> p c t", c=4)
        u_tm = fw(2048).bitcast(BF16).rearrange("p (k n) -> p k n", k=8)
        tb = o
        x0a = fw(1024); a1 = fw(1024); a2 = fw(1024); hraw = [fw(1024), fw(1024)]
        o = tb
        Ur = fw(2048).rearrange("p (k n) -> p k n", k=4)
        Pr = fw(2048).bitcast(BF16).rearrange("p (k n) -> p k n", k=8)
        Pi = fw(2048).bitcast(BF16).rearrange("p (k n) -> p k n", k=8)
        assert o <= AW, o
        self.ts(self.nconv[:, 0, :], self.conv[:, l, 0, :], self.cm1, None, ALU.mult)
        self.ts(self.nconv[:, 1, :], self.conv[:, l, 2, :], self.cm1, None, ALU.mult)
        nh = 0
        for j in range(4):
            sv = _slotload(self, [_wcols(self, W, 1280 + j * 128, 128), _wcols(self, W, 1792 + j * 128, 128),
                                  _wcols(self, W, 2304 + j * 128, 128)])
            accs = (x0a, a1, a2)
            for cc in range(3):
                hr = hraw[nh % 2]; nh += 1
                for th in range(2):
                    ps = self.bank()
                    for k in range(8):
                        self.mm(ps[:, :], sv[:, k, cc * 128:(cc + 1) * 128], self.hT[:, k, th * 512:(th + 1) * 512], k == 0, k == 7)
                    self.cp(hr[:, th * 512:(th + 1) * 512], ps[:, :], eng="act")
                ch = cc * 4 + j
                acc = accs[cc]
                self.act(acc, hr, AF.Copy, scale=self.conv[:, l, 1, ch:ch + 1])
                self.stt(acc[:, 1:T], hr[:, 0:T - 1], self.conv[:, l, 0, ch:ch + 1], acc[:, 1:T], ALU.mult, ALU.add)
                self.stt(acc[:, 0:T - 1], hr[:, 1:T], self.conv[:, l, 2, ch:ch + 1], acc[:, 0:T - 1], ALU.mult, ALU.add)
                self.stt(acc[:, 256:T:256], hr[:, 255:T - 1:256], self.nconv[:, 0, ch:ch + 1], acc[:, 256:T:256], ALU.mult, ALU.add)
                self.stt(acc[:, 255:T - 1:256], hr[:, 256:T:256], self.nconv[:, 1, ch:ch + 1], acc[:, 255:T - 1:256], ALU.mult, ALU.add)
            self.cp(x0b[:, j, :], x0a, eng="act")
            self.tt(ubf[:, j, :], a1, a2, ALU.mult)
        for j in range(4):
            pt = self.bank()
            ptb = pt[:, :].bitcast(BF16)
            for k in range(8):
                self.tr(ptb[:, k * 128:(k + 1) * 128], ubf[:, j, k * 128:(k + 1) * 128], self.ident[:])
            self.cp(u_tm[:, :, j * 128:(j + 1) * 128], ptb.rearrange("p (k n) -> p k n", k=8), eng=("act" if j % 2 == 0 else "dve"))
        tvC = self.d_Cf.rearrange("(k p) f -> p k f", p=128)
        tvS = self.d_Sf.rearrange("(k p) f -> p k f", p=128)
        for fh in range(2):
            sv = _slotload(self, [(tvC[:, :, fh * 512:(fh + 1) * 512], 512, "sp")])
            for fc in range(4):
                ps = self.bank()
                for k in range(8):
                    self.mm(ps[:, :], sv[:, k, fc * 128:(fc + 1) * 128], u_tm[:, k, :], k == 0, k == 7)
                self.cp(Ur[:, fc, :], ps[:, :], eng="act")
            sv = _slotload(self, [(tvS[:, :, fh * 512:(fh + 1) * 512], 512, "sp")])
            for fc in range(4):
                fi = fh * 4 + fc
                ps = self.bank()
                for k in range(8):
                    self.mm(ps[:, :], sv[:, k, fc * 128:(fc + 1) * 128], u_tm[:, k, :], k == 0, k == 7)
                Ui = ps[:, :]
                t0 = self.tmpf[0]; t1 = self.tmpf[1]
                self.tt(t0[:, 0:512], Gr[:, fi, :], Ur[:, fc, :], ALU.mult)
                self.tt(t0[:, 512:1024], Ui, Gi[:, fi, :], ALU.mult)
                self.tt(Pr[:, fi, :], t0[:, 0:512], t0[:, 512:1024], ALU.subtract)
                self.tt(t1[:, 0:512], Ui, Gr[:, fi, :], ALU.mult)
                self.tt(t1[:, 512:1024], Gi[:, fi, :], Ur[:, fc, :], ALU.mult)
                self.tt(Pi[:, fi, :], t1[:, 0:512], t1[:, 512:1024], ALU.add)
        tvCi = self.d_Ci.rearrange("(k p) t -> p k t", p=128)
        tvSi = self.d_Si.rearrange("(k p) t -> p k t", p=128)
        for th in range(2):
            sl = slice(th * 512, (th + 1) * 512)
            svc = _slotload(self, [(tvCi[:, :, sl], 512, "sp")])
            svs = _slotload(self, [(tvSi[:, :, sl], 512, "sp")])
            for j in range(4):
                py = self.bank()
                for k in range(8):
                    self.mm(py[:, :], Pr[:, k, j * 128:(j + 1) * 128], svc[:, k, :], k == 0, False)
                for k in range(8):
                    self.mm(py[:, :], Pi[:, k, j * 128:(j + 1) * 128], svs[:, k, :], False, k == 7)
                t0 = self.tmpf[j % 2]
                self.stt(t0[:, 0:512], ubf[:, j, sl], self.hbias[:, l, j:j + 1], py[:, :], ALU.mult, ALU.add)
                self.tt(mixT[:, 2 + j, sl], t0[:, 0:512], x0b[:, j, sl], ALU.mult)

    if 'gla' in PARTS:
        o = base
        qf = fw(1024); kf = fw(1024); lg = fw(1024); bb = fw(1024); E = fw(1024)
        qt = fw(512).bitcast(BF16); kt = fw(512).bitcast(BF16); kh = fw(512).bitcast(BF16)
        kh_tm = fw(512).bitcast(BF16).rearrange("p (k n) -> p k n", k=8)
        KV = fw(2112); Fr = fw(2112)
        SP = fw(2048).bitcast(BF16).rearrange("p (n m) -> p n m", n=32)
        osb = fw(1024)
        vpad = fw(1024).bitcast(BF16).rearrange("p (k h m) -> p k h m", k=8, h=2)
        vx = [fw(256).bitcast(BF16), fw(256).bitcast(BF16)]
        Ab = [fw(128).bitcast(BF16).rearrange("p (h n) -> p h n", h=2), fw(128).bitcast(BF16).rearrange("p (h n) -> p h n", h=2)]
        Sfin = self.sfin[:, :].rearrange("p (b v) -> p b v", b=4)
        assert o <= AW, o
        KV3 = KV.rearrange("p (d n) -> p d n", n=33)
        Fr3 = Fr.rearrange("p (d n) -> p d n", n=33)
        self.memset(SP[:, :, :], 0.0)
        self.memset(vpad[:, :, :, :], 0.0)
        self.memset(Fr3[:, :, 0:1], 0.0)
        nvx = 0
        for g in range(4):
            ret = g >= 2
            gp = g - 2
            vc0 = (256 + gp * 128) if ret else g * 128
            self.P.tag = "gla_proj"
            for h in range(2):
                self.cp(vpad[:, :, h, h * 64:(h + 1) * 64], v_tm[:, :, vc0 + h * 64:vc0 + (h + 1) * 64], eng="act")
            if ret:
                sv = _slotload(self, [_wcols(self, W, 2816 + gp * 128, 128), _wcols(self, W, 3072 + gp * 128, 128),
                                      _wcols(self, self.w_inp[l], gp * 128, 128), _wcols(self, self.w_inp[l], 256 + gp * 128, 128)])
            else:
                sv = _slotload(self, [_wcols(self, W, g * 128, 128), _wcols(self, W, 512 + g * 128, 128),
                                      _wcols(self, W, 768 + g * 128, 128), _wcols(self, W, 1024 + g * 128, 128)])
                svg, gcc = sv, 3
            if ret:
                svg, gcc = _slotload(self, [_wcols(self, W, 3584 + gp * 128, 128)]), 0
            mc = (6 + gp) if ret else g

            def proj(cc, th):
                ps = self.bank()
                for k in range(8):
                    self.mm(ps[:, :], sv[:, k, cc * 128:(cc + 1) * 128], self.hT[:, k, th * 512:(th + 1) * 512], k == 0, k == 7)
                return ps
            if ret:
                for (dst, c_a, c_b, sc) in ((qf, 0, 2, 1.0), (kf, 1, 3, 0.125)):
                    for th in range(2):
                        sl = slice(th * 512, (th + 1) * 512)
                        pa = proj(c_a, th)
                        pb_ = proj(c_b, th)
                        t0 = self.tmpf[self.ntmp % 2]; self.ntmp += 1
                        self.tt(t0[:, 0:512], pa[:, :], self.ropeC[:, sl], ALU.mult)
                        self.tt(t0[:, 512:1024], pb_[:, :], self.ropeS[:, sl], ALU.mult)
                        self.tt(t0[:, 0:512], t0[:, 0:512], t0[:, 512:1024], ALU.add)
                        self.ts(dst[:, sl], t0[:, 0:512], sc, None, ALU.mult)
            else:
                for th in range(2):
                    sl = slice(th * 512, (th + 1) * 512)
                    pa = proj(0, th)
                    self.act(qf[:, sl], pa[:, :], AF.Silu)
            for th in range(2):
                sl = slice(th * 512, (th + 1) * 512)
                pg = self.bank()
                for k in range(8):
                    self.mm(pg[:, :], svg[:, k, gcc * 128:(gcc + 1) * 128], self.hT[:, k, sl], k == 0, k == 7)
                self.act(mixT[:, mc, sl], pg[:, :], AF.Silu)
            for dr in [int(x) for x in os.environ.get('GLA_DIRS', '0,1').split(',')]:
                self.P.tag = "gla_A"
                q3 = qf.rearrange("p (n j) -> p n j", j=32)
                k3 = kf.rearrange("p (n j) -> p n j", j=32)
                qt3 = qt.rearrange("p (n j) -> p n j", j=32)
                kt3 = kt.rearrange("p (n j) -> p n j", j=32)
                kh3 = kh.rearrange("p (n j) -> p n j", j=32)
                if ret:
                    tab = lambda kind: self.retT[:, gp, dr, kind:kind + 1, :].broadcast_to([128, 32, 32])
                    fa4 = lambda kind: self.rfa[:, gp, dr, kind, :].unsqueeze(1).unsqueeze(3).broadcast_to([128, 8, 4, 32])
                    v4 = lambda v: v.rearrange("p (t a b) -> p t a b", a=4, b=32)
                    E3_ = E.rearrange("p (n j) -> p n j", j=32)
                    for (dst_, src3, kind) in ((qt, q3, 0), (kt, k3, 1), (kh, k3, 2)):
                        self.tt(E3_, src3, tab(kind), ALU.mult)
                        self.tt(v4(dst_), v4(E), fa4(kind), ALU.mult)
                    Fv = self.rF8[:, gp, dr, :]
                else:
                    a_idx = dr * 2 + g
                    b3 = bb.rearrange("p (n j) -> p n j", j=32)
                    tot = b3[:, :, 31:32]
                    HS = [slice(0, 512), slice(512, 1024)]
                    n3 = lambda v, th: v[:, HS[th]].rearrange("p (n j) -> p n j", j=32)
                    toth = lambda th: b3[:, th * 16:(th + 1) * 16, 31:32].broadcast_to([128, 16, 32])
                    t0 = self.tmpf[0]
                    lnoml = self.lnoml[:, a_idx, l:l + 1]
                    pzs = [proj(1 + dr, th) for th in range(2)]
                    for th in range(2):
                        self.act(t0[:, HS[th]], pzs[th][:, :], AF.Exp)
                    for th in range(2):
                        self.act(t0[:, HS[th]], t0[:, HS[th]], AF.Ln, bias=self.onesc, scale=1.0)
                    for th in range(2):
                        self.act(kf[:, HS[th]], t0[:, HS[th]], AF.Exp, bias=lnoml, scale=-1.0)
                    for th in range(2):
                        self.act(lg[:, HS[th]], kf[:, HS[th]], AF.Ln, bias=self.onesc, scale=-1.0)
                    for th in range(2):
                        self.scan(bb[:, HS[th]], self.smask[:, HS[th]], lg[:, HS[th]])
                    if dr == 0:
                        bc = bb
                    else:
                        for th in range(2):
                            self.tt(E[:, HS[th]], lg[:, HS[th]], bb[:, HS[th]], ALU.subtract)
                        for th in range(2):
                            self.tt(n3(lg, th), n3(E, th), toth(th), ALU.add)
                        bc = lg
                    for th in range(2):
                        self.act(E[:, HS[th]], bc[:, HS[th]], AF.Exp)
                    for th in range(2):
                        self.act(t0[:, HS[th]], bc[:, HS[th]], AF.Exp, scale=-1.0)
                    for th in range(2):
                        self.tt(qt[:, HS[th]], qf[:, HS[th]], E[:, HS[th]], ALU.mult)
                    for th in range(2):
                        self.tt(kt[:, HS[th]], kf[:, HS[th]], t0[:, HS[th]], ALU.mult)
                    for th in range(2):
                        self.tt(n3(E, th), toth(th), n3(bc, th), ALU.subtract)
                    for th in range(2):
                        self.act(E[:, HS[th]], E[:, HS[th]], AF.Exp)
                    for th in range(2):
                        self.tt(kh[:, HS[th]], kf[:, HS[th]], E[:, HS[th]], ALU.mult)
                    self.act(self.Fv.unsqueeze(2), tot, AF.Exp)
                    Fv = self.Fv
                if int(os.environ.get('GLA_STOP', '99')) <= 1:
                    continue
                self.P.tag = "gla_B"
                NC = 8 if ret else 32
                CPT = NC // 8
                BQ = NC // 4
                KVf = KV[:, 0:64 * (NC + 1)]
                Frf = Fr[:, 0:64 * (NC + 1)]
                KV3 = KVf.rearrange("p (d n) -> p d n", n=NC + 1)
                Fr3 = Frf.rearrange("p (d n) -> p d n", n=NC + 1)
                Fproc = Fv if dr == 0 else Fv[:, NC - 1::-1]
                Fg = self.Fg[:, 0:NC]
                self.tt(Fg, Fproc, (self.gvec8 if ret else self.gvec[:]), ALU.mult)
                self.memset(Fr3[:, :, 0:1], 0.0)
                self.cp(Fr3[:, :, 1:NC + 1], Fg.unsqueeze(1).broadcast_to([128, 64, NC]))
                self.cp(KV3[:, :, 0:1], self.s0[:, g, dr, :].unsqueeze(2))
                pt = self.bank()
                ptb = pt[:, :].bitcast(BF16)
                for k in range(8):
                    self.tr(ptb[:, k * 128:(k + 1) * 128], kh[:, k * 128:(k + 1) * 128], self.ident[:])
                self.cp(kh_tm[:, :, :], ptb.rearrange("p (k n) -> p k n", k=8), eng="act")
                if int(os.environ.get('GLA_STOP', '99')) <= 2:
                    continue
                for tile in (range(8) if ret else []):
                    ps = self.bank()
                    self.mm(ps[:, 0:128], kh_tm[:, tile, :], v_tm[:, tile, vc0:vc0 + 128], True, True)
                    for h in range(2):
                        hp = slice(h * 64, (h + 1) * 64)
                        slot_ = (1 + tile) if dr == 0 else (8 - tile)
                        self.cp(KV3[hp, :, slot_], ps[hp, h * 64:(h + 1) * 64], eng=("act" if h == 0 else "dve"))
                for tile in ([] if ret else range(8)):
                    vxt = vx[nvx % 2]; nvx += 1
                    v2 = v_tm[:, tile, vc0:vc0 + 128].rearrange("p (h d) -> p h d", h=2)
                    self.tt(vxt.rearrange("p (h c d) -> p h c d", h=2, c=4),
                            v2.unsqueeze(2).broadcast_to([128, 2, 4, 64]),
                            self.chm[:].unsqueeze(1).unsqueeze(3).broadcast_to([128, 2, 4, 64]), ALU.mult,
                            eng=("dve" if tile % 4 == 3 else "pool"))
                    ps = self.bank()
                    self.mm(ps[:, :], kh_tm[:, tile, :], vxt, True, True)
                    for h in range(2):
                        src = ps[h * 64:(h + 1) * 64, h * 256:(h + 1) * 256].rearrange("p (c d) -> p c d", c=4)
                        if dr == 0:
                            dst = KV3[h * 64:(h + 1) * 64, :, 1 + 4 * tile:5 + 4 * tile]
                        else:
                            hi = 32 - 4 * tile
                            dst = KV3[h * 64:(h + 1) * 64, :, hi:hi - 4:-1] if hi - 4 > 0 else KV3[h * 64:(h + 1) * 64, :, hi:0:-1]
                        self.cp(dst.transpose([0, 2, 1]), src, eng=("act" if h == 0 else "dve"))
                if int(os.environ.get('GLA_STOP', '99')) <= 3:
                    continue
                self.P.tag = "gla_C"
                self.scan(KVf, Frf, KVf)
                if int(os.environ.get('GLA_STOP', '99')) <= 4:
                    continue
                srcf = KV3[:, :, BQ:NC + 1:BQ] if dr == 0 else KV3[:, :, NC:0:-BQ]
                self.cp(Sfin, srcf.transpose([0, 2, 1]))
                od = (self.o_str if ret else self.o_sth)
                hh = (gp if ret else g) * 2
                self.dma("sp", od[:, l, dr, hh:hh + 2, :, :].rearrange("b h k v -> (h k) b v"), Sfin)
                if int(os.environ.get('GLA_STOP', '99')) <= 5:
                    continue
                for h in range(2):
                    s_ = KV3[h * 64:(h + 1) * 64, :, 0:NC] if dr == 0 else KV3[h * 64:(h + 1) * 64, :, NC - 1::-1]
                    self.cp(SP[h * 64:(h + 1) * 64, 0:NC, h * 64:(h + 1) * 64], s_.transpose([0, 2, 1]), eng=("act" if h == 0 else "dve"))
                if int(os.environ.get('GLA_STOP', '99')) <= 6:
                    continue
                for h in range(2):
                    hp = slice(h * 64, (h + 1) * 64)
                    if dr == 0:
                        gsrc = KV3[hp, :, BQ:NC:BQ]; gdst = SP[hp, BQ:NC:BQ, hp]
                    else:
                        gsrc = KV3[hp, :, NC - BQ:0:-BQ]; gdst = SP[hp, BQ - 1:NC - BQ:BQ, hp]
                    self.ts(gdst, gsrc.transpose([0, 2, 1]), self.cont[hp, :], None, ALU.mult)
                self.P.tag = "gla_D"
                qxa = self.rstd[:, :].bitcast(BF16).rearrange("p (k h n) -> p k h n", k=8, h=2)
                Aall = self.tmpf[1][:, :].bitcast(BF16).rearrange("p (k h n) -> p k h n", k=8, h=2)
                for h in range(2):
                    hp = slice(h * 64, (h + 1) * 64)
                    oth = slice((1 - h) * 64, (2 - h) * 64)
                    self.cp(qxa[hp, :, h, :], qt[hp, :].rearrange("p (k n) -> p k n", k=8), eng="act")
                    self.memset(qxa[oth, :, h, :], 0.0)
                for bp in range(4):
                    pa = self.bank()
                    for t2 in range(2):
                        tile = 2 * bp + t2
                        cs = slice(tile * 128, (tile + 1) * 128)
                        self.mm(pa[:, t2 * 256:(t2 + 1) * 256], kt[:, cs], qxa[:, tile, :, :], True, True)
                    self.tt(Aall[:, 2 * bp:2 * bp + 2, :, :], pa[:, :].rearrange("p (k h n) -> p k h n", k=2, h=2),
                            (self.cmr2 if ret else self.cm2)[:, dr, :, :, :], ALU.mult)
                for tile in range(8):
                    cs = slice(tile * 128, (tile + 1) * 128)
                    po = self.bank()
                    self.mm(po[:, 0:128], vpad[:, tile, 0, :], Aall[:, tile, 0, :], True, False)
                    self.mm(po[:, 0:128], vpad[:, tile, 1, :], Aall[:, tile, 1, :], False, False)
                    WC = 128 // CPT
                    for c in range(CPT):
                        n = CPT * tile + c
                        self.mm(po[:, WC * c:WC * (c + 1)], SP[:, n, :], qt[:, tile * 128 + WC * c:tile * 128 + WC * (c + 1)], False, c == CPT - 1)
                    if dr == 0:
                        self.cp(osb[:, cs], po[:, 0:128], eng="act")
                    else:
                        self.tt(osb[:, cs], osb[:, cs], po[:, 0:128], ALU.add)
            self.P.tag = "gla_epi"
            sq = qt
            self.act(sq, osb, AF.Square)
            nw = self.onesc if ret else self.hnw[:, l:l + 1]
            for th in range(2):
                sl = slice(th * 512, (th + 1) * 512)
                ps = self.bank()
                self.mm(ps[:, :], self.bones[:], sq[:, sl], True, True)
                rs = E[:, sl]
                self.act(rs, ps[:, :], AF.Ln, bias=self.epsc, scale=1.0 / 64.0)
                self.act(rs, rs, AF.Exp, scale=-0.5)
                t0 = self.tmpf[0]
                self.tt(t0[:, sl], osb[:, sl], rs, ALU.mult)
                self.stt(mixT[:, mc, sl], t0[:, sl], nw, mixT[:, mc, sl], ALU.mult, ALU.mult)

    self.P.tag = "wout"
    wo = self.w_out[l].rearrange("(k p) n -> p k n", p=128)
    for dh in range(2):
        sv = _slotload(self, [(wo[:, :, dh * 512:(dh + 1) * 512], 512, "pool")])
        for dcc in range(4):
            dc = dh * 4 + dcc
            for th in range(2):
                ps = self.bank()
                for k in range(8):
                    self.mm(ps[:, :], sv[:, k, dcc * 128:(dcc + 1) * 128], mixT[:, k, th * 512:(th + 1) * 512], k == 0, k == 7)
                xs = self.xT[:, dc, th * 512:(th + 1) * 512]
                self.stt(xs, ps[:, :], self.gates[:, 1, dc:dc + 1], xs, ALU.mult, ALU.add)


KB.mixer_decl = _mixer_decl
KB.mixer_init = _mixer_init
KB.mixer = _mixer

def _consts(kind):
    bf = ml_dtypes.bfloat16
    Ls = 1024 if kind == "s" else 256
    c = {}
    t = np.arange(T)
    tl = t % Ls
    cont = 1.0 if kind == "s" else 0.0
    c["cont"] = np.full((128, 1), cont, np.float32)
    d = np.arange(64)
    fr = 1.0 / (10000.0 ** (np.arange(0, 32, 2, dtype=np.float64) / 32.0))
    fidx = d % 16
    pos = np.where(d[:, None] < 32, (t // 64)[None, :], (t % 64)[None, :]).astype(np.float64)
    ang = pos * fr[fidx][:, None]
    sgn = np.where((d % 32) < 16, -1.0, 1.0)[:, None]
    if kind == "s":
        C = np.cos(ang); S = sgn * np.sin(ang)
    else:
        C = np.ones((64, T)); S = np.zeros((64, T))
    c["ropeC"] = np.tile(C, (2, 1)).astype(np.float32)
    c["ropeS"] = np.tile(S, (2, 1)).astype(np.float32)
    c["smask"] = np.tile((t % 32 != 0).astype(np.float32)[None], (128, 1))
    s_ = np.arange(128)[:, None]; t_ = np.arange(128)[None, :]
    same = (s_ // 32) == (t_ // 32)
    cm = np.stack([(same & (s_ <= t_)), (same & (s_ >= t_))], axis=1).astype(np.float32)
    c["cm"] = np.ascontiguousarray(np.concatenate([cm, cm, cm, cm], axis=2)).astype(bf)
    cmr = np.stack([(s_ <= t_), (s_ >= t_)], axis=1).astype(np.float32)
    c["cmr"] = np.ascontiguousarray(np.concatenate([cmr, cmr, cmr, cmr], axis=2)).astype(bf)
    gv8 = np.ones((128, 8), np.float32); gv8[:, [2, 4, 6]] = cont
    c["gvec8"] = gv8
    c["chm"] = ((np.arange(128)[:, None] // 32) == np.arange(4)[None, :]).astype(np.float32)
    gv = np.ones((128, 32), np.float32); gv[:, [8, 16, 24]] = cont
    c["gvec"] = gv
    c["ident"] = np.eye(128, dtype=np.float32).astype(bf)
    bo = np.zeros((128, 128), np.float32); bo[:64, :64] = 1; bo[64:, 64:] = 1
    c["bones"] = bo.astype(bf)
    lg_all = np.log1p(-np.exp2(-5.0 - 0.5 * np.arange(8, dtype=np.float64)))
    retT = np.zeros((128, 2, 2, 3, 32)); retF = np.zeros((128, 2, 2, 32))
    j = np.arange(32)
    for p in range(128):
        for gp in range(2):
            hh = gp * 2 + p // 64
            for dr in range(2):
                lg = lg_all[2 * hh + dr]
                if dr == 0:
                    retT[p, gp, dr, 0] = np.exp(lg * (j + 1)); retT[p, gp, dr, 1] = np.exp(-lg * (j + 1)); retT[p, gp, dr, 2] = np.exp(lg * (31 - j))
                else:
                    retT[p, gp, dr, 0] = np.exp(lg * (32 - j)); retT[p, gp, dr, 1] = np.exp(-lg * (32 - j)); retT[p, gp, dr, 2] = np.exp(lg * j)
                retF[p, gp, dr, :] = np.exp(32 * lg)
    c["retT"] = retT.astype(np.float32); c["retF"] = retF.astype(np.float32)
    rfa = np.zeros((128, 2, 2, 3, 4)); rF8 = np.zeros((128, 2, 2, 8))
    a_ = np.arange(4)
    for p in range(128):
        for gp in range(2):
            hh = gp * 2 + p // 64
            for dr in range(2):
                lg = lg_all[2 * hh + dr]
                if dr == 0:
                    rfa[p, gp, dr, 0] = np.exp(lg * 32 * a_); rfa[p, gp, dr, 1] = np.exp(-lg * 32 * a_); rfa[p, gp, dr, 2] = np.exp(lg * 32 * (3 - a_))
                else:
                    rfa[p, gp, dr, 0] = np.exp(lg * 32 * (3 - a_)); rfa[p, gp, dr, 1] = np.exp(-lg * 32 * (3 - a_)); rfa[p, gp, dr, 2] = np.exp(lg * 32 * a_)
                rF8[p, gp, dr, :] = np.exp(128 * lg)
    c["rfa"] = rfa.astype(np.float32); c["rF8"] = rF8.astype(np.float32)
    tn = tl.astype(np.float64) / Ls
    bands = np.arange(1, 17, dtype=np.float64)
    a2 = 2.0 * np.pi * tn[:, None] * bands[None]
    z = np.concatenate([tn[:, None], np.cos(a2), np.sin(a2)], axis=-1)
    c["zT"] = np.ascontiguousarray(z.T).astype(np.float32)
    MIN_DECAY = math.log(1e-2) / 1.5; MAX_DECAY = math.log(1e-2) / 0.3
    deltas = np.abs(np.linspace(MIN_DECAY, MAX_DECAY, 512, dtype=np.float32)).astype(np.float32)
    c["deltab"] = np.tile(deltas[None], (128, 1)).astype(np.float32)
    tt_ = (np.arange(8)[None, :] * 128 + np.arange(128)[:, None])
    c["negtl"] = (-((tt_ % Ls).astype(np.float64) / Ls)).astype(np.float32)
    c["m0"] = ((tt_ % Ls) != 0).astype(np.float32)
    th = np.zeros((T, T)); blk = (t[:, None] // Ls) == (t[None, :] // Ls)
    th = np.pi * (2 * tl[None, :] + 1) * tl[:, None] / (2.0 * Ls)
    Cm = np.where(blk, np.cos(th), 0.0); Sm = np.where(blk, -np.sin(th), 0.0)
    c["Cf"] = Cm.astype(np.float32).astype(bf); c["Sf"] = Sm.astype(np.float32).astype(bf)
    c["CiT"] = np.ascontiguousarray((Cm / Ls).T).astype(np.float32).astype(bf)
    c["SiT"] = np.ascontiguousarray((Sm / Ls).T).astype(np.float32).astype(bf)
    return c


def _prep_mixer_common(inputs, L):
    f = lambda a: np.ascontiguousarray(np.asarray(a, dtype=np.float32))
    com = {}
    w_in = np.asarray(inputs["w_in"], dtype=np.float32)[:L]
    com["w_in"] = f(w_in)
    com["w_out"] = f(inputs["w_out"])[:L]
    d = np.arange(64)
    perm = np.where((d % 32) < 16, d + 16, d - 16)
    colperm = (np.arange(4)[:, None] * 64 + perm[None, :]).reshape(-1)
    com["w_inp"] = f(np.concatenate([w_in[:, :, 2816 + colperm], w_in[:, :, 3072 + colperm]], axis=-1))
    lb = np.asarray(inputs["hgrn_lb"], dtype=np.float32)
    com["hgrn_lbT"] = f(lb.reshape(2, 4, 2, 128).transpose(3, 0, 2, 1).reshape(128, 4, 4))
    hn = np.asarray(inputs["hgrn_norm_w"], dtype=np.float32)[:L]
    com["hnwT"] = f(np.tile(hn.T, (2, 1)))
    hc = np.asarray(inputs["hyena_conv"], dtype=np.float32)[:L]
    com["convT"] = f(hc.reshape(L, 3, 12, 128).transpose(3, 0, 1, 2))
    hb = np.asarray(inputs["hyena_bias"], dtype=np.float32)[:L]
    com["hbiasT"] = f(hb.reshape(L, 4, 128).transpose(2, 0, 1))
    com["hy_w1"] = f(inputs["hyena_w1"])[:L]; com["hy_w2"] = f(inputs["hyena_w2"])[:L]; com["hy_w3"] = f(inputs["hyena_w3"])[:L]
    fb = np.zeros((64, L, 5), np.float32)
    fb[:, :, 0] = np.asarray(inputs["hyena_b1"])[:L].T; fb[:, :, 1] = np.asarray(inputs["hyena_b2"])[:L].T
    fb[:, :, 2] = np.asarray(inputs["hyena_freq"])[:L].T
    com["hy_fb"] = fb
    return com


def _s0_for(inputs, kind, i, L):
    s0 = np.zeros((128, 4, L, 2, 64), np.float32)
    if kind == "s":
        sh = np.asarray(inputs["state_hgrn"], dtype=np.float32)[i][:L]
        sr = np.asarray(inputs["state_ret"], dtype=np.float32)[i][:L]
        for g in range(2):
            s0[:, g] = sh[:, :, 2 * g:2 * g + 2].transpose(2, 3, 0, 1, 4).reshape(128, L, 2, 64)
            s0[:, 2 + g] = sr[:, :, 2 * g:2 * g + 2].transpose(2, 3, 0, 1, 4).reshape(128, L, 2, 64)
    return s0


_KB_CACHE = {}


def run_all(inputs, L=4):
    if L not in _KB_CACHE:
        kb = KB(L, do_mixer=True)
        kb.build()
        _KB_CACHE[L] = kb
    kb = _KB_CACHE[L]
    com = _prep_common(inputs, L)
    com.update(_prep_mixer_common(inputs, L))
    cst = {"s": _consts("s"), "p": _consts("p")}
    maps = []
    for kind, i in core_assign():
        if kind == "s":
            x = np.asarray(inputs["x_sample"], dtype=np.float32)[i]
            cond = np.asarray(inputs["c"], dtype=np.float32)[i]
        else:
            x = np.asarray(inputs["x_prompt"], dtype=np.float32)[4 * i:4 * i + 4].reshape(1024, 1024)
            cond = np.asarray(inputs["c_ctx"], dtype=np.float32)
        m = dict(com)
        m.update(cst[kind])
        m["xT"] = np.ascontiguousarray(x.T)
        m["cond"] = np.ascontiguousarray(cond.reshape(8, 128).T)
        m["s0"] = _s0_for(inputs, kind, i, L)
        maps.append(m)
    res = run_bass_kernel_spmd(kb.nc, maps, core_ids=list(range(8)))
    R_ = res.results
    y_s = np.stack([np.ascontiguousarray(R_[i]["yT"].T) for i in range(2)], axis=0)
    y_p = np.concatenate([np.ascontiguousarray(R_[2 + g]["yT"].T).reshape(4, 256, 1024) for g in range(4)], axis=0)
    sth = np.concatenate([R_[2 + g]["sth"] for g in range(4)], axis=0)
    str_ = np.concatenate([R_[2 + g]["str"] for g in range(4)], axis=0)
    return (y_p.astype(np.float32), y_s.astype(np.float32), sth.astype(np.float32), str_.astype(np.float32))


def kernel(**inputs):
    return run_all(inputs, 4)
```

```python
import concourse.bass as bass
import concourse.mybir as mybir

import os
ANNOTATE = bool(os.environ.get("KANNOT"))
ENGS = ("pe", "act", "dve", "pool", "sp")
DT_SIZE = {"dt.float32": 4, "dt.bfloat16": 2, "dt.int32": 4, "dt.uint32": 4, "dt.float16": 2, "dt.uint8": 1, "dt.int8": 1, "dt.uint16": 2, "dt.int16": 2}


class Op:
    __slots__ = ("id", "eng", "fn", "deps", "is_dma", "signal", "count", "dsem", "dval", "is_mm", "tag")

    def __init__(self, id, eng, fn, is_dma, is_mm):
        self.id = id
        self.eng = eng
        self.fn = fn
        self.deps = set()
        self.is_dma = is_dma
        self.is_mm = is_mm
        self.signal = False
        self.count = 0
        self.dsem = None
        self.dval = 0


def footprint(ap):
    sp = str(ap.space)
    if "DRAM" in sp.upper():
        return None
    t = ap.tensor
    shp = list(t.shape)
    F = 1
    for s in shp[1:]:
        F *= s
    off = int(ap.offset)
    pairs = ap.ap
    esz = DT_SIZE[str(ap.dtype)]
    p0 = off // F
    lo = off % F
    pstep, pcnt = pairs[0]
    if pstep == F or pcnt == 1:
        p1 = p0 + pcnt
        rest = pairs[1:]
    else:
        p1 = p0 + 1
        rest = pairs
    ext = 0
    for st, cn in rest:
        ext += abs(st) * (cn - 1)
    hi = lo + ext + 1
    if 'PSUM' in sp.upper():
        return (ap.tensor.name, (p0 // 32) * 32, ((p1 + 31) // 32) * 32, 0, 1 << 20)
    return (ap.tensor.name, p0, p1, lo * esz, hi * esz)


class Prog:
    def __init__(self, nc, same_engine_sync=True, ndma_sems=8):
        self.nc = nc
        self.ops = []
        self.recs = {}
        self.same_engine_sync = same_engine_sync
        self.ndma = ndma_sems

    def add(self, eng, fn, reads=(), writes=(), is_dma=False, is_mm=False):
        op = Op(len(self.ops), eng, fn, is_dma, is_mm)
        op.tag = getattr(self, 'tag', '')
        self.ops.append(op)
        for ap in reads:
            fp = footprint(ap)
            if fp is None:
                continue
            self._access(op, fp, False)
        for ap in writes:
            fp = footprint(ap)
            if fp is None:
                continue
            self._access(op, fp, True)
        return op

    def _access(self, op, fp, is_write):
        name, p0, p1, lo, hi = fp
        lst = self.recs.setdefault(name, [])
        keep = []
        for r in lst:
            ov = not (r[1] <= p0 or p1 <= r[0] or r[3] <= lo or hi <= r[2])
            if ov and r[4] != op.id:
                if is_write or r[5]:
                    op.deps.add(r[4])
                if is_write and r[0] >= p0 and r[1] <= p1 and r[2] >= lo and r[3] <= hi:
                    continue
            keep.append(r)
        keep.append([p0, p1, lo, hi, op.id, is_write])
        self.recs[name] = keep

    def finalize(self):
        nc = self.nc
        ops = self.ops
        for op in ops:
            best = {}
            for d in list(op.deps):
                o = ops[d]
                if o.is_dma:
                    continue
                if o.eng not in best or d > best[o.eng]:
                    best[o.eng] = d
            for d in list(op.deps):
                o = ops[d]
                if not o.is_dma and best[o.eng] != d:
                    op.deps.discard(d)
        for op in ops:
            for d in list(op.deps):
                o = ops[d]
                if o.is_dma:
                    continue
                if o.eng == op.eng and not op.is_dma:
                    if o.eng == "pe" or not self.same_engine_sync:
                        op.deps.discard(d)
                        continue
                o.signal = True
        cnt = {e: 0 for e in ENGS}
        dcnt = {e: 0 for e in ENGS}
        for op in ops:
            if op.is_dma:
                i = dcnt[op.eng]
                dcnt[op.eng] += 1
                op.dsem = (op.eng, i % self.ndma)
                op.dval = 16 * (i // self.ndma + 1)
            elif op.signal:
                cnt[op.eng] += 1
                op.count = cnt[op.eng]
        self.cnt = cnt
        import contextlib
        stack = contextlib.ExitStack()
        self.stack = stack
        esem = {e: stack.enter_context(nc.semaphore("c_" + e)) for e in ENGS}
        dsem = {}
        for e in ENGS:
            if dcnt[e]:
                for j in range(self.ndma):
                    dsem[(e, j)] = stack.enter_context(nc.semaphore("d_%s%d" % (e, j)))
        byeng = {e: [o for o in ops if o.eng == e] for e in ENGS}
        engobj = {"pe": "tensor", "act": "scalar", "dve": "vector", "pool": "gpsimd", "sp": "sync"}
        last_dma = {}

        def run_engine(e, eng):
            waited = {}
            lst = byeng[e]
            for op in lst:
                need = {}
                for d in op.deps:
                    o = ops[d]
                    if o.is_dma:
                        key = ("d",) + o.dsem
                        v = o.dval
                    else:
                        key = ("c", o.eng)
                        v = o.count
                    if v > need.get(key, 0):
                        need[key] = v
                if op.is_dma:
                    if op.dval > 16:
                        key = ("d",) + op.dsem
                        v = op.dval - 16
                        if v > need.get(key, 0):
                            need[key] = v
                for key, v in need.items():
                    if waited.get(key, 0) >= v:
                        continue
                    waited[key] = v
                    sem = dsem[key[1:]] if key[0] == "d" else esem[key[1]]
                    eng.wait_ge(sem, v)
                ins = op.fn(eng)
                if ANNOTATE and op.tag:
                    ins.annotate(op.tag)
                if op.is_dma:
                    ins.then_inc(dsem[op.dsem], 16)
                elif op.signal:
                    ins.then_inc(esem[op.eng], 1)
            for j in range(self.ndma):
                if (e, j) in dsem:
                    n = (dcnt[e] - 1 - j) // self.ndma + 1 if dcnt[e] > j else 0
                    if n > 0:
                        eng.wait_ge(dsem[(e, j)], 16 * n)

        with nc.Block() as block:
            @block.tensor
            def _(eng):
                run_engine("pe", eng)

            @block.scalar
            def _(eng):
                run_engine("act", eng)

            @block.vector
            def _(eng):
                run_engine("dve", eng)

            @block.gpsimd
            def _(eng):
                run_engine("pool", eng)

            @block.sync
            def _(eng):
                run_engine("sp", eng)
        stack.close()

import math
import numpy as np
import ml_dtypes
from concourse.bass_utils import run_bass_kernel_spmd

F32 = mybir.dt.float32
BF16 = mybir.dt.bfloat16
AF = mybir.ActivationFunctionType
ALU = mybir.AluOpType
AX = mybir.AxisListType

D = 1024
T = 1024
DFF = 2816
NF = 22
EPS = 1e-6


class KB:
    def __init__(self, L, do_mixer=True):
        self.L = L
        self.do_mixer = do_mixer
        nc = bass.Bass("TRN2", target_bir_lowering=False)
        self.nc = nc
        self.P = Prog(nc)
        self.din = {}
        self.dout = {}
        self.nbank = 0

    def inp(self, name, shape, dt=F32):
        self.din[name] = self.nc.dram_tensor(name, list(shape), dt, kind="ExternalInput").ap()
        return self.din[name]

    def outp(self, name, shape, dt=F32):
        self.dout[name] = self.nc.dram_tensor(name, list(shape), dt, kind="ExternalOutput").ap()
        return self.dout[name]

    def mm(self, out, lhsT, rhs, start, stop):
        self.P.add("pe", lambda e: e.matmul(out, lhsT=lhsT, rhs=rhs, start=start, stop=stop),
                   reads=[lhsT, rhs], writes=[out], is_mm=True)

    def tr(self, out, in_, ident):
        self.P.add("pe", lambda e: e.transpose(out, in_, ident), reads=[in_, ident], writes=[out], is_mm=True)

    def act(self, out, in_, func, bias=None, scale=None, eng="act"):
        kw = {}
        rd = [in_]
        if bias is not None:
            kw["bias"] = bias
            if not isinstance(bias, (int, float)):
                rd.append(bias)
        if scale is not None:
            kw["scale"] = scale
            if not isinstance(scale, (int, float)):
                rd.append(scale)
        self.P.add(eng, lambda e: e.activation(out=out, in_=in_, func=func, **kw), reads=rd, writes=[out])

    def tt(self, out, in0, in1, op, eng="dve"):
        self.P.add(eng, lambda e: e.tensor_tensor(out=out, in0=in0, in1=in1, op=op), reads=[in0, in1], writes=[out])

    def ts(self, out, in0, s1, s2, op0, op1=None, eng="dve"):
        rd = [in0]
        for s in (s1, s2):
            if s is not None and not isinstance(s, (int, float)):
                rd.append(s)
        if op1 is None:
            self.P.add(eng, lambda e: e.tensor_scalar(out=out, in0=in0, scalar1=s1, scalar2=None, op0=op0), reads=rd, writes=[out])
        else:
            self.P.add(eng, lambda e: e.tensor_scalar(out=out, in0=in0, scalar1=s1, scalar2=s2, op0=op0, op1=op1), reads=rd, writes=[out])

    def stt(self, out, in0, scalar, in1, op0, op1, eng="dve"):
        rd = [in0, in1]
        if not isinstance(scalar, (int, float)):
            rd.append(scalar)
        self.P.add(eng, lambda e: e.scalar_tensor_tensor(out=out, in0=in0, scalar=scalar, in1=in1, op0=op0, op1=op1),
                   reads=rd, writes=[out])

    def cp(self, out, in_, eng="dve"):
        if eng == "act":
            self.P.add(eng, lambda e: e.copy(out=out, in_=in_), reads=[in_], writes=[out])
        else:
            self.P.add(eng, lambda e: e.tensor_copy(out=out, in_=in_), reads=[in_], writes=[out])

    def memset(self, out, v, eng="dve"):
        self.P.add(eng, lambda e: e.memset(out, v), writes=[out])

    def scan(self, out, d0, d1, init=0.0):
        self.P.add("dve", lambda e: e.tensor_tensor_scan(out=out, data0=d0, data1=d1, initial=init, op0=ALU.mult, op1=ALU.add),
                   reads=[d0, d1], writes=[out])

    def dma(self, q, out, in_):
        self.P.add(q, lambda e: e.dma_start(out=out, in_=in_), reads=[in_], writes=[out], is_dma=True)

    def bank(self):
        b = self.banks[self.nbank % 8]
        self.nbank += 1
        return b

    def build(self):
        nc = self.nc
        L = self.L
        xT_in = self.inp("xT", [D, T])
        cond_in = self.inp("cond", [128, 8])
        w_ada = self.inp("w_ada", [L, D, 9 * D])
        b_adaT = self.inp("b_adaT", [128, L, 72])
        norm_wT = self.inp("norm_wT", [128, L, 3, 8])
        fin_wT = self.inp("fin_wT", [128, 8])
        ffn_in = self.inp("ffn_in", [L, 2, D, 2 * DFF])
        ffn_out = self.inp("ffn_out", [L, 2, DFF, D])
        yT_out = self.outp("yT", [D, T])
        if self.do_mixer:
            self.mixer_decl()

        self.xT = nc.alloc_sbuf_tensor("xTs", [128, 8, T], F32)
        self.hT = nc.alloc_sbuf_tensor("hTs", [128, 8, T], BF16)
        self.slots = [nc.alloc_sbuf_tensor("slot%d" % i, [128, 4096], BF16) for i in range(3)]
        self.nslot = 0
        self.AR = nc.alloc_sbuf_tensor("arena", [128, 22 * 1024], F32)
        self.small = nc.alloc_sbuf_tensor("small", [128, 548], F32)
        self.rstd = nc.alloc_sbuf_tensor("rstd", [128, T], F32)
        self.tmpf = [nc.alloc_sbuf_tensor("tmpf%d" % i, [128, T], F32) for i in range(2)]
        self.ones = nc.alloc_sbuf_tensor("ones", [128, 128], BF16)
        self.banks = [nc.alloc_psum_tensor("bank%d" % i, [128, 512], F32) for i in range(8)]
        self.ntmp = 0
        sm = self.small
        self.cond = sm[:, 0:8]
        self.scond = sm[:, 8:16]
        self.modT = sm[:, 16:88]
        self.badaT = sm[:, 88:88 + 72 * L].rearrange("p (l j) -> p l j", l=L)
        o = 88 + 72 * 4
        self.normw = sm[:, o:o + 24 * L].rearrange("p (l i c) -> p l i c", l=L, i=3)
        o += 24 * 4
        self.finw = sm[:, o:o + 8]
        o += 8
        self.Ascale = sm[:, o:o + 24].rearrange("p (i c) -> p i c", i=3)
        o += 24
        self.gates = sm[:, o:o + 24].rearrange("p (i c) -> p i c", i=3)
        o += 24
        self.scondb = nc.alloc_sbuf_tensor("scondb", [128, 8], BF16)
        self.modT2 = nc.alloc_sbuf_tensor("modT2", [128, 72], F32)
        self.modTs = [self.modT, self.modT2[:, :]]
        self.eps1k = sm[:, o:o + 1]
        o += 1
        self.small_o = o

        self.dma("sp", self.xT[:], xT_in.rearrange("(c p) t -> p c t", p=128))
        self.dma("sp", self.cond, cond_in)
        self.dma("sp", self.badaT, b_adaT)
        self.dma("sp", self.normw, norm_wT)
        self.dma("sp", self.finw, fin_wT)
        self.memset(self.ones[:], 1.0)
        self.memset(self.eps1k, D * EPS)
        if self.do_mixer:
            self.mixer_init()
        self.act(self.scond, self.cond, AF.Silu)
        self.cp(self.scondb[:], self.scond)

        for _ in self.ada_steps(0, w_ada):
            pass
        for l in range(L):
            self.ada_finish(l)
            self.norm(0, self.normw[:, l, 0, :])
            self.ffn(ffn_in[l, 0], ffn_out[l, 0], self.gates[:, 0, :])
            if self.do_mixer:
                self.norm(1, self.normw[:, l, 1, :])
                self.mixer(l)
            self.norm(2, self.normw[:, l, 2, :])
            nxt = self.ada_steps(l + 1, w_ada) if l + 1 < L else None
            self.ffn(ffn_in[l, 1], ffn_out[l, 1], self.gates[:, 2, :], extra=nxt)
        self.rms_stats()
        fa = sm[:, self.small_o:self.small_o + 8]
        self.ts(fa, self.finw, 32.0, None, ALU.mult)
        yv = self.AR[:, 0:8 * T].rearrange("p (c t) -> p c t", c=8)
        for c in range(8):
            self.stt(yv[:, c, :], self.xT[:, c, :], fa[:, c:c + 1], self.rstd[:], ALU.mult, ALU.mult)
        self.dma("sp", yT_out.rearrange("(c p) t -> p c t", p=128), yv)
        self.P.finalize()

    def next_slot(self):
        s = self.slots[self.nslot % len(self.slots)]
        self.nslot += 1
        return s

    def ada_steps(self, l, w_ada):
        modT = self.modTs[l % 2]
        for j4 in range(18):
            ptag = self.P.tag if hasattr(self.P, "tag") else ""
            self.P.tag = "ada"
            slot = self.next_slot()
            sv = slot[:, :].rearrange("p (k n) -> p k n", k=8)
            self.dma("pool", sv, w_ada[l].rearrange("(k p) n -> p k n", p=128)[:, :, j4 * 512:(j4 + 1) * 512])
            psm = self.bank()
            for jj in range(4):
                for k in range(8):
                    self.mm(psm[:, jj:jj + 1], sv[:, k, jj * 128:(jj + 1) * 128], self.scondb[:, k:k + 1], k == 0, k == 7)
            self.tt(modT[:, j4 * 4:(j4 + 1) * 4], psm[:, 0:4], self.badaT[:, l, j4 * 4:(j4 + 1) * 4], ALU.add)
            self.P.tag = ptag
            yield

    def ada_finish(self, l):
        mod = self.modTs[l % 2].rearrange("p (m c) -> p m c", m=9)
        for i in range(3):
            self.stt(self.Ascale[:, i, :], mod[:, 3 * i + 1, :], 1.0, self.normw[:, l, i, :], ALU.add, ALU.mult)
            self.ts(self.Ascale[:, i, :], self.Ascale[:, i, :], 32.0, None, ALU.mult)
        self.ts(self.gates[:, 0, :], mod[:, 2, :], 0.5, None, ALU.mult)
        self.cp(self.gates[:, 1, :], mod[:, 5, :])
        self.ts(self.gates[:, 2, :], mod[:, 8, :], 0.5, None, ALU.mult)
        self.mod = mod

    def rms_stats(self):
        self.P.tag = "norm"
        sq = self.hT
        for c in range(8):
            if c % 2 == 0:
                self.act(sq[:, c, :], self.xT[:, c, :], AF.Square)
            else:
                self.tt(sq[:, c, :], self.xT[:, c, :], self.xT[:, c, :], ALU.mult)
        for th in range(2):
            ps = self.bank()
            for c in range(8):
                self.mm(ps[:, :], self.ones[:], sq[:, c, th * 512:(th + 1) * 512], c == 0, c == 7)
            rs = self.rstd[:, th * 512:(th + 1) * 512]
            self.act(rs, ps[:, :], AF.Ln, bias=self.eps1k, scale=1.0)
            self.act(rs, rs, AF.Exp, scale=-0.5)

    def norm(self, i, nw):
        self.rms_stats()
        for c in range(8):
            tmp = self.tmpf[self.ntmp % 2]
            self.ntmp += 1
            self.stt(tmp[:], self.xT[:, c, :], self.Ascale[:, i, c:c + 1], self.rstd[:], ALU.mult, ALU.mult)
            self.act(self.hT[:, c, :], tmp[:], AF.Identity, bias=self.mod[:, 3 * i, c:c + 1], scale=1.0)

    def ffn(self, w_in, w_out, gate, extra=None):
        self.P.tag = "ffn_in"
        actT = self.AR[:, 0:NF * 512].bitcast(BF16).rearrange("p (f t) -> p f t", f=NF)
        wv = w_in.rearrange("(k p) (b j c) -> p k b j c", p=128, b=2, j=11)
        for j in range(11):
            slot = self.next_slot()
            sv = slot[:, :].rearrange("p (k b c) -> p k b c", k=8, b=2)
            for b in range(2):
                self.dma("pool", sv[:, :, b, :], wv[:, :, b, j, :])
            for fc in range(2):
                f = 2 * j + fc
                pg = [self.bank(), self.bank()]
                pu = [self.bank(), self.bank()]
                for k in range(8):
                    for tt_ in range(2):
                        self.mm(pg[tt_][:, :], sv[:, k, 0, fc * 128:(fc + 1) * 128], self.hT[:, k, tt_ * 512:(tt_ + 1) * 512], k == 0, k == 7)
                for k in range(8):
                    for tt_ in range(2):
                        self.mm(pu[tt_][:, :], sv[:, k, 1, fc * 128:(fc + 1) * 128], self.hT[:, k, tt_ * 512:(tt_ + 1) * 512], k == 0, k == 7)
                for tt_ in range(2):
                    tmp = self.tmpf[self.ntmp % 2]
                    self.ntmp += 1
                    self.act(tmp[:, 0:512], pg[tt_][:, :], AF.Silu)
                    self.tt(actT[:, f, tt_ * 512:(tt_ + 1) * 512], tmp[:, 0:512], pu[tt_][:, :], ALU.mult)
            if extra is not None:
                next(extra, None)
        self.P.tag = "ffn_out"
        wo = w_out.rearrange("(f p) d -> p f d", p=128)
        for dc in range(8):
            slot = self.next_slot()
            sv = slot[:, 0:NF * 128].rearrange("p (f d) -> p f d", f=NF)
            self.dma("pool", sv, wo[:, :, dc * 128:(dc + 1) * 128])
            po = [self.bank(), self.bank()]
            for f in range(NF):
                for tt_ in range(2):
                    self.mm(po[tt_][:, :], sv[:, f, :], actT[:, f, tt_ * 512:(tt_ + 1) * 512], f == 0, f == NF - 1)
            for tt_ in range(2):
                xs = self.xT[:, dc, tt_ * 512:(tt_ + 1) * 512]
                self.stt(xs, po[tt_][:, :], gate[:, dc:dc + 1], xs, ALU.mult, ALU.add)
            if extra is not None:
                next(extra, None)
        if extra is not None:
            for _ in extra:
                pass


def _prep_common(inputs, L):
    f = lambda a: np.ascontiguousarray(np.asarray(a, dtype=np.float32))
    com = {}
    com["w_ada"] = f(inputs["w_ada"])[:L]
    com["b_adaT"] = f(np.asarray(inputs["b_ada"])[:L].reshape(L, 72, 128).transpose(2, 0, 1))
    com["norm_wT"] = f(np.asarray(inputs["norm_w"])[:L].reshape(L, 3, 8, 128).transpose(3, 0, 1, 2))
    com["fin_wT"] = f(np.asarray(inputs["final_norm_w"]).reshape(8, 128).T)
    com["ffn_in"] = f(inputs["ffn_in"])[:L]
    com["ffn_out"] = f(inputs["ffn_out"])[:L]
    return com


def core_assign():
    return [("s", 0), ("s", 1), ("p", 0), ("p", 1), ("p", 2), ("p", 3), ("p", 3), ("p", 3)]

AW = 22 * 1024


def _mixer_decl(self):
    L = self.L
    i = self.inp
    self.w_in = i("w_in", [L, D, 3840])
    self.w_out = i("w_out", [L, D, D])
    self.w_inp = i("w_inp", [L, D, 512])
    self.d_cont = i("cont", [128, 1])
    self.d_ropeC = i("ropeC", [128, T])
    self.d_ropeS = i("ropeS", [128, T])
    self.d_smask = i("smask", [128, T])
    self.d_cm = i("cm", [128, 2, 512], BF16)
    self.d_cmr = i("cmr", [128, 2, 512], BF16)
    self.d_rfa = i("rfa", [128, 2, 2, 3, 4])
    self.d_rF8 = i("rF8", [128, 2, 2, 8])
    self.d_gvec8 = i("gvec8", [128, 8])
    self.d_chm = i("chm", [128, 4])
    self.d_gvec = i("gvec", [128, 32])
    self.d_ident = i("ident", [128, 128], BF16)
    self.d_bones = i("bones", [128, 128], BF16)
    self.d_retT = i("retT", [128, 2, 2, 3, 32])
    self.d_retF = i("retF", [128, 2, 2, 32])
    self.d_lb = i("hgrn_lbT", [128, 4, 4])
    self.d_hnw = i("hnwT", [128, L])
    self.d_conv = i("convT", [128, L, 3, 12])
    self.d_hbias = i("hbiasT", [128, L, 4])
    self.d_zT = i("zT", [33, T])
    self.d_w1 = i("hy_w1", [L, 33, 64])
    self.d_w2 = i("hy_w2", [L, 64, 64])
    self.d_w3 = i("hy_w3", [L, 64, 1024])
    self.d_fb = i("hy_fb", [64, L, 5])
    self.d_delta = i("deltab", [128, 512])
    self.d_negtl = i("negtl", [128, 8])
    self.d_m0 = i("m0", [128, 8])
    self.d_Cf = i("Cf", [T, T])
    self.d_Sf = i("Sf", [T, T])
    self.d_Ci = i("CiT", [T, T])
    self.d_Si = i("SiT", [T, T])
    self.d_s0 = i("s0", [128, 4, L, 2, 64])
    self.o_sth = self.outp("sth", [4, L, 2, 4, 64, 64])
    self.o_str = self.outp("str", [4, L, 2, 4, 64, 64])


def _mixer_init(self):
    nc = self.nc
    L = self.L
    a = lambda n, s, dt=F32: nc.alloc_sbuf_tensor("s_" + n, s, dt)
    self.ropeC = a("ropeC", [128, T]); self.ropeS = a("ropeS", [128, T]); self.smask = a("smask", [128, T])
    self.cm = a("cm", [128, 2, 512], BF16)
    self.cm2 = self.cm[:, :, :].rearrange("p d (k h n) -> p d k h n", k=2, h=2)
    self.cmr = a("cmr", [128, 2, 512], BF16)
    self.cmr2 = self.cmr[:, :, :].rearrange("p d (k h n) -> p d k h n", k=2, h=2); self.chm = a("chm", [128, 4]); self.gvec = a("gvec", [128, 32])
    self.ident = a("ident", [128, 128], BF16); self.bones = a("bones", [128, 128], BF16)
    self.retT = a("retT", [128, 2, 2, 3, 32]); self.retF = a("retF", [128, 2, 2, 32])
    self.delta = a("delta", [128, 512]); self.zT = a("zTs", [33, T])
    self.ms = a("msmall", [128, 512])
    self.sfin = a("sfin", [128, 256])
    self.qx = [a("qx0", [128, 256], BF16), a("qx1", [128, 256], BF16)]
    self.memset(self.qx[0][:], 0.0); self.memset(self.qx[1][:], 0.0)
    self.s0 = a("s0s", [128, 4, 2, 64])
    self.hw1 = a("hw1", [33, L, 64]); self.hw2 = a("hw2", [64, L, 64])
    ms = self.ms
    o = 0

    def take(n):
        nonlocal o
        v = ms[:, o:o + n]
        o += n
        return v
    self.cont = take(1); self.cm1 = take(1)
    self.lbx = take(16).rearrange("p (a l) -> p a l", l=4)
    self.oml = take(16).rearrange("p (a l) -> p a l", l=4)
    self.lbs = take(8)
    self.lnoml = take(16).rearrange("p (a l) -> p a l", l=4)
    self.epsc = take(1)
    self.hnw = take(L); self.onesc = take(1)
    self.conv = take(L * 36).rearrange("p (l t c) -> p l t c", l=L, t=3)
    self.nconv = take(24).rearrange("p (t c) -> p t c", t=2)
    self.hbias = take(L * 4).rearrange("p (l c) -> p l c", l=L)
    self.fb = take(L * 5).rearrange("p (l c) -> p l c", l=L)
    self.fs = take(4)
    self.negtl = take(8); self.m0 = take(8)
    self.Fv = take(32); self.Fg = take(32)
    self.rfa = take(48).rearrange("p (g d k a) -> p g d k a", g=2, d=2, k=3)
    self.rF8 = take(32).rearrange("p (g d n) -> p g d n", g=2, d=2)
    self.gvec8 = take(8)
    q = "sp"
    for dst, src in ((self.ropeC[:], self.d_ropeC), (self.ropeS[:], self.d_ropeS), (self.smask[:], self.d_smask),
                     (self.cm[:], self.d_cm), (self.cmr[:], self.d_cmr), (self.rfa, self.d_rfa), (self.rF8, self.d_rF8), (self.gvec8, self.d_gvec8), (self.chm[:], self.d_chm), (self.gvec[:], self.d_gvec),
                     (self.ident[:], self.d_ident), (self.bones[:], self.d_bones), (self.retT[:], self.d_retT),
                     (self.retF[:], self.d_retF), (self.delta[:], self.d_delta), (self.zT[:], self.d_zT),
                     (self.cont, self.d_cont), (self.lbx, self.d_lb), (self.hnw, self.d_hnw), (self.conv, self.d_conv),
                     (self.hbias, self.d_hbias), (self.fb[0:64], self.d_fb), (self.negtl, self.d_negtl), (self.m0, self.d_m0),
                     (self.hw1[:], self.d_w1.rearrange("l k n -> k l n")),
                     (self.hw2[:], self.d_w2.rearrange("l k n -> k l n"))):
        self.dma(q, dst, src)
    self.memset(self.onesc, 1.0)
    self.ts(self.cm1, self.cont, -1.0, None, ALU.add)
    self.act(self.lbx, self.lbx, AF.Exp)
    self.P.add("dve", lambda e: e.reduce_sum(out=self.lbs[:, 0:4], in_=self.lbx, axis=AX.X), reads=[self.lbx], writes=[self.lbs[:, 0:4]])
    self.P.add("dve", lambda e: e.reciprocal(out=self.lbs[:, 4:8], in_=self.lbs[:, 0:4]), reads=[self.lbs[:, 0:4]], writes=[self.lbs[:, 4:8]])
    self.tt(self.lbx, self.lbx, self.lbs[:, 4:8].unsqueeze(2).broadcast_to([128, 4, 4]), ALU.mult)
    self.memset(self.oml[:, :, 0:1], 1.0)
    for l in range(1, 4):
        self.tt(self.oml[:, :, l:l + 1], self.oml[:, :, l - 1:l], self.lbx[:, :, l:l + 1], ALU.subtract)
    self.act(self.lnoml, self.oml, AF.Ln)
    self.memset(self.epsc, EPS)


def _slotload(self, pieces):
    N = sum(p[1] for p in pieces)
    slot = self.next_slot()
    sv = slot[:, 0:8 * N].rearrange("p (k n) -> p k n", k=8)
    o = 0
    for src, n, q in pieces:
        self.dma(q, sv[:, :, o:o + n], src)
        o += n
    return sv


def _wcols(self, w, c0, n):
    return (w.rearrange("(k p) n -> p k n", p=128)[:, :, c0:c0 + n], n, "pool")


def _mixer(self, l):
    AR = self.AR
    L = self.L
    W = self.w_in[l]
    o = 0

    def fw(n):
        nonlocal o
        v = AR[:, o:o + n]
        o += n
        return v
    mixT = fw(4096).bitcast(BF16).rearrange("p (c t) -> p c t", c=8)
    v_tm = fw(2048).bitcast(BF16).rearrange("p (k n) -> p k n", k=8)
    base = o
    self.dma("sp", self.s0[:], self.d_s0[:, :, l, :, :])
    self.P.tag = "mix_v"
    sv = _slotload(self, [_wcols(self, W, 256, 256), _wcols(self, W, 3328, 256)])
    for tile in range(8):
        ps = self.bank()
        for k in range(8):
            self.mm(ps[:, :], self.hT[:, k, tile * 128:(tile + 1) * 128], sv[:, k, :], k == 0, k == 7)
        self.cp(v_tm[:, tile, :], ps[:, :], eng="act")

    import os
    PARTS = os.environ.get('MIX_PARTS', 'hy,gla')
    if 'hy' in PARTS:
        Gr = fw(2048).bitcast(BF16).rearrange("p (k n) -> p k n", k=8)
        Gi = fw(2048).bitcast(BF16).rearrange("p (k n) -> p k n", k=8)
        hb = o
        S_tm = fw(2048).bitcast(BF16).rearrange("p (k n) -> p k n", k=8)
        D_tm = fw(2048).bitcast(BF16).rearrange("p (k n) -> p k n", k=8)
        hid = [fw(1024), fw(1024)]
        hw3 = fw(1024)[0:64, :]
        self.P.tag = "hy_filt"
        self.dma("sp", hw3, self.d_w3[l])
        f3 = self.fs
        self.ts(f3[0:64, 0:1], self.fb[0:64, l, 2:3], 1.0 / 3.0, None, ALU.mult)
        self.tt(f3[0:64, 1:2], f3[0:64, 0:1], self.fb[0:64, l, 0:1], ALU.mult)
        self.tt(f3[0:64, 2:3], f3[0:64, 0:1], self.fb[0:64, l, 1:2], ALU.mult)

        def sin3(dst, ps, bcol):
            tmp = self.tmpf[self.ntmp % 2]
            self.ntmp += 1
            s = tmp[0:64, 0:512]
            s2 = tmp[0:64, 512:1024]
            self.act(s, ps, AF.Sin, bias=f3[0:64, bcol:bcol + 1], scale=f3[0:64, 0:1])
            self.tt(s2, s, s, ALU.mult)
            self.ts(s2, s2, -4.0, 3.0, ALU.mult, ALU.add)
            self.tt(dst, s2, s, ALU.mult)
        for th in range(2):
            ps = self.bank()
            self.mm(ps[0:64, :], self.hw1[:, l, :], self.zT[:, th * 512:(th + 1) * 512], True, True)
            sin3(hid[0][0:64, th * 512:(th + 1) * 512], ps[0:64, :], 1)
        for th in range(2):
            ps = self.bank()
            self.mm(ps[0:64, :], self.hw2[:, l, :], hid[0][0:64, th * 512:(th + 1) * 512], True, True)
            sin3(hid[1][0:64, th * 512:(th + 1) * 512], ps[0:64, :], 2)
        for tile in range(8):
            pf = self.bank()
            pb = self.bank()
            self.mm(pf[:, :], hid[1][0:64, tile * 128:(tile + 1) * 128], hw3[:, 0:512], True, True)
            self.mm(pb[:, :], hid[1][0:64, tile * 128:(tile + 1) * 128], hw3[:, 512:1024], True, True)
            t0 = self.tmpf[self.ntmp % 2]; self.ntmp += 1
            t1 = self.tmpf[self.ntmp % 2]; self.ntmp += 1
            dec = t0[:, 0:512]
            self.act(dec, self.delta[:], AF.Exp, scale=self.negtl[:, tile:tile + 1])
            self.tt(t0[:, 512:1024], pf[:, :], dec, ALU.mult)
            self.stt(t1[:, 0:512], pb[:, :], self.m0[:, tile:tile + 1], dec, ALU.mult, ALU.mult)
            self.tt(S_tm[:, tile, :], t0[:, 512:1024], t1[:, 0:512], ALU.add)
            self.tt(D_tm[:, tile, :], t0[:, 512:1024], t1[:, 0:512], ALU.subtract)
        for (tab, X, G) in ((self.d_Cf, S_tm, Gr), (self.d_Sf, D_tm, Gi)):
            tv = tab.rearrange("(k p) f -> p k f", p=128)
            for fh in range(2):
                sv = _slotload(self, [(tv[:, :, fh * 512:(fh + 1) * 512], 512, "pool")])
                for fc in range(4):
                    ps = self.bank()
                    for k in range(8):
                        self.mm(ps[:, :], sv[:, k, fc * 128:(fc + 1) * 128], X[:, k, :], k == 0, k == 7)
                    self.cp(G[:, fh * 4 + fc, :], ps[:, :], eng="act")
        self.P.tag = "hy_chunk"
        o = hb
        x0b = fw(2048).bitcast(BF16).rearrange("p (c t) -> p c t", c=4)
        ubf = fw(2048).bitcast(BF16).rearrange("p (c t) -> p c t", c=4)
        u_tm = fw(2048).bitcast(BF16).rearrange("p (k n) -> p k n", k=8)
        tb = o
        x0a = fw(1024); a1 = fw(1024); a2 = fw(1024); hraw = [fw(1024), fw(1024)]
        o = tb
        Ur = fw(2048).rearrange("p (k n) -> p k n", k=4)
        Pr = fw(2048).bitcast(BF16).rearrange("p (k n) -> p k n", k=8)
        Pi = fw(2048).bitcast(BF16).rearrange("p (k n) -> p k n", k=8)
        assert o <= AW, o
        self.ts(self.nconv[:, 0, :], self.conv[:, l, 0, :], self.cm1, None, ALU.mult)
        self.ts(self.nconv[:, 1, :], self.conv[:, l, 2, :], self.cm1, None, ALU.mult)
        nh = 0
        for j in range(4):
            sv = _slotload(self, [_wcols(self, W, 1280 + j * 128, 128), _wcols(self, W, 1792 + j * 128, 128),
                                  _wcols(self, W, 2304 + j * 128, 128)])
            accs = (x0a, a1, a2)
            for cc in range(3):
                hr = hraw[nh % 2]; nh += 1
                for th in range(2):
                    ps = self.bank()
                    for k in range(8):
                        self.mm(ps[:, :], sv[:, k, cc * 128:(cc + 1) * 128], self.hT[:, k, th * 512:(th + 1) * 512], k == 0, k == 7)
                    self.cp(hr[:, th * 512:(th + 1) * 512], ps[:, :], eng="act")
                ch = cc * 4 + j
                acc = accs[cc]
                self.act(acc, hr, AF.Copy, scale=self.conv[:, l, 1, ch:ch + 1])
                self.stt(acc[:, 1:T], hr[:, 0:T - 1], self.conv[:, l, 0, ch:ch + 1], acc[:, 1:T], ALU.mult, ALU.add)
                self.stt(acc[:, 0:T - 1], hr[:, 1:T], self.conv[:, l, 2, ch:ch + 1], acc[:, 0:T - 1], ALU.mult, ALU.add)
                self.stt(acc[:, 256:T:256], hr[:, 255:T - 1:256], self.nconv[:, 0, ch:ch + 1], acc[:, 256:T:256], ALU.mult, ALU.add)
                self.stt(acc[:, 255:T - 1:256], hr[:, 256:T:256], self.nconv[:, 1, ch:ch + 1], acc[:, 255:T - 1:256], ALU.mult, ALU.add)
            self.cp(x0b[:, j, :], x0a, eng="act")
            self.tt(ubf[:, j, :], a1, a2, ALU.mult)
        for j in range(4):
            pt = self.bank()
            ptb = pt[:, :].bitcast(BF16)
            for k in range(8):
                self.tr(ptb[:, k * 128:(k + 1) * 128], ubf[:, j, k * 128:(k + 1) * 128], self.ident[:])
            self.cp(u_tm[:, :, j * 128:(j + 1) * 128], ptb.rearrange("p (k n) -> p k n", k=8), eng=("act" if j % 2 == 0 else "dve"))
        tvC = self.d_Cf.rearrange("(k p) f -> p k f", p=128)
        tvS = self.d_Sf.rearrange("(k p) f -> p k f", p=128)
        for fh in range(2):
            sv = _slotload(self, [(tvC[:, :, fh * 512:(fh + 1) * 512], 512, "pool")])
            for fc in range(4):
                ps = self.bank()
                for k in range(8):
                    self.mm(ps[:, :], sv[:, k, fc * 128:(fc + 1) * 128], u_tm[:, k, :], k == 0, k == 7)
                self.cp(Ur[:, fc, :], ps[:, :], eng="act")
            sv = _slotload(self, [(tvS[:, :, fh * 512:(fh + 1) * 512], 512, "pool")])
            for fc in range(4):
                fi = fh * 4 + fc
                ps = self.bank()
                for k in range(8):
                    self.mm(ps[:, :], sv[:, k, fc * 128:(fc + 1) * 128], u_tm[:, k, :], k == 0, k == 7)
                Ui = ps[:, :]
                t0 = self.tmpf[0]; t1 = self.tmpf[1]
                self.tt(t0[:, 0:512], Gr[:, fi, :], Ur[:, fc, :], ALU.mult)
                self.tt(t0[:, 512:1024], Ui, Gi[:, fi, :], ALU.mult)
                self.tt(Pr[:, fi, :], t0[:, 0:512], t0[:, 512:1024], ALU.subtract)
                self.tt(t1[:, 0:512], Ui, Gr[:, fi, :], ALU.mult)
                self.tt(t1[:, 512:1024], Gi[:, fi, :], Ur[:, fc, :], ALU.mult)
                self.tt(Pi[:, fi, :], t1[:, 0:512], t1[:, 512:1024], ALU.add)
        tvCi = self.d_Ci.rearrange("(k p) t -> p k t", p=128)
        tvSi = self.d_Si.rearrange("(k p) t -> p k t", p=128)
        for th in range(2):
            sl = slice(th * 512, (th + 1) * 512)
            svc = _slotload(self, [(tvCi[:, :, sl], 512, "pool")])
            svs = _slotload(self, [(tvSi[:, :, sl], 512, "pool")])
            for j in range(4):
                py = self.bank()
                for k in range(8):
                    self.mm(py[:, :], Pr[:, k, j * 128:(j + 1) * 128], svc[:, k, :], k == 0, False)
                for k in range(8):
                    self.mm(py[:, :], Pi[:, k, j * 128:(j + 1) * 128], svs[:, k, :], False, k == 7)
                t0 = self.tmpf[j % 2]
                self.stt(t0[:, 0:512], ubf[:, j, sl], self.hbias[:, l, j:j + 1], py[:, :], ALU.mult, ALU.add)
                self.tt(mixT[:, 2 + j, sl], t0[:, 0:512], x0b[:, j, sl], ALU.mult)

    if 'gla' in PARTS:
        o = base
        qf = fw(1024); kf = fw(1024); lg = fw(1024); bb = fw(1024); E = fw(1024)
        qt = fw(512).bitcast(BF16); kt = fw(512).bitcast(BF16); kh = fw(512).bitcast(BF16)
        kh_tm = fw(512).bitcast(BF16).rearrange("p (k n) -> p k n", k=8)
        KV = fw(2112); Fr = fw(2112)
        SP = fw(2048).bitcast(BF16).rearrange("p (n m) -> p n m", n=32)
        osb = fw(1024)
        vpad = fw(1024).bitcast(BF16).rearrange("p (k h m) -> p k h m", k=8, h=2)
        vx = [fw(256).bitcast(BF16), fw(256).bitcast(BF16)]
        Ab = [fw(128).bitcast(BF16).rearrange("p (h n) -> p h n", h=2), fw(128).bitcast(BF16).rearrange("p (h n) -> p h n", h=2)]
        Sfin = self.sfin[:, :].rearrange("p (b v) -> p b v", b=4)
        assert o <= AW, o
        KV3 = KV.rearrange("p (d n) -> p d n", n=33)
        Fr3 = Fr.rearrange("p (d n) -> p d n", n=33)
        self.memset(SP[:, :, :], 0.0)
        self.memset(vpad[:, :, :, :], 0.0)
        self.memset(Fr3[:, :, 0:1], 0.0)
        nvx = 0
        for g in range(4):
            ret = g >= 2
            gp = g - 2
            vc0 = (256 + gp * 128) if ret else g * 128
            self.P.tag = "gla_proj"
            for h in range(2):
                self.cp(vpad[:, :, h, h * 64:(h + 1) * 64], v_tm[:, :, vc0 + h * 64:vc0 + (h + 1) * 64], eng="act")
            if ret:
                sv = _slotload(self, [_wcols(self, W, 2816 + gp * 128, 128), _wcols(self, W, 3072 + gp * 128, 128),
                                      _wcols(self, self.w_inp[l], gp * 128, 128), _wcols(self, self.w_inp[l], 256 + gp * 128, 128)])
            else:
                sv = _slotload(self, [_wcols(self, W, g * 128, 128), _wcols(self, W, 512 + g * 128, 128),
                                      _wcols(self, W, 768 + g * 128, 128), _wcols(self, W, 1024 + g * 128, 128)])
                svg, gcc = sv, 3
            if ret:
                svg, gcc = _slotload(self, [_wcols(self, W, 3584 + gp * 128, 128)]), 0
            mc = (6 + gp) if ret else g

            def proj(cc, th):
                ps = self.bank()
                for k in range(8):
                    self.mm(ps[:, :], sv[:, k, cc * 128:(cc + 1) * 128], self.hT[:, k, th * 512:(th + 1) * 512], k == 0, k == 7)
                return ps
            if ret:
                for (dst, c_a, c_b, sc) in ((qf, 0, 2, 1.0), (kf, 1, 3, 0.125)):
                    for th in range(2):
                        sl = slice(th * 512, (th + 1) * 512)
                        pa = proj(c_a, th)
                        pb_ = proj(c_b, th)
                        t0 = self.tmpf[self.ntmp % 2]; self.ntmp += 1
                        self.tt(t0[:, 0:512], pa[:, :], self.ropeC[:, sl], ALU.mult)
                        self.tt(t0[:, 512:1024], pb_[:, :], self.ropeS[:, sl], ALU.mult)
                        self.tt(t0[:, 0:512], t0[:, 0:512], t0[:, 512:1024], ALU.add)
                        self.ts(dst[:, sl], t0[:, 0:512], sc, None, ALU.mult)
            else:
                for th in range(2):
                    sl = slice(th * 512, (th + 1) * 512)
                    pa = proj(0, th)
                    self.act(qf[:, sl], pa[:, :], AF.Silu)
            for th in range(2):
                sl = slice(th * 512, (th + 1) * 512)
                pg = self.bank()
                for k in range(8):
                    self.mm(pg[:, :], svg[:, k, gcc * 128:(gcc + 1) * 128], self.hT[:, k, sl], k == 0, k == 7)
                self.act(mixT[:, mc, sl], pg[:, :], AF.Silu)
            for dr in [int(x) for x in os.environ.get('GLA_DIRS', '0,1').split(',')]:
                self.P.tag = "gla_A"
                q3 = qf.rearrange("p (n j) -> p n j", j=32)
                k3 = kf.rearrange("p (n j) -> p n j", j=32)
                qt3 = qt.rearrange("p (n j) -> p n j", j=32)
                kt3 = kt.rearrange("p (n j) -> p n j", j=32)
                kh3 = kh.rearrange("p (n j) -> p n j", j=32)
                if ret:
                    tab = lambda kind: self.retT[:, gp, dr, kind:kind + 1, :].broadcast_to([128, 32, 32])
                    fa4 = lambda kind: self.rfa[:, gp, dr, kind, :].unsqueeze(1).unsqueeze(3).broadcast_to([128, 8, 4, 32])
                    v4 = lambda v: v.rearrange("p (t a b) -> p t a b", a=4, b=32)
                    E3_ = E.rearrange("p (n j) -> p n j", j=32)
                    for (dst_, src3, kind) in ((qt, q3, 0), (kt, k3, 1), (kh, k3, 2)):
                        self.tt(E3_, src3, tab(kind), ALU.mult)
                        self.tt(v4(dst_), v4(E), fa4(kind), ALU.mult)
                    Fv = self.rF8[:, gp, dr, :]
                else:
                    a_idx = dr * 2 + g
                    b3 = bb.rearrange("p (n j) -> p n j", j=32)
                    tot = b3[:, :, 31:32]
                    HS = [slice(0, 512), slice(512, 1024)]
                    n3 = lambda v, th: v[:, HS[th]].rearrange("p (n j) -> p n j", j=32)
                    toth = lambda th: b3[:, th * 16:(th + 1) * 16, 31:32].broadcast_to([128, 16, 32])
                    t0 = self.tmpf[0]
                    lnoml = self.lnoml[:, a_idx, l:l + 1]
                    pzs = [proj(1 + dr, th) for th in range(2)]
                    for th in range(2):
                        self.act(t0[:, HS[th]], pzs[th][:, :], AF.Exp)
                    for th in range(2):
                        self.act(t0[:, HS[th]], t0[:, HS[th]], AF.Ln, bias=self.onesc, scale=1.0)
                    for th in range(2):
                        self.act(kf[:, HS[th]], t0[:, HS[th]], AF.Exp, bias=lnoml, scale=-1.0)
                    for th in range(2):
                        self.act(lg[:, HS[th]], kf[:, HS[th]], AF.Ln, bias=self.onesc, scale=-1.0)
                    for th in range(2):
                        self.scan(bb[:, HS[th]], self.smask[:, HS[th]], lg[:, HS[th]])
                    if dr == 0:
                        bc = bb
                    else:
                        for th in range(2):
                            self.tt(E[:, HS[th]], lg[:, HS[th]], bb[:, HS[th]], ALU.subtract)
                        for th in range(2):
                            self.tt(n3(lg, th), n3(E, th), toth(th), ALU.add)
                        bc = lg
                    for th in range(2):
                        self.act(E[:, HS[th]], bc[:, HS[th]], AF.Exp)
                    for th in range(2):
                        self.act(t0[:, HS[th]], bc[:, HS[th]], AF.Exp, scale=-1.0)
                    for th in range(2):
                        self.tt(qt[:, HS[th]], qf[:, HS[th]], E[:, HS[th]], ALU.mult)
                    for th in range(2):
                        self.tt(kt[:, HS[th]], kf[:, HS[th]], t0[:, HS[th]], ALU.mult)
                    for th in range(2):
                        self.tt(n3(E, th), toth(th), n3(bc, th), ALU.subtract)
                    for th in range(2):
                        self.act(E[:, HS[th]], E[:, HS[th]], AF.Exp)
                    for th in range(2):
                        self.tt(kh[:, HS[th]], kf[:, HS[th]], E[:, HS[th]], ALU.mult)
                    self.act(self.Fv.unsqueeze(2), tot, AF.Exp)
                    Fv = self.Fv
                if int(os.environ.get('GLA_STOP', '99')) <= 1:
                    continue
                self.P.tag = "gla_B"
                NC = 8 if ret else 32
                CPT = NC // 8
                BQ = NC // 4
                KVf = KV[:, 0:64 * (NC + 1)]
                Frf = Fr[:, 0:64 * (NC + 1)]
                KV3 = KVf.rearrange("p (d n) -> p d n", n=NC + 1)
                Fr3 = Frf.rearrange("p (d n) -> p d n", n=NC + 1)
                Fproc = Fv if dr == 0 else Fv[:, NC - 1::-1]
                Fg = self.Fg[:, 0:NC]
                self.tt(Fg, Fproc, (self.gvec8 if ret else self.gvec[:]), ALU.mult)
                self.memset(Fr3[:, :, 0:1], 0.0)
                self.cp(Fr3[:, :, 1:NC + 1], Fg.unsqueeze(1).broadcast_to([128, 64, NC]))
                self.cp(KV3[:, :, 0:1], self.s0[:, g, dr, :].unsqueeze(2))
                pt = self.bank()
                ptb = pt[:, :].bitcast(BF16)
                for k in range(8):
                    self.tr(ptb[:, k * 128:(k + 1) * 128], kh[:, k * 128:(k + 1) * 128], self.ident[:])
                self.cp(kh_tm[:, :, :], ptb.rearrange("p (k n) -> p k n", k=8), eng="act")
                if int(os.environ.get('GLA_STOP', '99')) <= 2:
                    continue
                for tile in (range(8) if ret else []):
                    ps = self.bank()
                    self.mm(ps[:, 0:128], kh_tm[:, tile, :], v_tm[:, tile, vc0:vc0 + 128], True, True)
                    for h in range(2):
                        hp = slice(h * 64, (h + 1) * 64)
                        slot_ = (1 + tile) if dr == 0 else (8 - tile)
                        self.cp(KV3[hp, :, slot_], ps[hp, h * 64:(h + 1) * 64], eng=("act" if h == 0 else "dve"))
                for tile in ([] if ret else range(8)):
                    vxt = vx[nvx % 2]; nvx += 1
                    v2 = v_tm[:, tile, vc0:vc0 + 128].rearrange("p (h d) -> p h d", h=2)
                    self.tt(vxt.rearrange("p (h c d) -> p h c d", h=2, c=4),
                            v2.unsqueeze(2).broadcast_to([128, 2, 4, 64]),
                            self.chm[:].unsqueeze(1).unsqueeze(3).broadcast_to([128, 2, 4, 64]), ALU.mult)
                    ps = self.bank()
                    self.mm(ps[:, :], kh_tm[:, tile, :], vxt, True, True)
                    for h in range(2):
                        src = ps[h * 64:(h + 1) * 64, h * 256:(h + 1) * 256].rearrange("p (c d) -> p c d", c=4)
                        if dr == 0:
                            dst = KV3[h * 64:(h + 1) * 64, :, 1 + 4 * tile:5 + 4 * tile]
                        else:
                            hi = 32 - 4 * tile
                            dst = KV3[h * 64:(h + 1) * 64, :, hi:hi - 4:-1] if hi - 4 > 0 else KV3[h * 64:(h + 1) * 64, :, hi:0:-1]
                        self.cp(dst.transpose([0, 2, 1]), src, eng="act")
                if int(os.environ.get('GLA_STOP', '99')) <= 3:
                    continue
                self.P.tag = "gla_C"
                self.scan(KVf, Frf, KVf)
                if int(os.environ.get('GLA_STOP', '99')) <= 4:
                    continue
                srcf = KV3[:, :, BQ:NC + 1:BQ] if dr == 0 else KV3[:, :, NC:0:-BQ]
                self.cp(Sfin, srcf.transpose([0, 2, 1]))
                od = (self.o_str if ret else self.o_sth)
                hh = (gp if ret else g) * 2
                self.dma("sp", od[:, l, dr, hh:hh + 2, :, :].rearrange("b h k v -> (h k) b v"), Sfin)
                if int(os.environ.get('GLA_STOP', '99')) <= 5:
                    continue
                for h in range(2):
                    s_ = KV3[h * 64:(h + 1) * 64, :, 0:NC] if dr == 0 else KV3[h * 64:(h + 1) * 64, :, NC - 1::-1]
                    self.cp(SP[h * 64:(h + 1) * 64, 0:NC, h * 64:(h + 1) * 64], s_.transpose([0, 2, 1]), eng=("act" if h == 0 else "dve"))
                if int(os.environ.get('GLA_STOP', '99')) <= 6:
                    continue
                for h in range(2):
                    hp = slice(h * 64, (h + 1) * 64)
                    if dr == 0:
                        gsrc = KV3[hp, :, BQ:NC:BQ]; gdst = SP[hp, BQ:NC:BQ, hp]
                    else:
                        gsrc = KV3[hp, :, NC - BQ:0:-BQ]; gdst = SP[hp, BQ - 1:NC - BQ:BQ, hp]
                    self.ts(gdst, gsrc.transpose([0, 2, 1]), self.cont[hp, :], None, ALU.mult)
                self.P.tag = "gla_D"
                qxa = self.rstd[:, :].bitcast(BF16).rearrange("p (k h n) -> p k h n", k=8, h=2)
                Aall = self.tmpf[1][:, :].bitcast(BF16).rearrange("p (k h n) -> p k h n", k=8, h=2)
                for h in range(2):
                    hp = slice(h * 64, (h + 1) * 64)
                    oth = slice((1 - h) * 64, (2 - h) * 64)
                    self.cp(qxa[hp, :, h, :], qt[hp, :].rearrange("p (k n) -> p k n", k=8), eng="act")
                    self.memset(qxa[oth, :, h, :], 0.0)
                for bp in range(4):
                    pa = self.bank()
                    for t2 in range(2):
                        tile = 2 * bp + t2
                        cs = slice(tile * 128, (tile + 1) * 128)
                        self.mm(pa[:, t2 * 256:(t2 + 1) * 256], kt[:, cs], qxa[:, tile, :, :], True, True)
                    self.tt(Aall[:, 2 * bp:2 * bp + 2, :, :], pa[:, :].rearrange("p (k h n) -> p k h n", k=2, h=2),
                            (self.cmr2 if ret else self.cm2)[:, dr, :, :, :], ALU.mult)
                for tile in range(8):
                    cs = slice(tile * 128, (tile + 1) * 128)
                    po = self.bank()
                    self.mm(po[:, 0:128], vpad[:, tile, 0, :], Aall[:, tile, 0, :], True, False)
                    self.mm(po[:, 0:128], vpad[:, tile, 1, :], Aall[:, tile, 1, :], False, False)
                    WC = 128 // CPT
                    for c in range(CPT):
                        n = CPT * tile + c
                        self.mm(po[:, WC * c:WC * (c + 1)], SP[:, n, :], qt[:, tile * 128 + WC * c:tile * 128 + WC * (c + 1)], False, c == CPT - 1)
                    if dr == 0:
                        self.cp(osb[:, cs], po[:, 0:128], eng="act")
                    else:
                        self.tt(osb[:, cs], osb[:, cs], po[:, 0:128], ALU.add)
            self.P.tag = "gla_epi"
            sq = qt
            self.act(sq, osb, AF.Square)
            nw = self.onesc if ret else self.hnw[:, l:l + 1]
            for th in range(2):
                sl = slice(th * 512, (th + 1) * 512)
                ps = self.bank()
                self.mm(ps[:, :], self.bones[:], sq[:, sl], True, True)
                rs = E[:, sl]
                self.act(rs, ps[:, :], AF.Ln, bias=self.epsc, scale=1.0 / 64.0)
                self.act(rs, rs, AF.Exp, scale=-0.5)
                t0 = self.tmpf[0]
                self.tt(t0[:, sl], osb[:, sl], rs, ALU.mult)
                self.stt(mixT[:, mc, sl], t0[:, sl], nw, mixT[:, mc, sl], ALU.mult, ALU.mult)

    self.P.tag = "wout"
    wo = self.w_out[l].rearrange("(k p) n -> p k n", p=128)
    for dh in range(2):
        sv = _slotload(self, [(wo[:, :, dh * 512:(dh + 1) * 512], 512, "pool")])
        for dcc in range(4):
            dc = dh * 4 + dcc
            for th in range(2):
                ps = self.bank()
                for k in range(8):
                    self.mm(ps[:, :], sv[:, k, dcc * 128:(dcc + 1) * 128], mixT[:, k, th * 512:(th + 1) * 512], k == 0, k == 7)
                xs = self.xT[:, dc, th * 512:(th + 1) * 512]
                self.stt(xs, ps[:, :], self.gates[:, 1, dc:dc + 1], xs, ALU.mult, ALU.add)


KB.mixer_decl = _mixer_decl
KB.mixer_init = _mixer_init
KB.mixer = _mixer

def _consts(kind):
    bf = ml_dtypes.bfloat16
    Ls = 1024 if kind == "s" else 256
    c = {}
    t = np.arange(T)
    tl = t % Ls
    cont = 1.0 if kind == "s" else 0.0
    c["cont"] = np.full((128, 1), cont, np.float32)
    d = np.arange(64)
    fr = 1.0 / (10000.0 ** (np.arange(0, 32, 2, dtype=np.float64) / 32.0))
    fidx = d % 16
    pos = np.where(d[:, None] < 32, (t // 64)[None, :], (t % 64)[None, :]).astype(np.float64)
    ang = pos * fr[fidx][:, None]
    sgn = np.where((d % 32) < 16, -1.0, 1.0)[:, None]
    if kind == "s":
        C = np.cos(ang); S = sgn * np.sin(ang)
    else:
        C = np.ones((64, T)); S = np.zeros((64, T))
    c["ropeC"] = np.tile(C, (2, 1)).astype(np.float32)
    c["ropeS"] = np.tile(S, (2, 1)).astype(np.float32)
    c["smask"] = np.tile((t % 32 != 0).astype(np.float32)[None], (128, 1))
    s_ = np.arange(128)[:, None]; t_ = np.arange(128)[None, :]
    same = (s_ // 32) == (t_ // 32)
    cm = np.stack([(same & (s_ <= t_)), (same & (s_ >= t_))], axis=1).astype(np.float32)
    c["cm"] = np.ascontiguousarray(np.concatenate([cm, cm, cm, cm], axis=2)).astype(bf)
    cmr = np.stack([(s_ <= t_), (s_ >= t_)], axis=1).astype(np.float32)
    c["cmr"] = np.ascontiguousarray(np.concatenate([cmr, cmr, cmr, cmr], axis=2)).astype(bf)
    gv8 = np.ones((128, 8), np.float32); gv8[:, [2, 4, 6]] = cont
    c["gvec8"] = gv8
    c["chm"] = ((np.arange(128)[:, None] // 32) == np.arange(4)[None, :]).astype(np.float32)
    gv = np.ones((128, 32), np.float32); gv[:, [8, 16, 24]] = cont
    c["gvec"] = gv
    c["ident"] = np.eye(128, dtype=np.float32).astype(bf)
    bo = np.zeros((128, 128), np.float32); bo[:64, :64] = 1; bo[64:, 64:] = 1
    c["bones"] = bo.astype(bf)
    lg_all = np.log1p(-np.exp2(-5.0 - 0.5 * np.arange(8, dtype=np.float64)))
    retT = np.zeros((128, 2, 2, 3, 32)); retF = np.zeros((128, 2, 2, 32))
    j = np.arange(32)
    for p in range(128):
        for gp in range(2):
            hh = gp * 2 + p // 64
            for dr in range(2):
                lg = lg_all[2 * hh + dr]
                if dr == 0:
                    retT[p, gp, dr, 0] = np.exp(lg * (j + 1)); retT[p, gp, dr, 1] = np.exp(-lg * (j + 1)); retT[p, gp, dr, 2] = np.exp(lg * (31 - j))
                else:
                    retT[p, gp, dr, 0] = np.exp(lg * (32 - j)); retT[p, gp, dr, 1] = np.exp(-lg * (32 - j)); retT[p, gp, dr, 2] = np.exp(lg * j)
                retF[p, gp, dr, :] = np.exp(32 * lg)
    c["retT"] = retT.astype(np.float32); c["retF"] = retF.astype(np.float32)
    rfa = np.zeros((128, 2, 2, 3, 4)); rF8 = np.zeros((128, 2, 2, 8))
    a_ = np.arange(4)
    for p in range(128):
        for gp in range(2):
            hh = gp * 2 + p // 64
            for dr in range(2):
                lg = lg_all[2 * hh + dr]
                if dr == 0:
                    rfa[p, gp, dr, 0] = np.exp(lg * 32 * a_); rfa[p, gp, dr, 1] = np.exp(-lg * 32 * a_); rfa[p, gp, dr, 2] = np.exp(lg * 32 * (3 - a_))
                else:
                    rfa[p, gp, dr, 0] = np.exp(lg * 32 * (3 - a_)); rfa[p, gp, dr, 1] = np.exp(-lg * 32 * (3 - a_)); rfa[p, gp, dr, 2] = np.exp(lg * 32 * a_)
                rF8[p, gp, dr, :] = np.exp(128 * lg)
    c["rfa"] = rfa.astype(np.float32); c["rF8"] = rF8.astype(np.float32)
    tn = tl.astype(np.float64) / Ls
    bands = np.arange(1, 17, dtype=np.float64)
    a2 = 2.0 * np.pi * tn[:, None] * bands[None]
    z = np.concatenate([tn[:, None], np.cos(a2), np.sin(a2)], axis=-1)
    c["zT"] = np.ascontiguousarray(z.T).astype(np.float32)
    MIN_DECAY = math.log(1e-2) / 1.5; MAX_DECAY = math.log(1e-2) / 0.3
    deltas = np.abs(np.linspace(MIN_DECAY, MAX_DECAY, 512, dtype=np.float32)).astype(np.float32)
    c["deltab"] = np.tile(deltas[None], (128, 1)).astype(np.float32)
    tt_ = (np.arange(8)[None, :] * 128 + np.arange(128)[:, None])
    c["negtl"] = (-((tt_ % Ls).astype(np.float64) / Ls)).astype(np.float32)
    c["m0"] = ((tt_ % Ls) != 0).astype(np.float32)
    th = np.zeros((T, T)); blk = (t[:, None] // Ls) == (t[None, :] // Ls)
    th = np.pi * (2 * tl[None, :] + 1) * tl[:, None] / (2.0 * Ls)
    Cm = np.where(blk, np.cos(th), 0.0); Sm = np.where(blk, -np.sin(th), 0.0)
    c["Cf"] = Cm.astype(np.float32); c["Sf"] = Sm.astype(np.float32)
    c["CiT"] = np.ascontiguousarray((Cm / Ls).T).astype(np.float32)
    c["SiT"] = np.ascontiguousarray((Sm / Ls).T).astype(np.float32)
    return c


def _prep_mixer_common(inputs, L):
    f = lambda a: np.ascontiguousarray(np.asarray(a, dtype=np.float32))
    com = {}
    w_in = np.asarray(inputs["w_in"], dtype=np.float32)[:L]
    com["w_in"] = f(w_in)
    com["w_out"] = f(inputs["w_out"])[:L]
    d = np.arange(64)
    perm = np.where((d % 32) < 16, d + 16, d - 16)
    colperm = (np.arange(4)[:, None] * 64 + perm[None, :]).reshape(-1)
    com["w_inp"] = f(np.concatenate([w_in[:, :, 2816 + colperm], w_in[:, :, 3072 + colperm]], axis=-1))
    lb = np.asarray(inputs["hgrn_lb"], dtype=np.float32)
    com["hgrn_lbT"] = f(lb.reshape(2, 4, 2, 128).transpose(3, 0, 2, 1).reshape(128, 4, 4))
    hn = np.asarray(inputs["hgrn_norm_w"], dtype=np.float32)[:L]
    com["hnwT"] = f(np.tile(hn.T, (2, 1)))
    hc = np.asarray(inputs["hyena_conv"], dtype=np.float32)[:L]
    com["convT"] = f(hc.reshape(L, 3, 12, 128).transpose(3, 0, 1, 2))
    hb = np.asarray(inputs["hyena_bias"], dtype=np.float32)[:L]
    com["hbiasT"] = f(hb.reshape(L, 4, 128).transpose(2, 0, 1))
    com["hy_w1"] = f(inputs["hyena_w1"])[:L]; com["hy_w2"] = f(inputs["hyena_w2"])[:L]; com["hy_w3"] = f(inputs["hyena_w3"])[:L]
    fb = np.zeros((64, L, 5), np.float32)
    fb[:, :, 0] = np.asarray(inputs["hyena_b1"])[:L].T; fb[:, :, 1] = np.asarray(inputs["hyena_b2"])[:L].T
    fb[:, :, 2] = np.asarray(inputs["hyena_freq"])[:L].T
    com["hy_fb"] = fb
    return com


def _s0_for(inputs, kind, i, L):
    s0 = np.zeros((128, 4, L, 2, 64), np.float32)
    if kind == "s":
        sh = np.asarray(inputs["state_hgrn"], dtype=np.float32)[i][:L]
        sr = np.asarray(inputs["state_ret"], dtype=np.float32)[i][:L]
        for g in range(2):
            s0[:, g] = sh[:, :, 2 * g:2 * g + 2].transpose(2, 3, 0, 1, 4).reshape(128, L, 2, 64)
            s0[:, 2 + g] = sr[:, :, 2 * g:2 * g + 2].transpose(2, 3, 0, 1, 4).reshape(128, L, 2, 64)
    return s0


_KB_CACHE = {}


def run_all(inputs, L=4):
    if L not in _KB_CACHE:
        kb = KB(L, do_mixer=True)
        kb.build()
        _KB_CACHE[L] = kb
    kb = _KB_CACHE[L]
    com = _prep_common(inputs, L)
    com.update(_prep_mixer_common(inputs, L))
    cst = {"s": _consts("s"), "p": _consts("p")}
    maps = []
    for kind, i in core_assign():
        if kind == "s":
            x = np.asarray(inputs["x_sample"], dtype=np.float32)[i]
            cond = np.asarray(inputs["c"], dtype=np.float32)[i]
        else:
            x = np.asarray(inputs["x_prompt"], dtype=np.float32)[4 * i:4 * i + 4].reshape(1024, 1024)
            cond = np.asarray(inputs["c_ctx"], dtype=np.float32)
        m = dict(com)
        m.update(cst[kind])
        m["xT"] = np.ascontiguousarray(x.T)
        m["cond"] = np.ascontiguousarray(cond.reshape(8, 128).T)
        m["s0"] = _s0_for(inputs, kind, i, L)
        maps.append(m)
    res = run_bass_kernel_spmd(kb.nc, maps, core_ids=list(range(8)))
    R_ = res.results
    y_s = np.stack([np.ascontiguousarray(R_[i]["yT"].T) for i in range(2)], axis=0)
    y_p = np.concatenate([np.ascontiguousarray(R_[2 + g]["yT"].T).reshape(4, 256, 1024) for g in range(4)], axis=0)
    sth = np.concatenate([R_[2 + g]["sth"] for g in range(4)], axis=0)
    str_ = np.concatenate([R_[2 + g]["str"] for g in range(4)], axis=0)
    return (y_p.astype(np.float32), y_s.astype(np.float32), sth.astype(np.float32), str_.astype(np.float32))


def kernel(**inputs):
    return run_all(inputs, 4)
```
